# Optimizing a Trainium2 kernel written in Bass

```python
import math
import jax, jax.numpy as jnp
from jax import lax
import numpy as np

D_MODEL = 1024
BATCH = 4
SEQ = 4096
DEPTH = 4

N_A_LAYERS = DEPTH // 2
N_B_LAYERS = DEPTH - N_A_LAYERS
QB = 128
DA_HEAD_DIM = 64
DA_HEADS = D_MODEL // (2 * DA_HEAD_DIM)
DA_QK = DA_HEADS * 2 * DA_HEAD_DIM
DA_V = DA_HEADS * 2 * DA_HEAD_DIM
DA_IN = 2 * DA_QK + DA_V + DA_V
LAMBDA_INIT = tuple(0.8 - 0.6 * math.exp(-0.3 * l) for l in range(N_A_LAYERS))
SB_HEAD_DIM = 64
SB_HEADS = D_MODEL // SB_HEAD_DIM
SB_W = SB_HEADS * SB_HEAD_DIM
SB_IN = SB_W + SB_W
ROPE_THETA = 500000.0
ROT_DIM = DA_HEAD_DIM // 4
RMS_EPS = 1e-6

kernel_name = "yoco_diffattn_stickbreaking_hybrid"


def _rms(x, g):
    xf = x.astype(jnp.float32)
    y = xf * lax.rsqrt(jnp.mean(xf * xf, axis=-1, keepdims=True) + RMS_EPS)
    return (y * g.astype(jnp.float32)).astype(x.dtype)


def _rope_partial(t, pos):
    half = ROT_DIM // 2
    inv = jnp.power(jnp.float32(ROPE_THETA), -jnp.arange(half, dtype=jnp.float32) / half)
    ang = pos.astype(jnp.float32)[:, None] * inv[None, :]
    cos = jnp.cos(ang)[None, :, None, :]
    sin = jnp.sin(ang)[None, :, None, :]
    t1 = t[..., :half].astype(jnp.float32)
    t2 = t[..., half:ROT_DIM].astype(jnp.float32)
    rot = jnp.concatenate([t1 * cos - t2 * sin, t1 * sin + t2 * cos], axis=-1)
    return jnp.concatenate([rot.astype(t.dtype), t[..., ROT_DIM:]], axis=-1)


def _to_blocks(t):
    b, s, h, d = t.shape
    return t.reshape(b, s // QB, QB, h, d).transpose(1, 0, 3, 2, 4)


def _from_blocks(o):
    nb, b, h, qb, e = o.shape
    return o.transpose(1, 0, 3, 2, 4).reshape(b, nb * qb, h, e)


def _diff_attention(q1, q2, k1, k2, v, lam):
    s_len = q1.shape[1]
    kpos = jnp.arange(s_len)
    scale = DA_HEAD_DIM ** -0.5

    def block(args):
        i, qb1, qb2 = args
        qpos = i * QB + jnp.arange(QB)
        mask = kpos[None, :] <= qpos[:, None]

        def probs(qb, k):
            s = jnp.einsum('bhqd,bshd->bhqs', qb, k).astype(jnp.float32) * scale
            return jax.nn.softmax(jnp.where(mask, s, -jnp.inf), axis=-1)

        a = probs(qb1, k1) - lam * probs(qb2, k2)
        return jnp.einsum('bhqs,bshe->bhqe', a.astype(v.dtype), v)

    nb = s_len // QB
    o = lax.map(block, (jnp.arange(nb), _to_blocks(q1), _to_blocks(q2)))
    return _from_blocks(o)


def _stick_breaking(q, k, v):
    s_len = q.shape[1]
    kpos = jnp.arange(s_len)
    scale = SB_HEAD_DIM ** -0.5

    def block(args):
        i, qb = args
        qpos = i * QB + jnp.arange(QB)
        mask = kpos[None, :] < qpos[:, None]
        z = jnp.einsum('bhqd,bshd->bhqs', qb, k).astype(jnp.float32) * scale
        log_beta = jax.nn.log_sigmoid(z)
        log_1mb = jnp.where(mask, jax.nn.log_sigmoid(-z), 0.0)
        suffix = lax.cumsum(log_1mb, axis=3, reverse=True) - log_1mb
        a = jnp.where(mask, jnp.exp(log_beta + suffix), 0.0)
        return jnp.einsum('bhqs,bshe->bhqe', a.astype(v.dtype), v)

    nb = s_len // QB
    o = lax.map(block, (jnp.arange(nb), _to_blocks(q)))
    return _from_blocks(o)


def _diff_layer(x, pos, norm_g, w_in, w_out, q_norm, k_norm, lq1, lk1, lq2, lk2,
                subln, lambda_init):
    b, s, _ = x.shape
    h = _rms(x, norm_g)
    proj = h @ w_in
    q, k, v, g = jnp.split(proj, [DA_QK, 2 * DA_QK, 2 * DA_QK + DA_V], axis=-1)
    q = q.reshape(b, s, DA_HEADS, 2, DA_HEAD_DIM)
    k = k.reshape(b, s, DA_HEADS, 2, DA_HEAD_DIM)
    v = v.reshape(b, s, DA_HEADS, 2 * DA_HEAD_DIM)
    qs = [_rope_partial(_rms(q[:, :, :, c], q_norm), pos) for c in range(2)]
    ks = [_rope_partial(_rms(k[:, :, :, c], k_norm), pos) for c in range(2)]
    lam = (jnp.exp(jnp.sum(lq1.astype(jnp.float32) * lk1.astype(jnp.float32)))
           - jnp.exp(jnp.sum(lq2.astype(jnp.float32) * lk2.astype(jnp.float32)))
           + lambda_init)
    o = _diff_attention(qs[0], qs[1], ks[0], ks[1], v, lam)
    o = _rms(o, subln) * (1.0 - lambda_init)
    o = o.reshape(b, s, DA_V) * jax.nn.silu(g)
    return x + o @ w_out


def _shared_kv(x, kv_norm, w_kv):
    b, s, _ = x.shape
    kv = _rms(x, kv_norm) @ w_kv
    k, v = jnp.split(kv, [SB_W], axis=-1)
    return (k.reshape(b, s, SB_HEADS, SB_HEAD_DIM), v.reshape(b, s, SB_HEADS, SB_HEAD_DIM))


def _sb_layer(x, k, v, norm_g, w_in, w_out):
    b, s, _ = x.shape
    proj = _rms(x, norm_g) @ w_in
    q, g = jnp.split(proj, [SB_W], axis=-1)
    o = _stick_breaking(q.reshape(b, s, SB_HEADS, SB_HEAD_DIM), k, v)
    o = o.reshape(b, s, SB_W) * jax.nn.silu(g)
    return x + o @ w_out


def setup_inputs(seed: int = 0) -> dict:
    key = jax.random.key(seed)
    ks = jax.random.split(key, 20)
    f32 = jnp.float32
    nrm = lambda k, shape, scale: jax.random.normal(k, shape, f32) * scale
    gain = lambda k, shape: 1.0 + 0.02 * jax.random.normal(k, shape, f32)
    return {
        "x": jax.random.normal(ks[0], (BATCH, SEQ, D_MODEL), f32),
        "a_norm": gain(ks[1], (N_A_LAYERS, D_MODEL)),
        "a_w_in": nrm(ks[2], (N_A_LAYERS, D_MODEL, DA_IN), D_MODEL ** -0.5),
        "a_w_out": nrm(ks[3], (N_A_LAYERS, DA_V, D_MODEL), DA_V ** -0.5),
        "a_q_norm": gain(ks[4], (N_A_LAYERS, DA_HEAD_DIM)),
        "a_k_norm": gain(ks[5], (N_A_LAYERS, DA_HEAD_DIM)),
        "a_lq1": nrm(ks[6], (N_A_LAYERS, DA_HEAD_DIM), 0.1),
        "a_lk1": nrm(ks[7], (N_A_LAYERS, DA_HEAD_DIM), 0.1),
        "a_lq2": nrm(ks[8], (N_A_LAYERS, DA_HEAD_DIM), 0.1),
        "a_lk2": nrm(ks[9], (N_A_LAYERS, DA_HEAD_DIM), 0.1),
        "a_subln": gain(ks[10], (N_A_LAYERS, 2 * DA_HEAD_DIM)),
        "kv_norm": gain(ks[11], (D_MODEL,)),
        "w_kv": nrm(ks[12], (D_MODEL, 2 * SB_W), D_MODEL ** -0.5),
        "b_norm": gain(ks[13], (N_B_LAYERS, D_MODEL)),
        "b_w_in": nrm(ks[14], (N_B_LAYERS, D_MODEL, SB_IN), D_MODEL ** -0.5),
        "b_w_out": nrm(ks[15], (N_B_LAYERS, SB_W, D_MODEL), SB_W ** -0.5),
    }


def reference(x, a_norm, a_w_in, a_w_out, a_q_norm, a_k_norm, a_lq1, a_lk1, a_lq2,
              a_lk2, a_subln, kv_norm, w_kv, b_norm, b_w_in, b_w_out):
    pos = jnp.arange(x.shape[1])
    shared_k = None
    shared_v = None
    for layer in range(DEPTH):
        if layer < N_A_LAYERS:
            x = _diff_layer(x, pos, a_norm[layer], a_w_in[layer], a_w_out[layer],
                            a_q_norm[layer], a_k_norm[layer], a_lq1[layer], a_lk1[layer],
                            a_lq2[layer], a_lk2[layer], a_subln[layer], LAMBDA_INIT[layer])
        else:
            if layer == N_A_LAYERS:
                shared_k, shared_v = _shared_kv(x, kv_norm, w_kv)
            j = layer - N_A_LAYERS
            x = _sb_layer(x, shared_k, shared_v, b_norm[j], b_w_in[j], b_w_out[j])
    return x
```

```python
from contextlib import ExitStack
import math
import numpy as np
import ml_dtypes
import concourse.bass as bass
import concourse.mybir as mybir
from concourse.bass_utils import run_bass_kernel_spmd

F32 = mybir.dt.float32
BF16 = mybir.dt.bfloat16
AF = mybir.ActivationFunctionType
ALU = mybir.AluOpType

D = 1024
SEQ = 4096
NB = 4
CHK = 512
NLC = 4
LT = NLC * CHK
GCH = [[0, 3, 4, 7], [1, 2, 5, 6]]
OWNER = {}
for _p in range(2):
    for _j, _g in enumerate(GCH[_p]):
        OWNER[_g] = (_p, _j)
MJ = [max(GCH[0][j], GCH[1][j]) for j in range(NLC)]
EPS = 1e-6
LAMBDA_INIT = [0.8 - 0.6 * math.exp(-0.3 * l) for l in range(2)]
RG = [[0, 1], [2, 3], [4, 5], [6, 7]]
SEMCH = 2048

PC_AN, PC_KVN, PC_BN, PC_QN, PC_KN, PC_SUB, PC_L = 0, 16, 24, 40, 42, 44, 46
NPAR = PC_L + 2 * 4 * 64


class Buf:
    __slots__ = ("name", "w", "r")

    def __init__(self, name):
        self.name = name
        self.w = None
        self.r = {}


class Sched:
    CE = ("pe", "act", "dve", "pool")
    QE = ("pe", "act", "dve", "pool", "sp")

    def __init__(self, nc):
        self.nc = nc
        self.ops = {e: [] for e in self.QE}
        self.cnt = {e: 0 for e in self.CE}
        self.known = {e: {} for e in self.QE}
        self.dsem_cnt = []
        self.esems = None
        self.dsems = None

    def _wait(self, eng, ev):
        if ev is None:
            return
        kind, key, val = ev
        if kind == "e" and key == eng and eng == "pe":
            return
        k = (kind, key)
        if self.known[eng].get(k, 0) >= val:
            return
        self.known[eng][k] = val
        self.ops[eng].append(("wait", ev))

    def new_dsem(self):
        self.dsem_cnt.append(0)
        return len(self.dsem_cnt) - 1

    def op(self, eng, fn, reads=(), writes=()):
        for b in reads:
            self._wait(eng, b.w)
        for b in writes:
            self._wait(eng, b.w)
            for ev in b.r.values():
                if ev[0] == "e" and ev[1] == eng and ev[2] == self.cnt[eng] and False:
                    continue
                self._wait(eng, ev)
        self.cnt[eng] += 1
        idx = self.cnt[eng]
        me = ("e", eng, idx)
        self.ops[eng].append(("op", fn, idx))
        for b in reads:
            b.r[("e", eng)] = me
        for b in writes:
            b.w = me
            b.r = {}
        return me

    def dma(self, q, fn, dsem, reads=(), writes=(), inc=16):
        for b in reads:
            self._wait(q, b.w)
        for b in writes:
            self._wait(q, b.w)
            for ev in b.r.values():
                self._wait(q, ev)
        self.dsem_cnt[dsem] += inc
        me = ("d", dsem, self.dsem_cnt[dsem])
        self.ops[q].append(("dma", fn, dsem, inc))
        for b in reads:
            b.r[("d", dsem)] = me
        for b in writes:
            b.w = me
            b.r = {}
        return me

    def wait_event(self, eng, ev):
        self._wait(eng, ev)

    def _sem_of(self, ev):
        kind, key, val = ev
        if kind == "e":
            return self.esems[key][(val - 1) // SEMCH], (val - 1) % SEMCH + 1
        return self.dsems[key], val

    def emit(self, stack):
        nc = self.nc
        self.esems = {}
        for e in self.CE:
            n = max(1, (self.cnt[e] + SEMCH - 1) // SEMCH)
            self.esems[e] = [stack.enter_context(nc.semaphore(f"s_{e}{i}")) for i in range(n)]
        self.dsems = [stack.enter_context(nc.semaphore(f"d{i}")) for i in range(max(1, len(self.dsem_cnt)))]
        block = stack.enter_context(nc.Block())

        def run(engname):
            def body(eng):
                for o in self.ops[engname]:
                    if o[0] == "wait":
                        s, v = self._sem_of(o[1])
                        eng.wait_ge(s, v)
                    elif o[0] == "op":
                        _, fn, idx = o
                        fn(eng).then_inc(self.esems[engname][(idx - 1) // SEMCH], 1)
                    else:
                        _, fn, dsem, inc = o
                        fn(eng).then_inc(self.dsems[dsem], inc)
            return body

        block.tensor(run("pe"))
        block.scalar(run("act"))
        block.vector(run("dve"))
        block.gpsimd(run("pool"))
        block.sync(run("sp"))


class Ring:
    def __init__(self, S, st, nc, name, n, shape, dtype, with_dsem=True):
        self.tiles = [st.enter_context(nc.sbuf_tensor(f"{name}{i}", shape, dtype)) for i in range(n)]
        self.bufs = [Buf(f"{name}{i}") for i in range(n)]
        self.dsems = [S.new_dsem() for _ in range(n)] if with_dsem else [None] * n
        self.i = 0
        self.n = n

    def next(self):
        k = self.i % self.n
        self.i += 1
        return self.tiles[k], self.bufs[k], self.dsems[k]


def build_nc(n_layers=4):
    nc = bass.Bass("TRN2", target_bir_lowering=False)
    dt_in = lambda n, s, d=F32: nc.dram_tensor(n, s, d, kind="ExternalInput").ap()
    x_d = dt_in("x", [LT, D])
    a_w_in = dt_in("a_w_in", [2, D, 4096])
    a_w_out = dt_in("a_w_out", [2, D, D])
    w_kv = dt_in("w_kv", [D, 2048])
    b_w_in = dt_in("b_w_in", [2, D, 2048])
    b_w_out = dt_in("b_w_out", [2, D, D])
    params_d = dt_in("params", [128, NPAR])
    cmat_d = dt_in("cmat", [128, 4, 128])
    rope_d = dt_in("rope", [2, 128, LT])
    bands_d = dt_in("bands", [2, 128, 2, 2, 896], BF16)
    y_d = nc.dram_tensor("y", [LT, D], F32, kind="ExternalOutput").ap()
    kvin = [[nc.dram_tensor(f"kvin_{l}_{h}", [512, LT], BF16) for h in range(4)] for l in range(3)]
    kvall = [[nc.dram_tensor(f"kvall_{l}_{h}", [1024, LT], BF16) for h in range(4)] for l in range(3)]

    S = Sched(nc)
    with ExitStack() as st:
        sb = lambda n, s, d: st.enter_context(nc.sbuf_tensor(n, s, d))
        xT = sb("xT", [128, 8, LT], F32)
        BxT = [Buf(f"xT{j}") for j in range(NLC)]
        params = sb("params_sb", [128, NPAR], F32)
        Bpar = Buf("params")
        cmat = sb("cmat_sb", [128, 4, 128], F32)
        Bcm = Buf("cmat")
        cbf = sb("cbf", [128, 9, 128], BF16)
        Bcbf = Buf("cbf")
        bands = sb("bands_sb", [128, 2, 2, 896], BF16)
        Bbands = Buf("bands")
        sc = sb("scal", [128, 16], F32)
        Bsc = Buf("scal")
        qT = sb("qT", [128, 8, CHK], BF16)
        BqT = [Buf(f"qT{u}") for u in range(8)]
        gT = sb("gT", [128, 8, CHK], BF16)
        BgT = [Buf(f"gT{u}") for u in range(8)]
        hring = Ring(S, st, nc, "hT", 2, [128, 8, CHK], BF16, with_dsem=False)
        wring = Ring(S, st, nc, "wt", 2, [128, 8, 512], BF16)
        kring = Ring(S, st, nc, "kt", 4, [128, 2, CHK], BF16)
        rring = Ring(S, st, nc, "rp", 2, [128, 2, CHK], F32)
        big32 = Ring(S, st, nc, "big32", 4, [128, 2 * CHK], F32)
        xsring = big32
        f32r = Ring(S, st, nc, "f32t", 4, [128, CHK], F32, with_dsem=False)
        bf16r = Ring(S, st, nc, "bft", 5, [128, CHK], BF16, with_dsem=False)
        ering = Ring(S, st, nc, "et", 4, [128, 2 * CHK], BF16, with_dsem=False)
        ebring = Ring(S, st, nc, "ebt", 5, [128, 2 * CHK], BF16, with_dsem=False)
        Rc = sb("Rc", [128, 2 * CHK], F32)
        BRc = Buf("Rc")
        spring = Ring(S, st, nc, "spt", 3, [128, 2 * CHK], BF16, with_dsem=False)
        wring2 = Ring(S, st, nc, "wbt", 2, [128, 2 * CHK], BF16, with_dsem=False)
        stK = Ring(S, st, nc, "stK", 2, [128, CHK], BF16)
        stV = Ring(S, st, nc, "stV", 2, [128, CHK], BF16)
        stV_d2 = {d: S.new_dsem() for d in stV.dsems}
        ps = [st.enter_context(nc.psum_tensor(f"ps{i}", [128, 2 * CHK], F32)) for i in range(4)]
        Bps = [Buf(f"bank{i}") for i in range(8)]

        def bank(i):
            return ps[i // 2][:, (i % 2) * CHK:(i % 2 + 1) * CHK]

        d_misc = S.new_dsem()
        d_bands = S.new_dsem()
        d_out = S.new_dsem()
        d_ag = [[S.new_dsem() for _ in range(4)] for _ in range(3)]
        Bkvall = [[Buf(f"kvall{l}{h}") for h in range(4)] for l in range(3)]
        kv_store_events = [[[] for _ in range(4)] for _ in range(3)]

        S.dma("sp", lambda e: e.dma_start(out=params[:], in_=params_d), d_misc, writes=[Bpar])
        S.dma("sp", lambda e: e.dma_start(out=cmat[:], in_=cmat_d), d_out, writes=[Bcm])
        S.dma("sp", lambda e: e.dma_start(out=bands[:], in_=bands_d[0]), d_bands, writes=[Bbands])
        S.op("dve", lambda e: e.memset(cbf[:, 0, :], 1.0 / 1024.0), writes=[Bcbf])
        S.op("dve", lambda e: e.memset(cbf[:, 1, :], 0.0), writes=[Bcbf])
        S.op("dve", lambda e: e.memset(cbf[0:64, 1, 0:64], 1.0 / 64.0), writes=[Bcbf])
        S.op("dve", lambda e: e.memset(cbf[64:128, 1, 64:128], 1.0 / 64.0), writes=[Bcbf])
        S.op("dve", lambda e: e.memset(cbf[:, 2, :], 1.0 / 128.0), writes=[Bcbf])
        S.op("dve", lambda e: e.memset(cbf[:, 3, :], 1.0), writes=[Bcbf])
        S.op("dve", lambda e: e.memset(cbf[:, 7, :], -1.0), writes=[Bcbf])
        S.op("dve", lambda e: e.tensor_copy(out=cbf[:, 4:7, :], in_=cmat[:, 1:4, :]), reads=[Bcm], writes=[Bcbf])
        S.op("dve", lambda e: e.tensor_copy(out=cbf[:, 8, :], in_=cmat[:, 0, :]), reads=[Bcm], writes=[Bcbf])
        ident = cmat[:, 0, :]
        C_MEAN, C_BLK, C_M128, C_ONES, C_ROT, C_TINC, C_TLOW, C_NONES, C_IDB = [cbf[:, i, :] for i in range(9)]
        for l in range(2):
            for t in range(2):
                c0 = PC_L + (l * 4 + 2 * t) * 64
                tmp, Btmp, _ = f32r.next()
                S.op("dve", lambda e, tmp=tmp, c0=c0: e.tensor_tensor(out=tmp[:, 0:64], in0=params[:, c0:c0 + 64],
                                                                     in1=params[:, c0 + 64:c0 + 128], op=ALU.mult),
                     reads=[Bpar], writes=[Btmp])
                S.op("dve", lambda e, tmp=tmp, l=l, t=t: e.reduce_sum(out=sc[:, 8 + 2 * l + t:9 + 2 * l + t], in_=tmp[:, 0:64],
                                                                     axis=mybir.AxisListType.X),
                     reads=[Btmp], writes=[Bsc])
            S.op("act", lambda e, l=l: e.activation(out=sc[:, 8 + 2 * l:10 + 2 * l], in_=sc[:, 8 + 2 * l:10 + 2 * l], func=AF.Exp),
                 reads=[Bsc], writes=[Bsc])
            S.op("dve", lambda e, l=l: e.scalar_tensor_tensor(out=sc[:, l:l + 1], in0=sc[:, 9 + 2 * l:10 + 2 * l],
                                                              scalar=-LAMBDA_INIT[l], in1=sc[:, 8 + 2 * l:9 + 2 * l],
                                                              op0=ALU.add, op1=ALU.subtract),
                 reads=[Bsc], writes=[Bsc])
            S.op("dve", lambda e, l=l: e.tensor_scalar(out=sc[:, 2 + l:3 + l], in0=params[:, PC_SUB + l:PC_SUB + l + 1],
                                                       scalar1=1.0 - LAMBDA_INIT[l], scalar2=None, op0=ALU.mult),
                 reads=[Bpar, Bsc], writes=[Bsc])
            S.op("dve", lambda e, l=l: e.tensor_scalar(out=sc[:, 4 + l:5 + l], in0=params[:, PC_QN + l:PC_QN + l + 1],
                                                       scalar1=0.125, scalar2=None, op0=ALU.mult),
                 reads=[Bpar, Bsc], writes=[Bsc])

        for tb in range(LT // 128):
            xs, Bxs, dxs = xsring.next()
            S.dma("sp", lambda e, xs=xs, tb=tb: e.dma_start(out=xs[:], in_=x_d[tb * 128:(tb + 1) * 128, :]), dxs, writes=[Bxs])
            for half in range(2):
                bk = (tb * 2 + half) % 4
                for q in range(4):
                    kc = half * 4 + q
                    S.op("pe", lambda e, bk=bk, q=q, xs=xs, kc=kc: e.transpose(out=bank(bk)[:, q * 128:(q + 1) * 128],
                                                                            in_=xs[:, kc * 128:(kc + 1) * 128], identity=ident),
                         reads=[Bxs, Bcm], writes=[Bps[bk]])
                eng = "dve" if half == 0 else "act"
                dst = xT[:, half * 4:half * 4 + 4, tb * 128:(tb + 1) * 128]
                src = bank(bk).rearrange("p (a b) -> p a b", a=4)
                if eng == "dve":
                    S.op("dve", lambda e, dst=dst, src=src: e.tensor_copy(out=dst, in_=src), reads=[Bps[bk]], writes=[BxT[tb // 4]])
                else:
                    S.op("act", lambda e, dst=dst, src=src: e.activation(out=dst, in_=src, func=AF.Copy), reads=[Bps[bk]], writes=[BxT[tb // 4]])

        def load_w(wmat, col0):
            wt, Bw, dw = wring.next()
            src = wmat[:, col0:col0 + 512].rearrange("(kc p) c -> p kc c", p=128)
            S.dma("pool", lambda e: e.dma_start(out=wt[:], in_=src), dw, writes=[Bw])
            return wt, Bw

        def rms_chunk(j, gcol, dst=None):
            if dst is None:
                hT, Bh1, _ = hring.next()
                Bh = [Bh1]
            else:
                hT, Bh = dst
            cs = slice(j * CHK, (j + 1) * CHK)
            sbk = 7
            for kc in range(8):
                sq, Bsq, _ = bf16r.next()
                if kc % 4 != 3:
                    S.op("act", lambda e, sq=sq, kc=kc: e.activation(out=sq[:], in_=xT[:, kc, cs], func=AF.Square), reads=[BxT[j]], writes=[Bsq])
                else:
                    S.op("pool", lambda e, sq=sq, kc=kc: e.tensor_tensor(out=sq[:], in0=xT[:, kc, cs], in1=xT[:, kc, cs], op=ALU.mult),
                         reads=[BxT[j]], writes=[Bsq])
                S.op("pe", lambda e, sq=sq, kc=kc: e.matmul(bank(sbk), lhsT=C_MEAN, rhs=sq[:], start=(kc == 0), stop=(kc == 7)),
                     reads=[Bsq, Bcbf], writes=[Bps[sbk]])
            rstd, Brs, _ = f32r.next()
            S.op("act", lambda e: e.activation(out=rstd[:], in_=bank(sbk), func=AF.Ln, bias=EPS), reads=[Bps[sbk]], writes=[Brs])
            S.op("act", lambda e: e.activation(out=rstd[:], in_=rstd[:], func=AF.Exp, scale=-0.5), reads=[Brs], writes=[Brs])
            for kc in range(8):
                S.op("dve", lambda e, kc=kc: e.scalar_tensor_tensor(out=hT[:, kc, :], in0=xT[:, kc, cs], scalar=params[:, gcol + kc:gcol + kc + 1],
                                                                 in1=rstd[:], op0=ALU.mult, op1=ALU.mult),
                     reads=[BxT[j], Brs, Bpar], writes=Bh)
            return hT, Bh

        def proj_fm(hT, Bh, wt, Bw, cw, bk):
            for kc in range(8):
                S.op("pe", lambda e, kc=kc: e.matmul(bank(bk), lhsT=wt[:, kc, cw:cw + 128], rhs=hT[:, kc, :], start=(kc == 0), stop=(kc == 7)),
                     reads=[Bw] + Bh, writes=[Bps[bk]])

        def load_rope(j):
            rp, Brp, drp = rring.next()
            S.dma("sp", lambda e: e.dma_start(out=rp[:], in_=rope_d[:, :, j * CHK:(j + 1) * CHK].rearrange("t p n -> p t n")), drp, writes=[Brp])
            return rp, Brp

        def qknorm_rope(bk, gain_ap, Bgain, rp, Brp, dst, Bdst, tb):
            sq, Bsq, _ = bf16r.next()
            S.op("act", lambda e: e.activation(out=sq[:], in_=bank(bk), func=AF.Square), reads=[Bps[bk]], writes=[Bsq])
            S.op("pe", lambda e: e.matmul(bank(tb), lhsT=C_BLK, rhs=sq[:], start=True, stop=True), reads=[Bsq, Bcbf], writes=[Bps[tb]])
            rstd, Brs, _ = f32r.next()
            S.op("act", lambda e: e.activation(out=rstd[:], in_=bank(tb), func=AF.Ln, bias=EPS), reads=[Bps[tb]], writes=[Brs])
            S.op("act", lambda e: e.activation(out=rstd[:], in_=rstd[:], func=AF.Exp, scale=-0.5), reads=[Brs], writes=[Brs])
            tn, Btn, _ = bf16r.next()
            S.op("dve", lambda e: e.scalar_tensor_tensor(out=tn[:], in0=bank(bk), scalar=gain_ap, in1=rstd[:], op0=ALU.mult, op1=ALU.mult),
                 reads=[Bps[bk], Brs, Bgain], writes=[Btn])
            S.op("pe", lambda e: e.matmul(bank(tb), lhsT=C_ROT, rhs=tn[:], start=True, stop=True), reads=[Btn, Bcbf], writes=[Bps[tb]])
            u1, Bu1, _ = f32r.next()
            S.op("pool", lambda e: e.tensor_tensor(out=u1[:], in0=tn[:], in1=rp[:, 0, :], op=ALU.mult), reads=[Btn, Brp], writes=[Bu1])
            u2, Bu2, _ = f32r.next()
            S.op("dve", lambda e: e.tensor_tensor(out=u2[:], in0=bank(tb), in1=rp[:, 1, :], op=ALU.mult), reads=[Bps[tb], Brp], writes=[Bu2])
            S.op("pool", lambda e: e.tensor_tensor(out=dst, in0=u1[:], in1=u2[:], op=ALU.add), reads=[Bu1, Bu2], writes=[Bdst])

        def qk_pipeline(n_units, make_proj, gain_ap, Bgain, rp_fn, dst_fn, hook=None):
            stq = {}

            def P(k):
                make_proj(k, k % 3)

            def N1(k):
                bk, mb = k % 3, 3 + k % 2
                sq, Bsq, _ = bf16r.next()
                S.op("act", lambda e: e.activation(out=sq[:], in_=bank(bk), func=AF.Square), reads=[Bps[bk]], writes=[Bsq])
                S.op("pe", lambda e: e.matmul(bank(mb), lhsT=C_BLK, rhs=sq[:], start=True, stop=True), reads=[Bsq, Bcbf], writes=[Bps[mb]])

            def N2(k):
                bk, mb = k % 3, 3 + k % 2
                rstd, Brs, _ = f32r.next()
                S.op("act", lambda e: e.activation(out=rstd[:], in_=bank(mb), func=AF.Ln, bias=EPS), reads=[Bps[mb]], writes=[Brs])
                S.op("act", lambda e: e.activation(out=rstd[:], in_=rstd[:], func=AF.Exp, scale=-0.5), reads=[Brs], writes=[Brs])
                tn, Btn, _ = bf16r.next()
                S.op("dve", lambda e: e.scalar_tensor_tensor(out=tn[:], in0=bank(bk), scalar=gain_ap, in1=rstd[:], op0=ALU.mult, op1=ALU.mult),
                     reads=[Bps[bk], Brs, Bgain], writes=[Btn])
                stq[k] = (tn, Btn)

            def N3(k):
                rb = 5 + k % 2
                tn, Btn = stq.pop(k)
                rp, Brp = rp_fn(k)
                S.op("pe", lambda e: e.matmul(bank(rb), lhsT=C_ROT, rhs=tn[:], start=True, stop=True), reads=[Btn, Bcbf], writes=[Bps[rb]])
                u1, Bu1, _ = f32r.next()
                S.op("dve", lambda e: e.tensor_tensor(out=u1[:], in0=tn[:], in1=rp[:, 0, :], op=ALU.mult), reads=[Btn, Brp], writes=[Bu1])
                u2, Bu2, _ = f32r.next()
                S.op("dve", lambda e: e.tensor_tensor(out=u2[:], in0=bank(rb), in1=rp[:, 1, :], op=ALU.mult), reads=[Bps[rb], Brp], writes=[Bu2])
                dst, Bdst, post = dst_fn(k)
                S.op("pool", lambda e: e.tensor_tensor(out=dst, in0=u1[:], in1=u2[:], op=ALU.add), reads=[Bu1, Bu2], writes=[Bdst])
                if post is not None:
                    post()

            for t in range(n_units + 3):
                if hook is not None:
                    hook(t)
                if t < n_units:
                    P(t)
                if 1 <= t <= n_units:
                    N1(t - 1)
                if 2 <= t <= n_units + 1:
                    N2(t - 2)
                if t >= 3:
                    N3(t - 3)

        def phase_kv(L, wmat, kcol, vcol, gcol, qk_layer):
            htiles = [(hring.tiles[0], [hring.bufs[0]]), (hring.tiles[1], [hring.bufs[1]]), (qT, list(BqT)), (gT, list(BgT))]
            for j in range(NLC):
                rms_chunk(j, gcol, dst=htiles[j])
            wK = load_w(wmat, kcol)
            wV = load_w(wmat, vcol)
            for half in range(2):
                wt, Bw = wK
                if qk_layer is not None:
                    ropes = {}

                    def mk(k, bk, wt=wt, Bw=Bw, ropes=ropes):
                        jj, uu = k // 4, k % 4
                        if uu == 0:
                            ropes[jj] = load_rope(jj)
                        proj_fm(htiles[jj][0], htiles[jj][1], wt, Bw, uu * 128, bk)

                    def dstf(k, half=half):
                        jj, uu = k // 4, k % 4
                        u = half * 4 + uu
                        kst, Bkst, dk = stK.next()

                        def post():
                            dst = kvin[L][u // 2].ap()[(u % 2) * 128:(u % 2) * 128 + 128, jj * CHK:(jj + 1) * CHK]
                            ev = S.dma("sp", lambda e: e.dma_start(out=dst, in_=kst[:]), dk, reads=[Bkst])
                            kv_store_events[L][u // 2].append(ev)
                        return kst[:], Bkst, post

                    qk_pipeline(16, mk, params[:, PC_KN + qk_layer:PC_KN + qk_layer + 1], Bpar, lambda k, ropes=ropes: ropes[k // 4], dstf)
                else:
                    for jj in range(NLC):
                        for uu in range(4):
                            u = half * 4 + uu
                            bk = uu % 2
                            proj_fm(htiles[jj][0], htiles[jj][1], wt, Bw, uu * 128, bk)
                            kst, Bkst, dk = stK.next()
                            S.op("act", lambda e, kst=kst, bk=bk: e.activation(out=kst[:], in_=bank(bk), func=AF.Copy), reads=[Bps[bk]], writes=[Bkst])
                            dst = kvin[L][u // 2].ap()[(u % 2) * 128:(u % 2) * 128 + 128, jj * CHK:(jj + 1) * CHK]
                            ev = S.dma("sp", lambda e, dst=dst, kst=kst: e.dma_start(out=dst, in_=kst[:]), dk, reads=[Bkst])
                            kv_store_events[L][u // 2].append(ev)
                if half == 0:
                    wK = load_w(wmat, kcol + 512)
                wt, Bw = wV
                for jj in range(NLC):
                    hT, Bh = htiles[jj]
                    for tb in range(4):
                        bk = 4 + tb % 2
                        for kc in range(8):
                            S.op("pe", lambda e, kc=kc, tb=tb, bk=bk, wt=wt, hT=hT: e.matmul(bank(bk), lhsT=hT[:, kc, tb * 128:(tb + 1) * 128], rhs=wt[:, kc, :],
                                                                                   start=(kc == 0), stop=(kc == 7)),
                                 reads=[Bw] + Bh, writes=[Bps[bk]])
                        vst, Bvst, dv = stV.next()
                        S.op("dve", lambda e, vst=vst, bk=bk: e.tensor_copy(out=vst[:], in_=bank(bk)), reads=[Bps[bk]], writes=[Bvst])
                        for pr in range(2):
                            hp = half * 2 + pr
                            dst = kvin[L][hp].ap()[256:512, :].rearrange("(h r) (t e) -> (r t) h e", h=2, e=128)[
                                jj * CHK + tb * 128:jj * CHK + (tb + 1) * 128, :, :]
                            src = vst[:, pr * 256:(pr + 1) * 256].rearrange("p (h e) -> p h e", h=2)
                            ev = S.dma("sp", lambda e, dst=dst, src=src: e.dma_start(out=dst, in_=src), dv if pr == 0 else stV_d2[dv], reads=[Bvst])
                            kv_store_events[L][hp].append(ev)
                if half == 0:
                    wV = load_w(wmat, vcol + 512)
                for hp in (2 * half, 2 * half + 1):
                    mx = {}
                    for ev in kv_store_events[L][hp]:
                        mx[ev[1]] = max(mx.get(ev[1], 0), ev[2])
                    for k_, v_ in mx.items():
                        S.wait_event("pool", ("d", k_, v_))
                    S.dma("pool", lambda e, hp=hp: e.collective_compute("AllGather", ALU.bypass, replica_groups=RG,
                                                                         ins=[kvin[L][hp].ap().opt()], outs=[kvall[L][hp].ap().opt()]),
                          d_ag[L][hp], writes=[Bkvall[L][hp]], inc=1)

        def load_kv(L, u, gk):
            kt, Bkt, dkt = kring.next()
            rho, jj = OWNER[gk]
            base = kvall[L][u // 2].ap()
            ksrc = base[rho * 512 + (u % 2) * 128:rho * 512 + (u % 2) * 128 + 128, jj * CHK:(jj + 1) * CHK]
            vsrc = base[rho * 512 + 256 + (u % 2) * 128:rho * 512 + 256 + (u % 2) * 128 + 128, :].rearrange(
                "r (t e) -> (r t) e", e=128)[jj * CHK:(jj + 1) * CHK, :].rearrange("(kb r) e -> r kb e", r=128)
            S.dma("sp", lambda e: e.dma_start(out=kt[:, 0, :], in_=ksrc), dkt, reads=[Bkvall[L][u // 2]], writes=[Bkt])
            S.dma("sp", lambda e: e.dma_start(out=kt[:, 1, :].rearrange("p (kb e) -> p kb e", e=128), in_=vsrc), dkt,
                  reads=[Bkvall[L][u // 2]], writes=[Bkt])
            return kt, Bkt

        def band_for(kind, j, gk):
            if gk == MJ[j]:
                ts = 0
            elif gk == MJ[j] - 1:
                ts = 1
            else:
                return None
            return lambda i: bands[:, j % 2, ts, 384 - 128 * i:896 - 128 * i]

        def out_proj(j, wmat, pre=None):
            cs = slice(j * CHK, (j + 1) * CHK)
            for half in range(2):
                wt, Bw = pre[half] if pre is not None else load_w(wmat, half * 512)
                for q in range(4):
                    ncn = half * 4 + q
                    bk = q % 2
                    for u in range(8):
                        S.op("pe", lambda e, u=u, q=q, bk=bk, wt=wt: e.matmul(bank(bk), lhsT=wt[:, u, q * 128:(q + 1) * 128], rhs=gT[:, u, :],
                                                                           start=(u == 0), stop=(u == 7)),
                             reads=[Bw, BgT[u]], writes=[Bps[bk]])
                    S.op("dve", lambda e, ncn=ncn, bk=bk: e.tensor_tensor(out=xT[:, ncn, cs], in0=bank(bk), in1=xT[:, ncn, cs], op=ALU.add),
                         reads=[Bps[bk]], writes=[BxT[j]])

        def diff_layer(l):
            wmat = a_w_in[l]
            phase_kv(l, wmat, 1024, 2048, PC_AN + 8 * l, l)
            for j in range(NLC):
                wts = [load_w(wmat, half * 512) for half in range(2)]
                hT, Bh = rms_chunk(j, PC_AN + 8 * l)
                rp, Brp = load_rope(j)
                wg = {}

                def mkq(k, bk, wts=wts, hT=hT, Bh=Bh):
                    proj_fm(hT, Bh, wts[k // 4][0], wts[k // 4][1], (k % 4) * 128, bk)

                def hookq(t, wg=wg):
                    if t == 5:
                        wg[0] = load_w(wmat, 3072)
                    if t == 9:
                        wg[1] = load_w(wmat, 3072 + 512)

                qk_pipeline(8, mkq, sc[:, 4 + l:5 + l], Bsc, lambda k: (rp, Brp), lambda k: (qT[:, k, :], BqT[k], None), hook=hookq)
                for half in range(2):
                    wt, Bw = wg[half]
                    for uu in range(4):
                        u = half * 4 + uu
                        bk = uu % 2
                        proj_fm(hT, Bh, wt, Bw, uu * 128, bk)
                        S.op("act", lambda e, u=u, bk=bk: e.activation(out=gT[:, u, :], in_=bank(bk), func=AF.Silu), reads=[Bps[bk]], writes=[BgT[u]])
                wo = [load_w(a_w_out[l], half * 512) for half in range(2)]
                nblk = (MJ[j] + 1) * 4
                items = [(u, gk, i) for u in range(8) for gk in range(MJ[j] + 1) for i in range(4)]
                stt = {}

                def stage_a(idx):
                    u, gk, i = items[idx]
                    n = idx % nblk
                    if i == 0:
                        stt[("kv", u, gk)] = load_kv(l, u, gk)
                    kt, Bkt = stt[("kv", u, gk)]
                    if n == 0:
                        stt[("es", u)] = big32.next()
                    es, Bes, _ = stt[("es", u)]
                    sp_ = idx % 2
                    b0, b1 = 2 * sp_, 2 * sp_ + 1
                    ksl = slice(i * 128, (i + 1) * 128)
                    bf = band_for(0, j, gk)
                    nomask = bf is None
                    S.op("pe", lambda e: e.matmul(bank(b0), lhsT=kt[0:64, 0, ksl], rhs=qT[0:64, u, :], start=True, stop=nomask),
                         reads=[Bkt, BqT[u]], writes=[Bps[b0]])
                    S.op("pe", lambda e: e.matmul(bank(b1), lhsT=kt[64:128, 0, ksl], rhs=qT[64:128, u, :], start=True, stop=nomask),
                         reads=[Bkt, BqT[u]], writes=[Bps[b1]])
                    if not nomask:
                        m = bf(i)
                        for bb in (b0, b1):
                            S.op("pe", lambda e, bb=bb: e.matmul(bank(bb), lhsT=C_IDB, rhs=m, start=False, stop=True),
                                 reads=[Bcbf, Bbands], writes=[Bps[bb]])
                    et, Bet, _ = ering.next()
                    S.op("act", lambda e: e.activation(out=et[:], in_=ps[sp_][:], func=AF.Exp), reads=[Bps[b0], Bps[b1]], writes=[Bet])
                    if n == 0:
                        S.op("dve", lambda e: e.tensor_copy(out=es[:, 0:CHK], in_=et[:, 0:CHK]), reads=[Bet], writes=[Bes])
                    else:
                        S.op("dve", lambda e: e.tensor_tensor(out=es[:, 0:CHK], in0=es[:, 0:CHK], in1=et[:, 0:CHK], op=ALU.add), reads=[Bet, Bes], writes=[Bes])
                    stt[("et", idx)] = (et, Bet)

                def stage_b(idx):
                    u, gk, i = items[idx]
                    n = idx % nblk
                    kt, Bkt = stt[("kv", u, gk)]
                    et, Bet = stt.pop(("et", idx))
                    first, last = (n == 0), (n == nblk - 1)
                    vblk = kt[:, 1, i * 128:(i + 1) * 128]
                    for hh in range(2):
                        S.op("pe", lambda e, hh=hh: e.matmul(bank(4 + hh), lhsT=vblk, rhs=et[:, hh * CHK:(hh + 1) * CHK], start=first, stop=last),
                             reads=[Bkt, Bet], writes=[Bps[4 + hh]])
                    S.op("pe", lambda e: e.matmul(bank(7), lhsT=C_ONES, rhs=et[:, CHK:2 * CHK], start=first, stop=last),
                         reads=[Bcbf, Bet], writes=[Bps[7]])
                    if last:
                        pending.extend(epilogue_steps(u))
                        pending.pop(0)()

                def epilogue_steps(u):
                    es, Bes, _ = stt.pop(("es", u))
                    o12, Bo12, _ = big32.next()
                    esb, Besb, _ = bf16r.next()
                    t1, Bt1, _ = f32r.next()
                    t2, Bt2, _ = f32r.next()
                    r1, Br1, _ = f32r.next()
                    sq, Bsq, _ = bf16r.next()

                    def e0():
                        S.op("dve", lambda e: e.tensor_copy(out=o12[:], in_=ps[2][:]), reads=[Bps[4], Bps[5]], writes=[Bo12])
                        S.op("dve", lambda e: e.tensor_copy(out=es[:, CHK:2 * CHK], in_=bank(7)), reads=[Bps[7], Bes], writes=[Bes])
                        S.op("dve", lambda e: e.tensor_copy(out=esb[:], in_=es[:, 0:CHK]), reads=[Bes], writes=[Besb])

                    def e1():
                        S.op("pe", lambda e: e.matmul(bank(6), lhsT=C_ONES, rhs=esb[:], start=True, stop=True), reads=[Bcbf, Besb], writes=[Bps[6]])

                    def e2():
                        S.op("dve", lambda e: e.tensor_tensor(out=t1[:], in0=o12[:, 0:CHK], in1=es[:, CHK:2 * CHK], op=ALU.mult), reads=[Bes, Bo12], writes=[Bt1])
                        S.op("dve", lambda e: e.scalar_tensor_tensor(out=t2[:], in0=o12[:, CHK:2 * CHK], scalar=sc[:, l:l + 1], in1=bank(6),
                                                                    op0=ALU.mult, op1=ALU.mult),
                             reads=[Bo12, Bps[6], Bsc], writes=[Bt2])
                        S.op("dve", lambda e: e.tensor_tensor(out=es[:, 0:CHK], in0=bank(6), in1=es[:, CHK:2 * CHK], op=ALU.mult), reads=[Bps[6], Bes], writes=[Bes])

                    def e3():
                        S.op("pool", lambda e: e.tensor_tensor(out=t1[:], in0=t1[:], in1=t2[:], op=ALU.add), reads=[Bt1, Bt2], writes=[Bt1])
                        S.op("pool", lambda e: e.tensor_tensor(out=sq[:], in0=t1[:], in1=t1[:], op=ALU.mult), reads=[Bt1], writes=[Bsq])
                        S.op("pool", lambda e: e.tensor_tensor(out=es[:, 0:CHK], in0=es[:, 0:CHK], in1=es[:, 0:CHK], op=ALU.mult), reads=[Bes], writes=[Bes])

                    def e4():
                        S.op("pe", lambda e: e.matmul(bank(6), lhsT=C_M128, rhs=sq[:], start=True, stop=True), reads=[Bsq, Bcbf], writes=[Bps[6]])
                        S.op("dve", lambda e: e.scalar_tensor_tensor(out=r1[:], in0=es[:, 0:CHK], scalar=EPS, in1=bank(6), op0=ALU.mult, op1=ALU.add),
                             reads=[Bes, Bps[6]], writes=[Br1])
                        S.op("act", lambda e: e.activation(out=r1[:], in_=r1[:], func=AF.Ln), reads=[Br1], writes=[Br1])

                    def e5():
                        S.op("act", lambda e: e.activation(out=r1[:], in_=r1[:], func=AF.Exp, scale=-0.5), reads=[Br1], writes=[Br1])
                        S.op("dve", lambda e: e.scalar_tensor_tensor(out=t1[:], in0=t1[:], scalar=sc[:, 2 + l:3 + l], in1=r1[:], op0=ALU.mult, op1=ALU.mult),
                             reads=[Bt1, Br1, Bsc], writes=[Bt1])
                        S.op("pool", lambda e: e.tensor_tensor(out=gT[:, u, :], in0=t1[:], in1=gT[:, u, :], op=ALU.mult), reads=[Bt1], writes=[BgT[u]])

                    return [e0, e1, e2, e3, e4, e5]

                pending = []
                for t in range(len(items) + 2):
                    if t < len(items):
                        stage_a(t)
                    if t >= 2:
                        stage_b(t - 2)
                    if pending and (t % nblk) >= 2:
                        pending.pop(0)()
                while pending:
                    pending.pop(0)()
                out_proj(j, a_w_out[l], pre=wo)

        def sb_layer(jl):
            wmat = b_w_in[jl]
            for j in range(NLC):
                wq = [load_w(wmat, half * 512) for half in range(2)]
                hT, Bh = rms_chunk(j, PC_BN + 8 * jl)
                wg = {}
                for half in range(2):
                    wt, Bw = wq[half]
                    for uu in range(4):
                        u = half * 4 + uu
                        bk = uu % 2
                        proj_fm(hT, Bh, wt, Bw, uu * 128, bk)
                        S.op("dve", lambda e, u=u, bk=bk: e.tensor_copy(out=qT[:, u, :], in_=bank(bk)), reads=[Bps[bk]], writes=[BqT[u]])
                    wg[half] = load_w(wmat, 1024 + half * 512)
                for half in range(2):
                    wt, Bw = wg[half]
                    for uu in range(4):
                        u = half * 4 + uu
                        bk = uu % 2
                        proj_fm(hT, Bh, wt, Bw, uu * 128, bk)
                        S.op("act", lambda e, u=u, bk=bk: e.activation(out=gT[:, u, :], in_=bank(bk), func=AF.Silu), reads=[Bps[bk]], writes=[BgT[u]])
                wo = [load_w(b_w_out[jl], half * 512) for half in range(2)]
                nblk = (MJ[j] + 1) * 4
                items = [(u, gk, i) for u in range(8) for gk in range(MJ[j], -1, -1) for i in range(3, -1, -1)]
                stt = {}

                def s1(idx):
                    u, gk, i = items[idx]
                    if i == 3:
                        stt[("kv", u, gk)] = load_kv(2, u, gk)
                    kt, Bkt = stt[("kv", u, gk)]
                    ksl = slice(i * 128, (i + 1) * 128)
                    bf = band_for(1, j, gk)
                    nomask = bf is None
                    S.op("pe", lambda e: e.matmul(bank(0), lhsT=kt[0:64, 0, ksl], rhs=qT[0:64, u, :], start=True, stop=nomask),
                         reads=[Bkt, BqT[u]], writes=[Bps[0]])
                    S.op("pe", lambda e: e.matmul(bank(1), lhsT=kt[64:128, 0, ksl], rhs=qT[64:128, u, :], start=True, stop=nomask),
                         reads=[Bkt, BqT[u]], writes=[Bps[1]])
                    if not nomask:
                        m = bf(i)
                        for bb in (0, 1):
                            S.op("pe", lambda e, bb=bb: e.matmul(bank(bb), lhsT=C_IDB, rhs=m, start=False, stop=True),
                                 reads=[Bcbf, Bbands], writes=[Bps[bb]])
                    eb, Beb, _ = ebring.next()
                    S.op("act", lambda e: e.activation(out=eb[:], in_=ps[0][:], func=AF.Exp, scale=0.125), reads=[Bps[0], Bps[1]], writes=[Beb])
                    stt[("s1a", idx)] = (eb, Beb)

                def s1b(idx):
                    eb, Beb = stt.pop(("s1a", idx))
                    spt, Bspt, _ = spring.next()
                    S.op("act", lambda e: e.activation(out=spt[:], in_=eb[:], func=AF.Ln, bias=1.0), reads=[Beb], writes=[Bspt])
                    stt[("s1", idx)] = (eb, Beb, spt, Bspt)

                def s2(idx):
                    u, gk, i = items[idx]
                    n = idx % nblk
                    eb, Beb, spt, Bspt = stt.pop(("s1", idx))
                    for hh in range(2):
                        S.op("pe", lambda e, hh=hh: e.matmul(bank(2 + hh), lhsT=C_TINC, rhs=spt[:, hh * CHK:(hh + 1) * CHK], start=True, stop=True),
                             reads=[Bcbf, Bspt], writes=[Bps[2 + hh]])
                    tt, Btt, _ = big32.next()
                    if n == 0:
                        S.op("dve", lambda e: e.tensor_copy(out=tt[:], in_=ps[1][:]), reads=[Bps[2], Bps[3]], writes=[Btt])
                    else:
                        S.op("dve", lambda e: e.tensor_tensor(out=tt[:], in0=ps[1][:], in1=Rc[:], op=ALU.add), reads=[Bps[2], Bps[3], BRc], writes=[Btt])
                    if n != nblk - 1:
                        for hh in range(2):
                            S.op("pe", lambda e, hh=hh: e.matmul(bank(4 + hh), lhsT=C_NONES, rhs=spt[:, hh * CHK:(hh + 1) * CHK], start=True, stop=True),
                                 reads=[Bcbf, Bspt], writes=[Bps[4 + hh]])
                        if n == 0:
                            S.op("dve", lambda e: e.tensor_copy(out=Rc[:], in_=ps[2][:]), reads=[Bps[4], Bps[5]], writes=[BRc])
                        else:
                            S.op("dve", lambda e: e.tensor_tensor(out=Rc[:], in0=ps[2][:], in1=Rc[:], op=ALU.add), reads=[Bps[4], Bps[5], BRc], writes=[BRc])
                    stt[("s2", idx)] = (eb, Beb, tt, Btt)

                def s2b(idx):
                    eb, Beb, tt, Btt = stt.pop(("s2", idx))
                    wb, Bwb, _ = wring2.next()
                    S.op("act", lambda e: e.activation(out=wb[:], in_=tt[:], func=AF.Exp), reads=[Btt], writes=[Bwb])
                    at, Bat, _ = ering.next()
                    S.op("pool", lambda e: e.tensor_tensor(out=at[:], in0=eb[:], in1=wb[:], op=ALU.mult), reads=[Beb, Bwb], writes=[Bat])
                    stt[("at", idx)] = (at, Bat)

                def s3(idx):
                    u, gk, i = items[idx]
                    n = idx % nblk
                    kt, Bkt = stt[("kv", u, gk)]
                    at, Bat = stt.pop(("at", idx))
                    first, last = (n == 0), (n == nblk - 1)
                    for hh in range(2):
                        S.op("pe", lambda e, hh=hh: e.matmul(bank(6)[hh * 64:(hh + 1) * 64, :], lhsT=kt[:, 1, i * 128 + hh * 64:i * 128 + (hh + 1) * 64],
                                                            rhs=at[:, hh * CHK:(hh + 1) * CHK], start=first, stop=last, tile_position=(0, hh * 64)),
                             reads=[Bkt, Bat], writes=[Bps[6]])
                    if last:
                        S.op("dve", lambda e: e.tensor_tensor(out=gT[:, u, :], in0=bank(6), in1=gT[:, u, :], op=ALU.mult), reads=[Bps[6]], writes=[BgT[u]])

                NI = len(items)
                for t in range(NI + 4):
                    if t < NI:
                        s1(t)
                    if 3 <= t <= NI + 2:
                        s2b(t - 3)
                    if 2 <= t <= NI + 1:
                        s2(t - 2)
                    if t < NI:
                        s1b(t)
                    if t >= 4:
                        s3(t - 4)
                out_proj(j, b_w_out[jl], pre=wo)

        if n_layers >= 1:
            diff_layer(0)
        if n_layers >= 2:
            diff_layer(1)
        if n_layers >= 3:
            S.dma("sp", lambda e: e.dma_start(out=bands[:], in_=bands_d[1]), d_bands, writes=[Bbands])
            phase_kv(2, w_kv, 0, 1024, PC_KVN, None)
            sb_layer(0)
        if n_layers >= 4:
            sb_layer(1)

        for tb in range(LT // 128):
            xs, Bxs, dxs = xsring.next()
            for half in range(2):
                bk = (tb * 2 + half) % 4
                for q in range(4):
                    kc = half * 4 + q
                    S.op("pe", lambda e, bk=bk, q=q, kc=kc, tb=tb: e.transpose(out=bank(bk)[:, q * 128:(q + 1) * 128],
                                                                            in_=xT[:, kc, tb * 128:(tb + 1) * 128], identity=ident),
                         reads=[BxT[tb // 4], Bcm], writes=[Bps[bk]])
                dst = xs[:, half * 512:(half + 1) * 512]
                if half == 0:
                    S.op("dve", lambda e, dst=dst, bk=bk: e.tensor_copy(out=dst, in_=bank(bk)), reads=[Bps[bk]], writes=[Bxs])
                else:
                    S.op("act", lambda e, dst=dst, bk=bk: e.activation(out=dst, in_=bank(bk), func=AF.Copy), reads=[Bps[bk]], writes=[Bxs])
            S.dma("sp", lambda e, xs=xs, tb=tb: e.dma_start(out=y_d[tb * 128:(tb + 1) * 128, :], in_=xs[:]), dxs, reads=[Bxs])
        for dd in xsring.dsems:
            S.wait_event("sp", ("d", dd, S.dsem_cnt[dd]))
        S.emit(st)
    return nc


def _host_constants():
    ident = np.eye(128, dtype=np.float32)
    rot = np.zeros((128, 128), np.float32)
    for c in range(2):
        for d in range(8):
            rot[c * 64 + d + 8, c * 64 + d] = -1.0
            rot[c * 64 + d, c * 64 + d + 8] = 1.0
    jj, ss = np.meshgrid(np.arange(128), np.arange(128), indexing="ij")
    tinc = -(jj >= ss).astype(np.float32)
    tlow = -(jj < ss).astype(np.float32)
    cmat = np.stack([ident, rot, tinc, tlow], axis=1)
    return np.ascontiguousarray(cmat)


def _rope_tables(p):
    half = 8
    inv = np.power(np.float32(500000.0), -np.arange(half, dtype=np.float32) / np.float32(half)).astype(np.float32)
    pos = np.concatenate([np.arange(g * CHK, (g + 1) * CHK) for g in GCH[p]]).astype(np.float32)
    ang = pos[None, :] * inv[:, None]
    cos, sin = np.cos(ang).astype(np.float32), np.sin(ang).astype(np.float32)
    C = np.ones((128, LT), np.float32)
    Sn = np.zeros((128, LT), np.float32)
    for c in range(2):
        for d in range(16):
            C[c * 64 + d] = cos[d % 8]
            Sn[c * 64 + d] = sin[d % 8]
    return np.ascontiguousarray(np.stack([C, Sn], 0))


def _bands(p):
    r = np.arange(128)[:, None]
    u = np.arange(896)[None, :]
    diag = [(r <= u - 384), (r < u - 384)]
    out = np.zeros((2, 128, 2, 2, 896), np.float32)
    for kind in range(2):
        for jpar in range(2):
            high = ((jpar + p) % 2 == 1)
            if high:
                out[kind, :, jpar, 0] = diag[kind]
                out[kind, :, jpar, 1] = 1.0
            else:
                out[kind, :, jpar, 0] = 0.0
                out[kind, :, jpar, 1] = diag[kind]
    out = (out - 1.0) * 30000.0
    return out.astype(ml_dtypes.bfloat16)


def _params(inp):
    P = np.zeros((128, NPAR), np.float32)
    f = lambda v: np.asarray(v, np.float32)
    for l in range(2):
        P[:, PC_AN + 8 * l:PC_AN + 8 * l + 8] = f(inp["a_norm"])[l].reshape(8, 128).T
        P[:, PC_BN + 8 * l:PC_BN + 8 * l + 8] = f(inp["b_norm"])[l].reshape(8, 128).T
        P[:, PC_QN + l] = np.tile(f(inp["a_q_norm"])[l], 2)
        P[:, PC_KN + l] = np.tile(f(inp["a_k_norm"])[l], 2)
        P[:, PC_SUB + l] = f(inp["a_subln"])[l]
        for t, nm in enumerate(["a_lq1", "a_lk1", "a_lq2", "a_lk2"]):
            c0 = PC_L + (l * 4 + t) * 64
            P[:, c0:c0 + 64] = f(inp[nm])[l][None, :]
    P[:, PC_KVN:PC_KVN + 8] = f(inp["kv_norm"]).reshape(8, 128).T
    return P


_NC_CACHE = {}


def _run(inp, n_layers=4):
    x = np.asarray(inp["x"], np.float32)
    if n_layers not in _NC_CACHE:
        _NC_CACHE[n_layers] = build_nc(n_layers)
    nc = _NC_CACHE[n_layers]
    cmat = _host_constants()
    params = _params(inp)
    shared = {k: np.ascontiguousarray(np.asarray(inp[k], np.float32)) for k in ("a_w_in", "a_w_out", "w_kv", "b_w_in", "b_w_out")}
    in_maps = []
    for c in range(8):
        b, p = c // 2, c % 2
        xs = np.concatenate([x[b, g * CHK:(g + 1) * CHK] for g in GCH[p]], 0)
        m = {"x": np.ascontiguousarray(xs), "params": params, "cmat": cmat, "rope": _rope_tables(p), "bands": _bands(p)}
        m.update(shared)
        in_maps.append(m)
    res = run_bass_kernel_spmd(nc, in_maps, core_ids=list(range(8)))
    out = np.zeros((NB, SEQ, D), np.float32)
    for c in range(8):
        b, p = c // 2, c % 2
        y = res.results[c]["y"]
        for j, g in enumerate(GCH[p]):
            out[b, g * CHK:(g + 1) * CHK] = y[j * CHK:(j + 1) * CHK]
    return out


def kernel(**inputs):
    return _run(inputs, 4)
```

```python
from contextlib import ExitStack
import math
import numpy as np
import ml_dtypes
import concourse.bass as bass
import concourse.mybir as mybir
from concourse.bass_utils import run_bass_kernel_spmd

F32 = mybir.dt.float32
BF16 = mybir.dt.bfloat16
AF = mybir.ActivationFunctionType
ALU = mybir.AluOpType

D = 1024
SEQ = 4096
NB = 4
CHK = 512
NLC = 4
LT = NLC * CHK
GCH = [[0, 3, 4, 7], [1, 2, 5, 6]]
OWNER = {}
for _p in range(2):
    for _j, _g in enumerate(GCH[_p]):
        OWNER[_g] = (_p, _j)
MJ = [max(GCH[0][j], GCH[1][j]) for j in range(NLC)]
EPS = 1e-6
LAMBDA_INIT = [0.8 - 0.6 * math.exp(-0.3 * l) for l in range(2)]
RG = [[0, 1], [2, 3], [4, 5], [6, 7]]
SEMCH = 2048

PC_AN, PC_KVN, PC_BN, PC_QN, PC_KN, PC_SUB, PC_L = 0, 16, 24, 40, 42, 44, 46
NPAR = PC_L + 2 * 4 * 64


class Buf:
    __slots__ = ("name", "w", "r")

    def __init__(self, name):
        self.name = name
        self.w = None
        self.r = {}


class Sched:
    CE = ("pe", "act", "dve", "pool")
    QE = ("pe", "act", "dve", "pool", "sp")

    def __init__(self, nc):
        self.nc = nc
        self.ops = {e: [] for e in self.QE}
        self.cnt = {e: 0 for e in self.CE}
        self.known = {e: {} for e in self.QE}
        self.dsem_cnt = []
        self.esems = None
        self.dsems = None

    def _wait(self, eng, ev):
        if ev is None:
            return
        kind, key, val = ev
        if kind == "e" and key == eng and eng == "pe":
            return
        k = (kind, key)
        if self.known[eng].get(k, 0) >= val:
            return
        self.known[eng][k] = val
        self.ops[eng].append(("wait", ev))

    def new_dsem(self):
        self.dsem_cnt.append(0)
        return len(self.dsem_cnt) - 1

    def op(self, eng, fn, reads=(), writes=()):
        for b in reads:
            self._wait(eng, b.w)
        for b in writes:
            self._wait(eng, b.w)
            for ev in b.r.values():
                if ev[0] == "e" and ev[1] == eng and ev[2] == self.cnt[eng] and False:
                    continue
                self._wait(eng, ev)
        self.cnt[eng] += 1
        idx = self.cnt[eng]
        me = ("e", eng, idx)
        self.ops[eng].append(("op", fn, idx))
        for b in reads:
            b.r[("e", eng)] = me
        for b in writes:
            b.w = me
            b.r = {}
        return me

    def dma(self, q, fn, dsem, reads=(), writes=(), inc=16):
        for b in reads:
            self._wait(q, b.w)
        for b in writes:
            self._wait(q, b.w)
            for ev in b.r.values():
                self._wait(q, ev)
        self.dsem_cnt[dsem] += inc
        me = ("d", dsem, self.dsem_cnt[dsem])
        self.ops[q].append(("dma", fn, dsem, inc))
        for b in reads:
            b.r[("d", dsem)] = me
        for b in writes:
            b.w = me
            b.r = {}
        return me

    def wait_event(self, eng, ev):
        self._wait(eng, ev)

    def _sem_of(self, ev):
        kind, key, val = ev
        if kind == "e":
            return self.esems[key][(val - 1) // SEMCH], (val - 1) % SEMCH + 1
        return self.dsems[key], val

    def emit(self, stack):
        nc = self.nc
        self.esems = {}
        for e in self.CE:
            n = max(1, (self.cnt[e] + SEMCH - 1) // SEMCH)
            self.esems[e] = [stack.enter_context(nc.semaphore(f"s_{e}{i}")) for i in range(n)]
        self.dsems = [stack.enter_context(nc.semaphore(f"d{i}")) for i in range(max(1, len(self.dsem_cnt)))]
        block = stack.enter_context(nc.Block())

        def run(engname):
            def body(eng):
                for o in self.ops[engname]:
                    if o[0] == "wait":
                        s, v = self._sem_of(o[1])
                        eng.wait_ge(s, v)
                    elif o[0] == "op":
                        _, fn, idx = o
                        fn(eng).then_inc(self.esems[engname][(idx - 1) // SEMCH], 1)
                    else:
                        _, fn, dsem, inc = o
                        fn(eng).then_inc(self.dsems[dsem], inc)
            return body

        block.tensor(run("pe"))
        block.scalar(run("act"))
        block.vector(run("dve"))
        block.gpsimd(run("pool"))
        block.sync(run("sp"))


class Ring:
    def __init__(self, S, st, nc, name, n, shape, dtype, with_dsem=True):
        self.tiles = [st.enter_context(nc.sbuf_tensor(f"{name}{i}", shape, dtype)) for i in range(n)]
        self.bufs = [Buf(f"{name}{i}") for i in range(n)]
        self.dsems = [S.new_dsem() for _ in range(n)] if with_dsem else [None] * n
        self.i = 0
        self.n = n

    def next(self):
        k = self.i % self.n
        self.i += 1
        return self.tiles[k], self.bufs[k], self.dsems[k]


def build_nc(n_layers=4):
    nc = bass.Bass("TRN2", target_bir_lowering=False)
    dt_in = lambda n, s, d=F32: nc.dram_tensor(n, s, d, kind="ExternalInput").ap()
    x_d = dt_in("x", [LT, D])
    a_w_in = dt_in("a_w_in", [2, D, 4096])
    a_w_out = dt_in("a_w_out", [2, D, D])
    w_kv = dt_in("w_kv", [D, 2048])
    b_w_in = dt_in("b_w_in", [2, D, 2048])
    b_w_out = dt_in("b_w_out", [2, D, D])
    params_d = dt_in("params", [128, NPAR])
    cmat_d = dt_in("cmat", [128, 4, 128])
    rope_d = dt_in("rope", [2, 128, LT])
    bands_d = dt_in("bands", [2, 128, 2, 2, 896], BF16)
    y_d = nc.dram_tensor("y", [LT, D], F32, kind="ExternalOutput").ap()
    kvin = [[nc.dram_tensor(f"kvin_{l}_{h}", [512, LT], BF16) for h in range(4)] for l in range(3)]
    kvall = [[nc.dram_tensor(f"kvall_{l}_{h}", [1024, LT], BF16) for h in range(4)] for l in range(3)]

    S = Sched(nc)
    with ExitStack() as st:
        sb = lambda n, s, d: st.enter_context(nc.sbuf_tensor(n, s, d))
        xT = sb("xT", [128, 8, LT], F32)
        BxT = [Buf(f"xT{j}") for j in range(NLC)]
        params = sb("params_sb", [128, NPAR], F32)
        Bpar = Buf("params")
        cmat = sb("cmat_sb", [128, 4, 128], F32)
        Bcm = Buf("cmat")
        cbf = sb("cbf", [128, 9, 128], BF16)
        Bcbf = Buf("cbf")
        bands = sb("bands_sb", [128, 2, 2, 896], BF16)
        Bbands = Buf("bands")
        sc = sb("scal", [128, 16], F32)
        Bsc = Buf("scal")
        qT = sb("qT", [128, 8, CHK], BF16)
        BqT = [Buf(f"qT{u}") for u in range(8)]
        gT = sb("gT", [128, 8, CHK], BF16)
        BgT = [Buf(f"gT{u}") for u in range(8)]
        hring = Ring(S, st, nc, "hT", 2, [128, 8, CHK], BF16, with_dsem=False)
        wring = Ring(S, st, nc, "wt", 2, [128, 8, 512], BF16)
        kring = Ring(S, st, nc, "kt", 4, [128, 2, CHK], BF16)
        rring = Ring(S, st, nc, "rp", 2, [128, 2, CHK], F32)
        big32 = Ring(S, st, nc, "big32", 4, [128, 2 * CHK], F32)
        xsring = big32
        f32r = Ring(S, st, nc, "f32t", 4, [128, CHK], F32, with_dsem=False)
        bf16r = Ring(S, st, nc, "bft", 5, [128, CHK], BF16, with_dsem=False)
        ering = Ring(S, st, nc, "et", 4, [128, 2 * CHK], BF16, with_dsem=False)
        ebring = Ring(S, st, nc, "ebt", 5, [128, 2 * CHK], BF16, with_dsem=False)
        Rc = sb("Rc", [128, 2 * CHK], F32)
        BRc = Buf("Rc")
        spring = Ring(S, st, nc, "spt", 3, [128, 2 * CHK], BF16, with_dsem=False)
        wring2 = Ring(S, st, nc, "wbt", 2, [128, 2 * CHK], BF16, with_dsem=False)
        stK = Ring(S, st, nc, "stK", 2, [128, CHK], BF16)
        stV = Ring(S, st, nc, "stV", 2, [128, CHK], BF16)
        stV_d2 = {d: S.new_dsem() for d in stV.dsems}
        ps = [st.enter_context(nc.psum_tensor(f"ps{i}", [128, 2 * CHK], F32)) for i in range(4)]
        Bps = [Buf(f"bank{i}") for i in range(8)]

        def bank(i):
            return ps[i // 2][:, (i % 2) * CHK:(i % 2 + 1) * CHK]

        d_misc = S.new_dsem()
        d_bands = S.new_dsem()
        d_out = S.new_dsem()
        d_ag = [[S.new_dsem() for _ in range(4)] for _ in range(3)]
        Bkvall = [[Buf(f"kvall{l}{h}") for h in range(4)] for l in range(3)]
        kv_store_events = [[[] for _ in range(4)] for _ in range(3)]

        S.dma("sp", lambda e: e.dma_start(out=params[:], in_=params_d), d_misc, writes=[Bpar])
        S.dma("sp", lambda e: e.dma_start(out=cmat[:], in_=cmat_d), d_out, writes=[Bcm])
        S.dma("sp", lambda e: e.dma_start(out=bands[:], in_=bands_d[0]), d_bands, writes=[Bbands])
        S.op("dve", lambda e: e.memset(cbf[:, 0, :], 1.0 / 1024.0), writes=[Bcbf])
        S.op("dve", lambda e: e.memset(cbf[:, 1, :], 0.0), writes=[Bcbf])
        S.op("dve", lambda e: e.memset(cbf[0:64, 1, 0:64], 1.0 / 64.0), writes=[Bcbf])
        S.op("dve", lambda e: e.memset(cbf[64:128, 1, 64:128], 1.0 / 64.0), writes=[Bcbf])
        S.op("dve", lambda e: e.memset(cbf[:, 2, :], 1.0 / 128.0), writes=[Bcbf])
        S.op("dve", lambda e: e.memset(cbf[:, 3, :], 1.0), writes=[Bcbf])
        S.op("dve", lambda e: e.memset(cbf[:, 7, :], -1.0), writes=[Bcbf])
        S.op("dve", lambda e: e.tensor_copy(out=cbf[:, 4:7, :], in_=cmat[:, 1:4, :]), reads=[Bcm], writes=[Bcbf])
        S.op("dve", lambda e: e.tensor_copy(out=cbf[:, 8, :], in_=cmat[:, 0, :]), reads=[Bcm], writes=[Bcbf])
        ident = cmat[:, 0, :]
        C_MEAN, C_BLK, C_M128, C_ONES, C_ROT, C_TINC, C_TLOW, C_NONES, C_IDB = [cbf[:, i, :] for i in range(9)]
        for l in range(2):
            for t in range(2):
                c0 = PC_L + (l * 4 + 2 * t) * 64
                tmp, Btmp, _ = f32r.next()
                S.op("dve", lambda e, tmp=tmp, c0=c0: e.tensor_tensor(out=tmp[:, 0:64], in0=params[:, c0:c0 + 64],
                                                                     in1=params[:, c0 + 64:c0 + 128], op=ALU.mult),
                     reads=[Bpar], writes=[Btmp])
                S.op("dve", lambda e, tmp=tmp, l=l, t=t: e.reduce_sum(out=sc[:, 8 + 2 * l + t:9 + 2 * l + t], in_=tmp[:, 0:64],
                                                                     axis=mybir.AxisListType.X),
                     reads=[Btmp], writes=[Bsc])
            S.op("act", lambda e, l=l: e.activation(out=sc[:, 8 + 2 * l:10 + 2 * l], in_=sc[:, 8 + 2 * l:10 + 2 * l], func=AF.Exp),
                 reads=[Bsc], writes=[Bsc])
            S.op("dve", lambda e, l=l: e.scalar_tensor_tensor(out=sc[:, l:l + 1], in0=sc[:, 9 + 2 * l:10 + 2 * l],
                                                              scalar=-LAMBDA_INIT[l], in1=sc[:, 8 + 2 * l:9 + 2 * l],
                                                              op0=ALU.add, op1=ALU.subtract),
                 reads=[Bsc], writes=[Bsc])
            S.op("dve", lambda e, l=l: e.tensor_scalar(out=sc[:, 2 + l:3 + l], in0=params[:, PC_SUB + l:PC_SUB + l + 1],
                                                       scalar1=1.0 - LAMBDA_INIT[l], scalar2=None, op0=ALU.mult),
                 reads=[Bpar, Bsc], writes=[Bsc])
            S.op("dve", lambda e, l=l: e.tensor_scalar(out=sc[:, 4 + l:5 + l], in0=params[:, PC_QN + l:PC_QN + l + 1],
                                                       scalar1=0.125, scalar2=None, op0=ALU.mult),
                 reads=[Bpar, Bsc], writes=[Bsc])

        for tb in range(LT // 128):
            xs, Bxs, dxs = xsring.next()
            S.dma("sp", lambda e, xs=xs, tb=tb: e.dma_start(out=xs[:], in_=x_d[tb * 128:(tb + 1) * 128, :]), dxs, writes=[Bxs])
            for half in range(2):
                bk = (tb * 2 + half) % 4
                for q in range(4):
                    kc = half * 4 + q
                    S.op("pe", lambda e, bk=bk, q=q, xs=xs, kc=kc: e.transpose(out=bank(bk)[:, q * 128:(q + 1) * 128],
                                                                            in_=xs[:, kc * 128:(kc + 1) * 128], identity=ident),
                         reads=[Bxs, Bcm], writes=[Bps[bk]])
                eng = "dve" if half == 0 else "act"
                dst = xT[:, half * 4:half * 4 + 4, tb * 128:(tb + 1) * 128]
                src = bank(bk).rearrange("p (a b) -> p a b", a=4)
                if eng == "dve":
                    S.op("dve", lambda e, dst=dst, src=src: e.tensor_copy(out=dst, in_=src), reads=[Bps[bk]], writes=[BxT[tb // 4]])
                else:
                    S.op("act", lambda e, dst=dst, src=src: e.activation(out=dst, in_=src, func=AF.Copy), reads=[Bps[bk]], writes=[BxT[tb // 4]])

        def load_w(wmat, col0):
            wt, Bw, dw = wring.next()
            src = wmat[:, col0:col0 + 512].rearrange("(kc p) c -> p kc c", p=128)
            S.dma("pool", lambda e: e.dma_start(out=wt[:], in_=src), dw, writes=[Bw])
            return wt, Bw

        def rms_chunk(j, gcol, dst=None):
            if dst is None:
                hT, Bh1, _ = hring.next()
                Bh = [Bh1]
            else:
                hT, Bh = dst
            cs = slice(j * CHK, (j + 1) * CHK)
            sbk = 7
            for kc in range(8):
                sq, Bsq, _ = bf16r.next()
                if kc % 4 != 3:
                    S.op("act", lambda e, sq=sq, kc=kc: e.activation(out=sq[:], in_=xT[:, kc, cs], func=AF.Square), reads=[BxT[j]], writes=[Bsq])
                else:
                    S.op("pool", lambda e, sq=sq, kc=kc: e.tensor_tensor(out=sq[:], in0=xT[:, kc, cs], in1=xT[:, kc, cs], op=ALU.mult),
                         reads=[BxT[j]], writes=[Bsq])
                S.op("pe", lambda e, sq=sq, kc=kc: e.matmul(bank(sbk), lhsT=C_MEAN, rhs=sq[:], start=(kc == 0), stop=(kc == 7)),
                     reads=[Bsq, Bcbf], writes=[Bps[sbk]])
            rstd, Brs, _ = f32r.next()
            S.op("act", lambda e: e.activation(out=rstd[:], in_=bank(sbk), func=AF.Ln, bias=EPS), reads=[Bps[sbk]], writes=[Brs])
            S.op("act", lambda e: e.activation(out=rstd[:], in_=rstd[:], func=AF.Exp, scale=-0.5), reads=[Brs], writes=[Brs])
            for kc in range(8):
                S.op("dve", lambda e, kc=kc: e.scalar_tensor_tensor(out=hT[:, kc, :], in0=xT[:, kc, cs], scalar=params[:, gcol + kc:gcol + kc + 1],
                                                                 in1=rstd[:], op0=ALU.mult, op1=ALU.mult),
                     reads=[BxT[j], Brs, Bpar], writes=Bh)
            return hT, Bh

        def proj_fm(hT, Bh, wt, Bw, cw, bk):
            for kc in range(8):
                S.op("pe", lambda e, kc=kc: e.matmul(bank(bk), lhsT=wt[:, kc, cw:cw + 128], rhs=hT[:, kc, :], start=(kc == 0), stop=(kc == 7)),
                     reads=[Bw] + Bh, writes=[Bps[bk]])

        def load_rope(j):
            rp, Brp, drp = rring.next()
            S.dma("sp", lambda e: e.dma_start(out=rp[:], in_=rope_d[:, :, j * CHK:(j + 1) * CHK].rearrange("t p n -> p t n")), drp, writes=[Brp])
            return rp, Brp

        def qknorm_rope(bk, gain_ap, Bgain, rp, Brp, dst, Bdst, tb):
            sq, Bsq, _ = bf16r.next()
            S.op("act", lambda e: e.activation(out=sq[:], in_=bank(bk), func=AF.Square), reads=[Bps[bk]], writes=[Bsq])
            S.op("pe", lambda e: e.matmul(bank(tb), lhsT=C_BLK, rhs=sq[:], start=True, stop=True), reads=[Bsq, Bcbf], writes=[Bps[tb]])
            rstd, Brs, _ = f32r.next()
            S.op("act", lambda e: e.activation(out=rstd[:], in_=bank(tb), func=AF.Ln, bias=EPS), reads=[Bps[tb]], writes=[Brs])
            S.op("act", lambda e: e.activation(out=rstd[:], in_=rstd[:], func=AF.Exp, scale=-0.5), reads=[Brs], writes=[Brs])
            tn, Btn, _ = bf16r.next()
            S.op("dve", lambda e: e.scalar_tensor_tensor(out=tn[:], in0=bank(bk), scalar=gain_ap, in1=rstd[:], op0=ALU.mult, op1=ALU.mult),
                 reads=[Bps[bk], Brs, Bgain], writes=[Btn])
            S.op("pe", lambda e: e.matmul(bank(tb), lhsT=C_ROT, rhs=tn[:], start=True, stop=True), reads=[Btn, Bcbf], writes=[Bps[tb]])
            u1, Bu1, _ = f32r.next()
            S.op("pool", lambda e: e.tensor_tensor(out=u1[:], in0=tn[:], in1=rp[:, 0, :], op=ALU.mult), reads=[Btn, Brp], writes=[Bu1])
            u2, Bu2, _ = f32r.next()
            S.op("dve", lambda e: e.tensor_tensor(out=u2[:], in0=bank(tb), in1=rp[:, 1, :], op=ALU.mult), reads=[Bps[tb], Brp], writes=[Bu2])
            S.op("pool", lambda e: e.tensor_tensor(out=dst, in0=u1[:], in1=u2[:], op=ALU.add), reads=[Bu1, Bu2], writes=[Bdst])

        def qk_pipeline(n_units, make_proj, gain_ap, Bgain, rp_fn, dst_fn, hook=None):
            stq = {}

            def P(k):
                make_proj(k, k % 3)

            def N1(k):
                bk, mb = k % 3, 3 + k % 2
                sq, Bsq, _ = bf16r.next()
                S.op("act", lambda e: e.activation(out=sq[:], in_=bank(bk), func=AF.Square), reads=[Bps[bk]], writes=[Bsq])
                S.op("pe", lambda e: e.matmul(bank(mb), lhsT=C_BLK, rhs=sq[:], start=True, stop=True), reads=[Bsq, Bcbf], writes=[Bps[mb]])

            def N2(k):
                bk, mb = k % 3, 3 + k % 2
                rstd, Brs, _ = f32r.next()
                S.op("act", lambda e: e.activation(out=rstd[:], in_=bank(mb), func=AF.Ln, bias=EPS), reads=[Bps[mb]], writes=[Brs])
                S.op("act", lambda e: e.activation(out=rstd[:], in_=rstd[:], func=AF.Exp, scale=-0.5), reads=[Brs], writes=[Brs])
                tn, Btn, _ = bf16r.next()
                S.op("dve", lambda e: e.scalar_tensor_tensor(out=tn[:], in0=bank(bk), scalar=gain_ap, in1=rstd[:], op0=ALU.mult, op1=ALU.mult),
                     reads=[Bps[bk], Brs, Bgain], writes=[Btn])
                stq[k] = (tn, Btn)

            def N3(k):
                rb = 5 + k % 2
                tn, Btn = stq.pop(k)
                rp, Brp = rp_fn(k)
                S.op("pe", lambda e: e.matmul(bank(rb), lhsT=C_ROT, rhs=tn[:], start=True, stop=True), reads=[Btn, Bcbf], writes=[Bps[rb]])
                u1, Bu1, _ = f32r.next()
                S.op("dve", lambda e: e.tensor_tensor(out=u1[:], in0=tn[:], in1=rp[:, 0, :], op=ALU.mult), reads=[Btn, Brp], writes=[Bu1])
                u2, Bu2, _ = f32r.next()
                S.op("dve", lambda e: e.tensor_tensor(out=u2[:], in0=bank(rb), in1=rp[:, 1, :], op=ALU.mult), reads=[Bps[rb], Brp], writes=[Bu2])
                dst, Bdst, post = dst_fn(k)
                S.op("pool", lambda e: e.tensor_tensor(out=dst, in0=u1[:], in1=u2[:], op=ALU.add), reads=[Bu1, Bu2], writes=[Bdst])
                if post is not None:
                    post()

            for t in range(n_units + 3):
                if hook is not None:
                    hook(t)
                if t < n_units:
                    P(t)
                if 1 <= t <= n_units:
                    N1(t - 1)
                if 2 <= t <= n_units + 1:
                    N2(t - 2)
                if t >= 3:
                    N3(t - 3)

        def phase_kv(L, wmat, kcol, vcol, gcol, qk_layer):
            htiles = [(hring.tiles[0], [hring.bufs[0]]), (hring.tiles[1], [hring.bufs[1]]), (qT, list(BqT)), (gT, list(BgT))]
            for j in range(NLC):
                rms_chunk(j, gcol, dst=htiles[j])
            wK = load_w(wmat, kcol)
            wV = load_w(wmat, vcol)
            for half in range(2):
                wt, Bw = wK
                if qk_layer is not None:
                    ropes = {}

                    def mk(k, bk, wt=wt, Bw=Bw, ropes=ropes):
                        jj, uu = k // 4, k % 4
                        if uu == 0:
                            ropes[jj] = load_rope(jj)
                        proj_fm(htiles[jj][0], htiles[jj][1], wt, Bw, uu * 128, bk)

                    def dstf(k, half=half):
                        jj, uu = k // 4, k % 4
                        u = half * 4 + uu
                        kst, Bkst, dk = stK.next()

                        def post():
                            dst = kvin[L][u // 2].ap()[(u % 2) * 128:(u % 2) * 128 + 128, jj * CHK:(jj + 1) * CHK]
                            ev = S.dma("sp", lambda e: e.dma_start(out=dst, in_=kst[:]), dk, reads=[Bkst])
                            kv_store_events[L][u // 2].append(ev)
                        return kst[:], Bkst, post

                    qk_pipeline(16, mk, params[:, PC_KN + qk_layer:PC_KN + qk_layer + 1], Bpar, lambda k, ropes=ropes: ropes[k // 4], dstf)
                else:
                    for jj in range(NLC):
                        for uu in range(4):
                            u = half * 4 + uu
                            bk = uu % 2
                            proj_fm(htiles[jj][0], htiles[jj][1], wt, Bw, uu * 128, bk)
                            kst, Bkst, dk = stK.next()
                            S.op("act", lambda e, kst=kst, bk=bk: e.activation(out=kst[:], in_=bank(bk), func=AF.Copy), reads=[Bps[bk]], writes=[Bkst])
                            dst = kvin[L][u // 2].ap()[(u % 2) * 128:(u % 2) * 128 + 128, jj * CHK:(jj + 1) * CHK]
                            ev = S.dma("sp", lambda e, dst=dst, kst=kst: e.dma_start(out=dst, in_=kst[:]), dk, reads=[Bkst])
                            kv_store_events[L][u // 2].append(ev)
                if half == 0:
                    wK = load_w(wmat, kcol + 512)
                wt, Bw = wV
                for jj in range(NLC):
                    hT, Bh = htiles[jj]
                    for tb in range(4):
                        bk = 4 + tb % 2
                        for kc in range(8):
                            S.op("pe", lambda e, kc=kc, tb=tb, bk=bk, wt=wt, hT=hT: e.matmul(bank(bk), lhsT=hT[:, kc, tb * 128:(tb + 1) * 128], rhs=wt[:, kc, :],
                                                                                   start=(kc == 0), stop=(kc == 7)),
                                 reads=[Bw] + Bh, writes=[Bps[bk]])
                        vst, Bvst, dv = stV.next()
                        S.op("dve", lambda e, vst=vst, bk=bk: e.tensor_copy(out=vst[:], in_=bank(bk)), reads=[Bps[bk]], writes=[Bvst])
                        for pr in range(2):
                            hp = half * 2 + pr
                            dst = kvin[L][hp].ap()[256:512, :].rearrange("(h r) (t e) -> (r t) h e", h=2, e=128)[
                                jj * CHK + tb * 128:jj * CHK + (tb + 1) * 128, :, :]
                            src = vst[:, pr * 256:(pr + 1) * 256].rearrange("p (h e) -> p h e", h=2)
                            ev = S.dma("sp", lambda e, dst=dst, src=src: e.dma_start(out=dst, in_=src), dv if pr == 0 else stV_d2[dv], reads=[Bvst])
                            kv_store_events[L][hp].append(ev)
                if half == 0:
                    wV = load_w(wmat, vcol + 512)
                for hp in (2 * half, 2 * half + 1):
                    mx = {}
                    for ev in kv_store_events[L][hp]:
                        mx[ev[1]] = max(mx.get(ev[1], 0), ev[2])
                    for k_, v_ in mx.items():
                        S.wait_event("pool", ("d", k_, v_))
                    S.dma("pool", lambda e, hp=hp: e.collective_compute("AllGather", ALU.bypass, replica_groups=RG,
                                                                         ins=[kvin[L][hp].ap().opt()], outs=[kvall[L][hp].ap().opt()]),
                          d_ag[L][hp], writes=[Bkvall[L][hp]], inc=1)

        def load_kv(L, u, gk):
            kt, Bkt, dkt = kring.next()
            rho, jj = OWNER[gk]
            base = kvall[L][u // 2].ap()
            ksrc = base[rho * 512 + (u % 2) * 128:rho * 512 + (u % 2) * 128 + 128, jj * CHK:(jj + 1) * CHK]
            vsrc = base[rho * 512 + 256 + (u % 2) * 128:rho * 512 + 256 + (u % 2) * 128 + 128, :].rearrange(
                "r (t e) -> (r t) e", e=128)[jj * CHK:(jj + 1) * CHK, :].rearrange("(kb r) e -> r kb e", r=128)
            S.dma("sp", lambda e: e.dma_start(out=kt[:, 0, :], in_=ksrc), dkt, reads=[Bkvall[L][u // 2]], writes=[Bkt])
            S.dma("sp", lambda e: e.dma_start(out=kt[:, 1, :].rearrange("p (kb e) -> p kb e", e=128), in_=vsrc), dkt,
                  reads=[Bkvall[L][u // 2]], writes=[Bkt])
            return kt, Bkt

        def band_for(kind, j, gk):
            if gk == MJ[j]:
                ts = 0
            elif gk == MJ[j] - 1:
                ts = 1
            else:
                return None
            return lambda i: bands[:, j % 2, ts, 384 - 128 * i:896 - 128 * i]

        def out_proj(j, wmat, pre=None):
            cs = slice(j * CHK, (j + 1) * CHK)
            for half in range(2):
                wt, Bw = pre[half] if pre is not None else load_w(wmat, half * 512)
                for q in range(4):
                    ncn = half * 4 + q
                    bk = q % 2
                    for u in range(8):
                        S.op("pe", lambda e, u=u, q=q, bk=bk, wt=wt: e.matmul(bank(bk), lhsT=wt[:, u, q * 128:(q + 1) * 128], rhs=gT[:, u, :],
                                                                           start=(u == 0), stop=(u == 7)),
                             reads=[Bw, BgT[u]], writes=[Bps[bk]])
                    S.op("dve", lambda e, ncn=ncn, bk=bk: e.tensor_tensor(out=xT[:, ncn, cs], in0=bank(bk), in1=xT[:, ncn, cs], op=ALU.add),
                         reads=[Bps[bk]], writes=[BxT[j]])

        def diff_layer(l):
            wmat = a_w_in[l]
            phase_kv(l, wmat, 1024, 2048, PC_AN + 8 * l, l)
            for j in range(NLC):
                wts = [load_w(wmat, half * 512) for half in range(2)]
                hT, Bh = rms_chunk(j, PC_AN + 8 * l)
                rp, Brp = load_rope(j)
                wg = {}

                def mkq(k, bk, wts=wts, hT=hT, Bh=Bh):
                    proj_fm(hT, Bh, wts[k // 4][0], wts[k // 4][1], (k % 4) * 128, bk)

                def hookq(t, wg=wg):
                    if t == 5:
                        wg[0] = load_w(wmat, 3072)
                    if t == 9:
                        wg[1] = load_w(wmat, 3072 + 512)

                qk_pipeline(8, mkq, sc[:, 4 + l:5 + l], Bsc, lambda k: (rp, Brp), lambda k: (qT[:, k, :], BqT[k], None), hook=hookq)
                for half in range(2):
                    wt, Bw = wg[half]
                    for uu in range(4):
                        u = half * 4 + uu
                        bk = uu % 2
                        proj_fm(hT, Bh, wt, Bw, uu * 128, bk)
                        S.op("act", lambda e, u=u, bk=bk: e.activation(out=gT[:, u, :], in_=bank(bk), func=AF.Silu), reads=[Bps[bk]], writes=[BgT[u]])
                wo = [load_w(a_w_out[l], half * 512) for half in range(2)]
                nblk = (MJ[j] + 1) * 4
                items = [(u, gk, i) for u in range(8) for gk in range(MJ[j] + 1) for i in range(4)]
                stt = {}

                def stage_a(idx):
                    u, gk, i = items[idx]
                    n = idx % nblk
                    if i == 0:
                        stt[("kv", u, gk)] = load_kv(l, u, gk)
                    kt, Bkt = stt[("kv", u, gk)]
                    if n == 0:
                        stt[("es", u)] = big32.next()
                    es, Bes, _ = stt[("es", u)]
                    sp_ = idx % 2
                    b0, b1 = 2 * sp_, 2 * sp_ + 1
                    ksl = slice(i * 128, (i + 1) * 128)
                    bf = band_for(0, j, gk)
                    nomask = bf is None
                    S.op("pe", lambda e: e.matmul(bank(b0), lhsT=kt[0:64, 0, ksl], rhs=qT[0:64, u, :], start=True, stop=nomask),
                         reads=[Bkt, BqT[u]], writes=[Bps[b0]])
                    S.op("pe", lambda e: e.matmul(bank(b1), lhsT=kt[64:128, 0, ksl], rhs=qT[64:128, u, :], start=True, stop=nomask),
                         reads=[Bkt, BqT[u]], writes=[Bps[b1]])
                    if not nomask:
                        m = bf(i)
                        for bb in (b0, b1):
                            S.op("pe", lambda e, bb=bb: e.matmul(bank(bb), lhsT=C_IDB, rhs=m, start=False, stop=True),
                                 reads=[Bcbf, Bbands], writes=[Bps[bb]])
                    et, Bet, _ = ering.next()
                    S.op("act", lambda e: e.activation(out=et[:], in_=ps[sp_][:], func=AF.Exp), reads=[Bps[b0], Bps[b1]], writes=[Bet])
                    if n == 0:
                        S.op("dve", lambda e: e.tensor_copy(out=es[:, 0:CHK], in_=et[:, 0:CHK]), reads=[Bet], writes=[Bes])
                    else:
                        S.op("dve", lambda e: e.tensor_tensor(out=es[:, 0:CHK], in0=es[:, 0:CHK], in1=et[:, 0:CHK], op=ALU.add), reads=[Bet, Bes], writes=[Bes])
                    stt[("et", idx)] = (et, Bet)

                def stage_b(idx):
                    u, gk, i = items[idx]
                    n = idx % nblk
                    kt, Bkt = stt[("kv", u, gk)]
                    et, Bet = stt.pop(("et", idx))
                    first, last = (n == 0), (n == nblk - 1)
                    vblk = kt[:, 1, i * 128:(i + 1) * 128]
                    for hh in range(2):
                        S.op("pe", lambda e, hh=hh: e.matmul(bank(4 + hh), lhsT=vblk, rhs=et[:, hh * CHK:(hh + 1) * CHK], start=first, stop=last),
                             reads=[Bkt, Bet], writes=[Bps[4 + hh]])
                    S.op("pe", lambda e: e.matmul(bank(7), lhsT=C_ONES, rhs=et[:, CHK:2 * CHK], start=first, stop=last),
                         reads=[Bcbf, Bet], writes=[Bps[7]])
                    if last:
                        pending.extend(epilogue_steps(u))
                        pending.pop(0)()

                def epilogue_steps(u):
                    es, Bes, _ = stt.pop(("es", u))
                    o12, Bo12, _ = big32.next()
                    esb, Besb, _ = bf16r.next()
                    t1, Bt1, _ = f32r.next()
                    t2, Bt2, _ = f32r.next()
                    r1, Br1, _ = f32r.next()
                    sq, Bsq, _ = bf16r.next()

                    def e0():
                        S.op("act", lambda e: e.activation(out=o12[:], in_=ps[2][:], func=AF.Copy), reads=[Bps[4], Bps[5]], writes=[Bo12])
                        S.op("dve", lambda e: e.tensor_copy(out=es[:, CHK:2 * CHK], in_=bank(7)), reads=[Bps[7], Bes], writes=[Bes])
                        S.op("dve", lambda e: e.tensor_copy(out=esb[:], in_=es[:, 0:CHK]), reads=[Bes], writes=[Besb])

                    def e1():
                        S.op("pe", lambda e: e.matmul(bank(6), lhsT=C_ONES, rhs=esb[:], start=True, stop=True), reads=[Bcbf, Besb], writes=[Bps[6]])

                    def e2():
                        S.op("dve", lambda e: e.tensor_tensor(out=t1[:], in0=o12[:, 0:CHK], in1=es[:, CHK:2 * CHK], op=ALU.mult), reads=[Bes, Bo12], writes=[Bt1])
                        S.op("dve", lambda e: e.scalar_tensor_tensor(out=t2[:], in0=o12[:, CHK:2 * CHK], scalar=sc[:, l:l + 1], in1=bank(6),
                                                                    op0=ALU.mult, op1=ALU.mult),
                             reads=[Bo12, Bps[6], Bsc], writes=[Bt2])
                        S.op("dve", lambda e: e.tensor_tensor(out=es[:, 0:CHK], in0=bank(6), in1=es[:, CHK:2 * CHK], op=ALU.mult), reads=[Bps[6], Bes], writes=[Bes])

                    def e3():
                        S.op("pool", lambda e: e.tensor_tensor(out=t1[:], in0=t1[:], in1=t2[:], op=ALU.add), reads=[Bt1, Bt2], writes=[Bt1])
                        S.op("pool", lambda e: e.tensor_tensor(out=sq[:], in0=t1[:], in1=t1[:], op=ALU.mult), reads=[Bt1], writes=[Bsq])
                        S.op("pool", lambda e: e.tensor_tensor(out=es[:, 0:CHK], in0=es[:, 0:CHK], in1=es[:, 0:CHK], op=ALU.mult), reads=[Bes], writes=[Bes])

                    def e4():
                        S.op("pe", lambda e: e.matmul(bank(6), lhsT=C_M128, rhs=sq[:], start=True, stop=True), reads=[Bsq, Bcbf], writes=[Bps[6]])
                        S.op("dve", lambda e: e.scalar_tensor_tensor(out=r1[:], in0=es[:, 0:CHK], scalar=EPS, in1=bank(6), op0=ALU.mult, op1=ALU.add),
                             reads=[Bes, Bps[6]], writes=[Br1])
                        S.op("act", lambda e: e.activation(out=r1[:], in_=r1[:], func=AF.Ln), reads=[Br1], writes=[Br1])

                    def e5():
                        S.op("act", lambda e: e.activation(out=r1[:], in_=r1[:], func=AF.Exp, scale=-0.5), reads=[Br1], writes=[Br1])
                        S.op("dve", lambda e: e.scalar_tensor_tensor(out=t1[:], in0=t1[:], scalar=sc[:, 2 + l:3 + l], in1=r1[:], op0=ALU.mult, op1=ALU.mult),
                             reads=[Bt1, Br1, Bsc], writes=[Bt1])
                        S.op("pool", lambda e: e.tensor_tensor(out=gT[:, u, :], in0=t1[:], in1=gT[:, u, :], op=ALU.mult), reads=[Bt1], writes=[BgT[u]])

                    return [e0, e1, e2, e3, e4, e5]

                pending = []
                for t in range(len(items) + 2):
                    if t < len(items):
                        stage_a(t)
                    if t >= 2:
                        stage_b(t - 2)
                    if pending and (t % nblk) >= 3:
                        pending.pop(0)()
                while pending:
                    pending.pop(0)()
                out_proj(j, a_w_out[l], pre=wo)

        def sb_layer(jl):
            wmat = b_w_in[jl]
            for j in range(NLC):
                wq = [load_w(wmat, half * 512) for half in range(2)]
                hT, Bh = rms_chunk(j, PC_BN + 8 * jl)
                wg = {}
                for half in range(2):
                    wt, Bw = wq[half]
                    for uu in range(4):
                        u = half * 4 + uu
                        bk = uu % 2
                        proj_fm(hT, Bh, wt, Bw, uu * 128, bk)
                        S.op("dve", lambda e, u=u, bk=bk: e.tensor_copy(out=qT[:, u, :], in_=bank(bk)), reads=[Bps[bk]], writes=[BqT[u]])
                    wg[half] = load_w(wmat, 1024 + half * 512)
                for half in range(2):
                    wt, Bw = wg[half]
                    for uu in range(4):
                        u = half * 4 + uu
                        bk = uu % 2
                        proj_fm(hT, Bh, wt, Bw, uu * 128, bk)
                        S.op("act", lambda e, u=u, bk=bk: e.activation(out=gT[:, u, :], in_=bank(bk), func=AF.Silu), reads=[Bps[bk]], writes=[BgT[u]])
                wo = [load_w(b_w_out[jl], half * 512) for half in range(2)]
                nblk = (MJ[j] + 1) * 4
                items = [(u, gk, i) for u in range(8) for gk in range(MJ[j], -1, -1) for i in range(3, -1, -1)]
                stt = {}

                def s1(idx):
                    u, gk, i = items[idx]
                    if i == 3:
                        stt[("kv", u, gk)] = load_kv(2, u, gk)
                    kt, Bkt = stt[("kv", u, gk)]
                    ksl = slice(i * 128, (i + 1) * 128)
                    bf = band_for(1, j, gk)
                    nomask = bf is None
                    S.op("pe", lambda e: e.matmul(bank(0), lhsT=kt[0:64, 0, ksl], rhs=qT[0:64, u, :], start=True, stop=nomask),
                         reads=[Bkt, BqT[u]], writes=[Bps[0]])
                    S.op("pe", lambda e: e.matmul(bank(1), lhsT=kt[64:128, 0, ksl], rhs=qT[64:128, u, :], start=True, stop=nomask),
                         reads=[Bkt, BqT[u]], writes=[Bps[1]])
                    if not nomask:
                        m = bf(i)
                        for bb in (0, 1):
                            S.op("pe", lambda e, bb=bb: e.matmul(bank(bb), lhsT=C_IDB, rhs=m, start=False, stop=True),
                                 reads=[Bcbf, Bbands], writes=[Bps[bb]])
                    eb, Beb, _ = ebring.next()
                    S.op("act", lambda e: e.activation(out=eb[:], in_=ps[0][:], func=AF.Exp, scale=0.125), reads=[Bps[0], Bps[1]], writes=[Beb])
                    stt[("s1a", idx)] = (eb, Beb)

                def s1b(idx):
                    eb, Beb = stt.pop(("s1a", idx))
                    spt, Bspt, _ = spring.next()
                    S.op("act", lambda e: e.activation(out=spt[:], in_=eb[:], func=AF.Ln, bias=1.0), reads=[Beb], writes=[Bspt])
                    stt[("s1", idx)] = (eb, Beb, spt, Bspt)

                def s2(idx):
                    u, gk, i = items[idx]
                    n = idx % nblk
                    eb, Beb, spt, Bspt = stt.pop(("s1", idx))
                    for hh in range(2):
                        S.op("pe", lambda e, hh=hh: e.matmul(bank(2 + hh), lhsT=C_TINC, rhs=spt[:, hh * CHK:(hh + 1) * CHK], start=True, stop=True),
                             reads=[Bcbf, Bspt], writes=[Bps[2 + hh]])
                    tt, Btt, _ = big32.next()
                    if n == 0:
                        S.op("dve", lambda e: e.tensor_copy(out=tt[:], in_=ps[1][:]), reads=[Bps[2], Bps[3]], writes=[Btt])
                    else:
                        S.op("dve", lambda e: e.tensor_tensor(out=tt[:], in0=ps[1][:], in1=Rc[:], op=ALU.add), reads=[Bps[2], Bps[3], BRc], writes=[Btt])
                    if n != nblk - 1:
                        for hh in range(2):
                            S.op("pe", lambda e, hh=hh: e.matmul(bank(4 + hh), lhsT=C_NONES, rhs=spt[:, hh * CHK:(hh + 1) * CHK], start=True, stop=True),
                                 reads=[Bcbf, Bspt], writes=[Bps[4 + hh]])
                        if n == 0:
                            S.op("dve", lambda e: e.tensor_copy(out=Rc[:], in_=ps[2][:]), reads=[Bps[4], Bps[5]], writes=[BRc])
                        else:
                            S.op("dve", lambda e: e.tensor_tensor(out=Rc[:], in0=ps[2][:], in1=Rc[:], op=ALU.add), reads=[Bps[4], Bps[5], BRc], writes=[BRc])
                    stt[("s2", idx)] = (eb, Beb, tt, Btt)

                def s2b(idx):
                    eb, Beb, tt, Btt = stt.pop(("s2", idx))
                    wb, Bwb, _ = wring2.next()
                    S.op("act", lambda e: e.activation(out=wb[:], in_=tt[:], func=AF.Exp), reads=[Btt], writes=[Bwb])
                    at, Bat, _ = ering.next()
                    S.op("pool", lambda e: e.tensor_tensor(out=at[:], in0=eb[:], in1=wb[:], op=ALU.mult), reads=[Beb, Bwb], writes=[Bat])
                    stt[("at", idx)] = (at, Bat)

                def s3(idx):
                    u, gk, i = items[idx]
                    n = idx % nblk
                    kt, Bkt = stt[("kv", u, gk)]
                    at, Bat = stt.pop(("at", idx))
                    first, last = (n == 0), (n == nblk - 1)
                    for hh in range(2):
                        S.op("pe", lambda e, hh=hh: e.matmul(bank(6)[hh * 64:(hh + 1) * 64, :], lhsT=kt[:, 1, i * 128 + hh * 64:i * 128 + (hh + 1) * 64],
                                                            rhs=at[:, hh * CHK:(hh + 1) * CHK], start=first, stop=last, tile_position=(0, hh * 64)),
                             reads=[Bkt, Bat], writes=[Bps[6]])
                    if last:
                        S.op("dve", lambda e: e.tensor_tensor(out=gT[:, u, :], in0=bank(6), in1=gT[:, u, :], op=ALU.mult), reads=[Bps[6]], writes=[BgT[u]])

                NI = len(items)
                for t in range(NI + 4):
                    if t < NI:
                        s1(t)
                    if 3 <= t <= NI + 2:
                        s2b(t - 3)
                    if 2 <= t <= NI + 1:
                        s2(t - 2)
                    if t < NI:
                        s1b(t)
                    if t >= 4:
                        s3(t - 4)
                out_proj(j, b_w_out[jl], pre=wo)

        if n_layers >= 1:
            diff_layer(0)
        if n_layers >= 2:
            diff_layer(1)
        if n_layers >= 3:
            S.dma("sp", lambda e: e.dma_start(out=bands[:], in_=bands_d[1]), d_bands, writes=[Bbands])
            phase_kv(2, w_kv, 0, 1024, PC_KVN, None)
            sb_layer(0)
        if n_layers >= 4:
            sb_layer(1)

        for tb in range(LT // 128):
            xs, Bxs, dxs = xsring.next()
            for half in range(2):
                bk = (tb * 2 + half) % 4
                for q in range(4):
                    kc = half * 4 + q
                    S.op("pe", lambda e, bk=bk, q=q, kc=kc, tb=tb: e.transpose(out=bank(bk)[:, q * 128:(q + 1) * 128],
                                                                            in_=xT[:, kc, tb * 128:(tb + 1) * 128], identity=ident),
                         reads=[BxT[tb // 4], Bcm], writes=[Bps[bk]])
                dst = xs[:, half * 512:(half + 1) * 512]
                if half == 0:
                    S.op("dve", lambda e, dst=dst, bk=bk: e.tensor_copy(out=dst, in_=bank(bk)), reads=[Bps[bk]], writes=[Bxs])
                else:
                    S.op("act", lambda e, dst=dst, bk=bk: e.activation(out=dst, in_=bank(bk), func=AF.Copy), reads=[Bps[bk]], writes=[Bxs])
            S.dma("sp", lambda e, xs=xs, tb=tb: e.dma_start(out=y_d[tb * 128:(tb + 1) * 128, :], in_=xs[:]), dxs, reads=[Bxs])
        for dd in xsring.dsems:
            S.wait_event("sp", ("d", dd, S.dsem_cnt[dd]))
        S.emit(st)
    return nc


def _host_constants():
    ident = np.eye(128, dtype=np.float32)
    rot = np.zeros((128, 128), np.float32)
    for c in range(2):
        for d in range(8):
            rot[c * 64 + d + 8, c * 64 + d] = -1.0
            rot[c * 64 + d, c * 64 + d + 8] = 1.0
    jj, ss = np.meshgrid(np.arange(128), np.arange(128), indexing="ij")
    tinc = -(jj >= ss).astype(np.float32)
    tlow = -(jj < ss).astype(np.float32)
    cmat = np.stack([ident, rot, tinc, tlow], axis=1)
    return np.ascontiguousarray(cmat)


def _rope_tables(p):
    half = 8
    inv = np.power(np.float32(500000.0), -np.arange(half, dtype=np.float32) / np.float32(half)).astype(np.float32)
    pos = np.concatenate([np.arange(g * CHK, (g + 1) * CHK) for g in GCH[p]]).astype(np.float32)
    ang = pos[None, :] * inv[:, None]
    cos, sin = np.cos(ang).astype(np.float32), np.sin(ang).astype(np.float32)
    C = np.ones((128, LT), np.float32)
    Sn = np.zeros((128, LT), np.float32)
    for c in range(2):
        for d in range(16):
            C[c * 64 + d] = cos[d % 8]
            Sn[c * 64 + d] = sin[d % 8]
    return np.ascontiguousarray(np.stack([C, Sn], 0))


def _bands(p):
    r = np.arange(128)[:, None]
    u = np.arange(896)[None, :]
    diag = [(r <= u - 384), (r < u - 384)]
    out = np.zeros((2, 128, 2, 2, 896), np.float32)
    for kind in range(2):
        for jpar in range(2):
            high = ((jpar + p) % 2 == 1)
            if high:
                out[kind, :, jpar, 0] = diag[kind]
                out[kind, :, jpar, 1] = 1.0
            else:
                out[kind, :, jpar, 0] = 0.0
                out[kind, :, jpar, 1] = diag[kind]
    out = (out - 1.0) * 30000.0
    return out.astype(ml_dtypes.bfloat16)


def _params(inp):
    P = np.zeros((128, NPAR), np.float32)
    f = lambda v: np.asarray(v, np.float32)
    for l in range(2):
        P[:, PC_AN + 8 * l:PC_AN + 8 * l + 8] = f(inp["a_norm"])[l].reshape(8, 128).T
        P[:, PC_BN + 8 * l:PC_BN + 8 * l + 8] = f(inp["b_norm"])[l].reshape(8, 128).T
        P[:, PC_QN + l] = np.tile(f(inp["a_q_norm"])[l], 2)
        P[:, PC_KN + l] = np.tile(f(inp["a_k_norm"])[l], 2)
        P[:, PC_SUB + l] = f(inp["a_subln"])[l]
        for t, nm in enumerate(["a_lq1", "a_lk1", "a_lq2", "a_lk2"]):
            c0 = PC_L + (l * 4 + t) * 64
            P[:, c0:c0 + 64] = f(inp[nm])[l][None, :]
    P[:, PC_KVN:PC_KVN + 8] = f(inp["kv_norm"]).reshape(8, 128).T
    return P


_NC_CACHE = {}


def _run(inp, n_layers=4):
    x = np.asarray(inp["x"], np.float32)
    if n_layers not in _NC_CACHE:
        _NC_CACHE[n_layers] = build_nc(n_layers)
    nc = _NC_CACHE[n_layers]
    cmat = _host_constants()
    params = _params(inp)
    shared = {k: np.ascontiguousarray(np.asarray(inp[k], np.float32)) for k in ("a_w_in", "a_w_out", "w_kv", "b_w_in", "b_w_out")}
    in_maps = []
    for c in range(8):
        b, p = c // 2, c % 2
        xs = np.concatenate([x[b, g * CHK:(g + 1) * CHK] for g in GCH[p]], 0)
        m = {"x": np.ascontiguousarray(xs), "params": params, "cmat": cmat, "rope": _rope_tables(p), "bands": _bands(p)}
        m.update(shared)
        in_maps.append(m)
    res = run_bass_kernel_spmd(nc, in_maps, core_ids=list(range(8)))
    out = np.zeros((NB, SEQ, D), np.float32)
    for c in range(8):
        b, p = c // 2, c % 2
        y = res.results[c]["y"]
        for j, g in enumerate(GCH[p]):
            out[b, g * CHK:(g + 1) * CHK] = y[j * CHK:(j + 1) * CHK]
    return out


def kernel(**inputs):
    return _run(inputs, 4)
```

```python
from contextlib import ExitStack
import math
import numpy as np
import ml_dtypes
import concourse.bass as bass
import concourse.mybir as mybir
from concourse.bass_utils import run_bass_kernel_spmd

F32 = mybir.dt.float32
BF16 = mybir.dt.bfloat16
AF = mybir.ActivationFunctionType
ALU = mybir.AluOpType

D = 1024
SEQ = 4096
NB = 4
CHK = 512
NLC = 4
LT = NLC * CHK
GCH = [[0, 3, 4, 7], [1, 2, 5, 6]]
OWNER = {}
for _p in range(2):
    for _j, _g in enumerate(GCH[_p]):
        OWNER[_g] = (_p, _j)
MJ = [max(GCH[0][j], GCH[1][j]) for j in range(NLC)]
EPS = 1e-6
LAMBDA_INIT = [0.8 - 0.6 * math.exp(-0.3 * l) for l in range(2)]
RG = [[0, 1], [2, 3], [4, 5], [6, 7]]
SEMCH = 2048

PC_AN, PC_KVN, PC_BN, PC_QN, PC_KN, PC_SUB, PC_L = 0, 16, 24, 40, 42, 44, 46
NPAR = PC_L + 2 * 4 * 64


class Buf:
    __slots__ = ("name", "w", "r")

    def __init__(self, name):
        self.name = name
        self.w = None
        self.r = {}


class Sched:
    CE = ("pe", "act", "dve", "pool")
    QE = ("pe", "act", "dve", "pool", "sp")

    def __init__(self, nc):
        self.nc = nc
        self.ops = {e: [] for e in self.QE}
        self.cnt = {e: 0 for e in self.CE}
        self.known = {e: {} for e in self.QE}
        self.dsem_cnt = []
        self.esems = None
        self.dsems = None

    def _wait(self, eng, ev):
        if ev is None:
            return
        kind, key, val = ev
        if kind == "e" and key == eng and eng == "pe":
            return
        k = (kind, key)
        if self.known[eng].get(k, 0) >= val:
            return
        self.known[eng][k] = val
        self.ops[eng].append(("wait", ev))

    def new_dsem(self):
        self.dsem_cnt.append(0)
        return len(self.dsem_cnt) - 1

    def op(self, eng, fn, reads=(), writes=()):
        for b in reads:
            self._wait(eng, b.w)
        for b in writes:
            self._wait(eng, b.w)
            for ev in b.r.values():
                if ev[0] == "e" and ev[1] == eng and ev[2] == self.cnt[eng] and False:
                    continue
                self._wait(eng, ev)
        self.cnt[eng] += 1
        idx = self.cnt[eng]
        me = ("e", eng, idx)
        self.ops[eng].append(("op", fn, idx))
        for b in reads:
            b.r[("e", eng)] = me
        for b in writes:
            b.w = me
            b.r = {}
        return me

    def dma(self, q, fn, dsem, reads=(), writes=(), inc=16):
        for b in reads:
            self._wait(q, b.w)
        for b in writes:
            self._wait(q, b.w)
            for ev in b.r.values():
                self._wait(q, ev)
        self.dsem_cnt[dsem] += inc
        me = ("d", dsem, self.dsem_cnt[dsem])
        self.ops[q].append(("dma", fn, dsem, inc))
        for b in reads:
            b.r[("d", dsem)] = me
        for b in writes:
            b.w = me
            b.r = {}
        return me

    def wait_event(self, eng, ev):
        self._wait(eng, ev)

    def _sem_of(self, ev):
        kind, key, val = ev
        if kind == "e":
            return self.esems[key][(val - 1) // SEMCH], (val - 1) % SEMCH + 1
        return self.dsems[key], val

    def emit(self, stack):
        nc = self.nc
        self.esems = {}
        for e in self.CE:
            n = max(1, (self.cnt[e] + SEMCH - 1) // SEMCH)
            self.esems[e] = [stack.enter_context(nc.semaphore(f"s_{e}{i}")) for i in range(n)]
        self.dsems = [stack.enter_context(nc.semaphore(f"d{i}")) for i in range(max(1, len(self.dsem_cnt)))]
        block = stack.enter_context(nc.Block())

        def run(engname):
            def body(eng):
                for o in self.ops[engname]:
                    if o[0] == "wait":
                        s, v = self._sem_of(o[1])
                        eng.wait_ge(s, v)
                    elif o[0] == "op":
                        _, fn, idx = o
                        fn(eng).then_inc(self.esems[engname][(idx - 1) // SEMCH], 1)
                    else:
                        _, fn, dsem, inc = o
                        fn(eng).then_inc(self.dsems[dsem], inc)
            return body

        block.tensor(run("pe"))
        block.scalar(run("act"))
        block.vector(run("dve"))
        block.gpsimd(run("pool"))
        block.sync(run("sp"))


class Ring:
    def __init__(self, S, st, nc, name, n, shape, dtype, with_dsem=True):
        self.tiles = [st.enter_context(nc.sbuf_tensor(f"{name}{i}", shape, dtype)) for i in range(n)]
        self.bufs = [Buf(f"{name}{i}") for i in range(n)]
        self.dsems = [S.new_dsem() for _ in range(n)] if with_dsem else [None] * n
        self.i = 0
        self.n = n

    def next(self):
        k = self.i % self.n
        self.i += 1
        return self.tiles[k], self.bufs[k], self.dsems[k]


def build_nc(n_layers=4):
    nc = bass.Bass("TRN2", target_bir_lowering=False)
    dt_in = lambda n, s, d=F32: nc.dram_tensor(n, s, d, kind="ExternalInput").ap()
    x_d = dt_in("x", [LT, D])
    a_w_in = dt_in("a_w_in", [2, D, 4096])
    a_w_out = dt_in("a_w_out", [2, D, D])
    w_kv = dt_in("w_kv", [D, 2048])
    b_w_in = dt_in("b_w_in", [2, D, 2048])
    b_w_out = dt_in("b_w_out", [2, D, D])
    params_d = dt_in("params", [128, NPAR])
    cmat_d = dt_in("cmat", [128, 4, 128])
    rope_d = dt_in("rope", [2, 128, LT])
    bands_d = dt_in("bands", [2, 128, 2, 2, 896], BF16)
    y_d = nc.dram_tensor("y", [LT, D], F32, kind="ExternalOutput").ap()
    kvin = [[nc.dram_tensor(f"kvin_{l}_{h}", [512, LT], BF16) for h in range(4)] for l in range(3)]
    kvall = [[nc.dram_tensor(f"kvall_{l}_{h}", [1024, LT], BF16) for h in range(4)] for l in range(3)]

    S = Sched(nc)
    with ExitStack() as st:
        sb = lambda n, s, d: st.enter_context(nc.sbuf_tensor(n, s, d))
        xT = sb("xT", [128, 8, LT], F32)
        BxT = [Buf(f"xT{j}") for j in range(NLC)]
        params = sb("params_sb", [128, NPAR], F32)
        Bpar = Buf("params")
        cmat = sb("cmat_sb", [128, 4, 128], F32)
        Bcm = Buf("cmat")
        cbf = sb("cbf", [128, 9, 128], BF16)
        Bcbf = Buf("cbf")
        bands = sb("bands_sb", [128, 2, 2, 896], BF16)
        Bbands = Buf("bands")
        sc = sb("scal", [128, 16], F32)
        Bsc = Buf("scal")
        qT = sb("qT", [128, 8, CHK], BF16)
        BqT = [Buf(f"qT{u}") for u in range(8)]
        gT = sb("gT", [128, 8, CHK], BF16)
        BgT = [Buf(f"gT{u}") for u in range(8)]
        hring = Ring(S, st, nc, "hT", 2, [128, 8, CHK], BF16, with_dsem=False)
        wring = Ring(S, st, nc, "wt", 2, [128, 8, 512], BF16)
        kring = Ring(S, st, nc, "kt", 4, [128, 2, CHK], BF16)
        rring = Ring(S, st, nc, "rp", 2, [128, 2, CHK], F32)
        big32 = Ring(S, st, nc, "big32", 4, [128, 2 * CHK], F32)
        xsring = big32
        f32r = Ring(S, st, nc, "f32t", 4, [128, CHK], F32, with_dsem=False)
        bf16r = Ring(S, st, nc, "bft", 5, [128, CHK], BF16, with_dsem=False)
        ering = Ring(S, st, nc, "et", 4, [128, 2 * CHK], BF16, with_dsem=False)
        ebring = Ring(S, st, nc, "ebt", 5, [128, 2 * CHK], BF16, with_dsem=False)
        Rc = sb("Rc", [128, 2 * CHK], F32)
        BRc = Buf("Rc")
        spring = Ring(S, st, nc, "spt", 3, [128, 2 * CHK], BF16, with_dsem=False)
        wring2 = Ring(S, st, nc, "wbt", 2, [128, 2 * CHK], BF16, with_dsem=False)
        stK = Ring(S, st, nc, "stK", 2, [128, CHK], BF16)
        stV = Ring(S, st, nc, "stV", 2, [128, CHK], BF16)
        stV_d2 = {d: S.new_dsem() for d in stV.dsems}
        ps = [st.enter_context(nc.psum_tensor(f"ps{i}", [128, 2 * CHK], F32)) for i in range(4)]
        Bps = [Buf(f"bank{i}") for i in range(8)]

        def bank(i):
            return ps[i // 2][:, (i % 2) * CHK:(i % 2 + 1) * CHK]

        d_misc = S.new_dsem()
        d_bands = S.new_dsem()
        d_out = S.new_dsem()
        d_ag = [[S.new_dsem() for _ in range(4)] for _ in range(3)]
        Bkvall = [[Buf(f"kvall{l}{h}") for h in range(4)] for l in range(3)]
        kv_store_events = [[[] for _ in range(4)] for _ in range(3)]

        S.dma("sp", lambda e: e.dma_start(out=params[:], in_=params_d), d_misc, writes=[Bpar])
        S.dma("sp", lambda e: e.dma_start(out=cmat[:], in_=cmat_d), d_out, writes=[Bcm])
        S.dma("sp", lambda e: e.dma_start(out=bands[:], in_=bands_d[0]), d_bands, writes=[Bbands])
        S.op("dve", lambda e: e.memset(cbf[:, 0, :], 1.0 / 1024.0), writes=[Bcbf])
        S.op("dve", lambda e: e.memset(cbf[:, 1, :], 0.0), writes=[Bcbf])
        S.op("dve", lambda e: e.memset(cbf[0:64, 1, 0:64], 1.0 / 64.0), writes=[Bcbf])
        S.op("dve", lambda e: e.memset(cbf[64:128, 1, 64:128], 1.0 / 64.0), writes=[Bcbf])
        S.op("dve", lambda e: e.memset(cbf[:, 2, :], 1.0 / 128.0), writes=[Bcbf])
        S.op("dve", lambda e: e.memset(cbf[:, 3, :], 1.0), writes=[Bcbf])
        S.op("dve", lambda e: e.memset(cbf[:, 7, :], -1.0), writes=[Bcbf])
        S.op("dve", lambda e: e.tensor_copy(out=cbf[:, 4:7, :], in_=cmat[:, 1:4, :]), reads=[Bcm], writes=[Bcbf])
        S.op("dve", lambda e: e.tensor_copy(out=cbf[:, 8, :], in_=cmat[:, 0, :]), reads=[Bcm], writes=[Bcbf])
        ident = cmat[:, 0, :]
        C_MEAN, C_BLK, C_M128, C_ONES, C_ROT, C_TINC, C_TLOW, C_NONES, C_IDB = [cbf[:, i, :] for i in range(9)]
        for l in range(2):
            for t in range(2):
                c0 = PC_L + (l * 4 + 2 * t) * 64
                tmp, Btmp, _ = f32r.next()
                S.op("dve", lambda e, tmp=tmp, c0=c0: e.tensor_tensor(out=tmp[:, 0:64], in0=params[:, c0:c0 + 64],
                                                                     in1=params[:, c0 + 64:c0 + 128], op=ALU.mult),
                     reads=[Bpar], writes=[Btmp])
                S.op("dve", lambda e, tmp=tmp, l=l, t=t: e.reduce_sum(out=sc[:, 8 + 2 * l + t:9 + 2 * l + t], in_=tmp[:, 0:64],
                                                                     axis=mybir.AxisListType.X),
                     reads=[Btmp], writes=[Bsc])
            S.op("act", lambda e, l=l: e.activation(out=sc[:, 8 + 2 * l:10 + 2 * l], in_=sc[:, 8 + 2 * l:10 + 2 * l], func=AF.Exp),
                 reads=[Bsc], writes=[Bsc])
            S.op("dve", lambda e, l=l: e.scalar_tensor_tensor(out=sc[:, l:l + 1], in0=sc[:, 9 + 2 * l:10 + 2 * l],
                                                              scalar=-LAMBDA_INIT[l], in1=sc[:, 8 + 2 * l:9 + 2 * l],
                                                              op0=ALU.add, op1=ALU.subtract),
                 reads=[Bsc], writes=[Bsc])
            S.op("dve", lambda e, l=l: e.tensor_scalar(out=sc[:, 2 + l:3 + l], in0=params[:, PC_SUB + l:PC_SUB + l + 1],
                                                       scalar1=1.0 - LAMBDA_INIT[l], scalar2=None, op0=ALU.mult),
                 reads=[Bpar, Bsc], writes=[Bsc])
            S.op("dve", lambda e, l=l: e.tensor_scalar(out=sc[:, 4 + l:5 + l], in0=params[:, PC_QN + l:PC_QN + l + 1],
                                                       scalar1=0.125, scalar2=None, op0=ALU.mult),
                 reads=[Bpar, Bsc], writes=[Bsc])

        for tb in range(LT // 128):
            xs, Bxs, dxs = xsring.next()
            S.dma("sp", lambda e, xs=xs, tb=tb: e.dma_start(out=xs[:], in_=x_d[tb * 128:(tb + 1) * 128, :]), dxs, writes=[Bxs])
            for half in range(2):
                bk = (tb * 2 + half) % 4
                for q in range(4):
                    kc = half * 4 + q
                    S.op("pe", lambda e, bk=bk, q=q, xs=xs, kc=kc: e.transpose(out=bank(bk)[:, q * 128:(q + 1) * 128],
                                                                            in_=xs[:, kc * 128:(kc + 1) * 128], identity=ident),
                         reads=[Bxs, Bcm], writes=[Bps[bk]])
                eng = "dve" if half == 0 else "act"
                dst = xT[:, half * 4:half * 4 + 4, tb * 128:(tb + 1) * 128]
                src = bank(bk).rearrange("p (a b) -> p a b", a=4)
                if eng == "dve":
                    S.op("dve", lambda e, dst=dst, src=src: e.tensor_copy(out=dst, in_=src), reads=[Bps[bk]], writes=[BxT[tb // 4]])
                else:
                    S.op("act", lambda e, dst=dst, src=src: e.activation(out=dst, in_=src, func=AF.Copy), reads=[Bps[bk]], writes=[BxT[tb // 4]])

        def load_w(wmat, col0):
            wt, Bw, dw = wring.next()
            src = wmat[:, col0:col0 + 512].rearrange("(kc p) c -> p kc c", p=128)
            S.dma("pool", lambda e: e.dma_start(out=wt[:], in_=src), dw, writes=[Bw])
            return wt, Bw

        def rms_chunk(j, gcol, dst=None):
            if dst is None:
                hT, Bh1, _ = hring.next()
                Bh = [Bh1]
            else:
                hT, Bh = dst
            cs = slice(j * CHK, (j + 1) * CHK)
            sbk = 7
            for kc in range(8):
                sq, Bsq, _ = bf16r.next()
                if kc % 4 != 3:
                    S.op("act", lambda e, sq=sq, kc=kc: e.activation(out=sq[:], in_=xT[:, kc, cs], func=AF.Square), reads=[BxT[j]], writes=[Bsq])
                else:
                    S.op("pool", lambda e, sq=sq, kc=kc: e.tensor_tensor(out=sq[:], in0=xT[:, kc, cs], in1=xT[:, kc, cs], op=ALU.mult),
                         reads=[BxT[j]], writes=[Bsq])
                S.op("pe", lambda e, sq=sq, kc=kc: e.matmul(bank(sbk), lhsT=C_MEAN, rhs=sq[:], start=(kc == 0), stop=(kc == 7)),
                     reads=[Bsq, Bcbf], writes=[Bps[sbk]])
            rstd, Brs, _ = f32r.next()
            S.op("act", lambda e: e.activation(out=rstd[:], in_=bank(sbk), func=AF.Ln, bias=EPS), reads=[Bps[sbk]], writes=[Brs])
            S.op("act", lambda e: e.activation(out=rstd[:], in_=rstd[:], func=AF.Exp, scale=-0.5), reads=[Brs], writes=[Brs])
            for kc in range(8):
                S.op("dve", lambda e, kc=kc: e.scalar_tensor_tensor(out=hT[:, kc, :], in0=xT[:, kc, cs], scalar=params[:, gcol + kc:gcol + kc + 1],
                                                                 in1=rstd[:], op0=ALU.mult, op1=ALU.mult),
                     reads=[BxT[j], Brs, Bpar], writes=Bh)
            return hT, Bh

        def proj_fm(hT, Bh, wt, Bw, cw, bk):
            for kc in range(8):
                S.op("pe", lambda e, kc=kc: e.matmul(bank(bk), lhsT=wt[:, kc, cw:cw + 128], rhs=hT[:, kc, :], start=(kc == 0), stop=(kc == 7)),
                     reads=[Bw] + Bh, writes=[Bps[bk]])

        def load_rope(j):
            rp, Brp, drp = rring.next()
            S.dma("sp", lambda e: e.dma_start(out=rp[:], in_=rope_d[:, :, j * CHK:(j + 1) * CHK].rearrange("t p n -> p t n")), drp, writes=[Brp])
            return rp, Brp

        def qknorm_rope(bk, gain_ap, Bgain, rp, Brp, dst, Bdst, tb):
            sq, Bsq, _ = bf16r.next()
            S.op("act", lambda e: e.activation(out=sq[:], in_=bank(bk), func=AF.Square), reads=[Bps[bk]], writes=[Bsq])
            S.op("pe", lambda e: e.matmul(bank(tb), lhsT=C_BLK, rhs=sq[:], start=True, stop=True), reads=[Bsq, Bcbf], writes=[Bps[tb]])
            rstd, Brs, _ = f32r.next()
            S.op("act", lambda e: e.activation(out=rstd[:], in_=bank(tb), func=AF.Ln, bias=EPS), reads=[Bps[tb]], writes=[Brs])
            S.op("act", lambda e: e.activation(out=rstd[:], in_=rstd[:], func=AF.Exp, scale=-0.5), reads=[Brs], writes=[Brs])
            tn, Btn, _ = bf16r.next()
            S.op("dve", lambda e: e.scalar_tensor_tensor(out=tn[:], in0=bank(bk), scalar=gain_ap, in1=rstd[:], op0=ALU.mult, op1=ALU.mult),
                 reads=[Bps[bk], Brs, Bgain], writes=[Btn])
            S.op("pe", lambda e: e.matmul(bank(tb), lhsT=C_ROT, rhs=tn[:], start=True, stop=True), reads=[Btn, Bcbf], writes=[Bps[tb]])
            u1, Bu1, _ = f32r.next()
            S.op("pool", lambda e: e.tensor_tensor(out=u1[:], in0=tn[:], in1=rp[:, 0, :], op=ALU.mult), reads=[Btn, Brp], writes=[Bu1])
            u2, Bu2, _ = f32r.next()
            S.op("dve", lambda e: e.tensor_tensor(out=u2[:], in0=bank(tb), in1=rp[:, 1, :], op=ALU.mult), reads=[Bps[tb], Brp], writes=[Bu2])
            S.op("pool", lambda e: e.tensor_tensor(out=dst, in0=u1[:], in1=u2[:], op=ALU.add), reads=[Bu1, Bu2], writes=[Bdst])

        def qk_pipeline(n_units, make_proj, gain_ap, Bgain, rp_fn, dst_fn, hook=None):
            stq = {}

            def P(k):
                make_proj(k, k % 3)

            def N1(k):
                bk, mb = k % 3, 3 + k % 2
                sq, Bsq, _ = bf16r.next()
                S.op("act", lambda e: e.activation(out=sq[:], in_=bank(bk), func=AF.Square), reads=[Bps[bk]], writes=[Bsq])
                S.op("pe", lambda e: e.matmul(bank(mb), lhsT=C_BLK, rhs=sq[:], start=True, stop=True), reads=[Bsq, Bcbf], writes=[Bps[mb]])

            def N2(k):
                bk, mb = k % 3, 3 + k % 2
                rstd, Brs, _ = f32r.next()
                S.op("act", lambda e: e.activation(out=rstd[:], in_=bank(mb), func=AF.Ln, bias=EPS), reads=[Bps[mb]], writes=[Brs])
                S.op("act", lambda e: e.activation(out=rstd[:], in_=rstd[:], func=AF.Exp, scale=-0.5), reads=[Brs], writes=[Brs])
                tn, Btn, _ = bf16r.next()
                S.op("dve", lambda e: e.scalar_tensor_tensor(out=tn[:], in0=bank(bk), scalar=gain_ap, in1=rstd[:], op0=ALU.mult, op1=ALU.mult),
                     reads=[Bps[bk], Brs, Bgain], writes=[Btn])
                stq[k] = (tn, Btn)

            def N3(k):
                rb = 5 + k % 2
                tn, Btn = stq.pop(k)
                rp, Brp = rp_fn(k)
                S.op("pe", lambda e: e.matmul(bank(rb), lhsT=C_ROT, rhs=tn[:], start=True, stop=True), reads=[Btn, Bcbf], writes=[Bps[rb]])
                u1, Bu1, _ = f32r.next()
                S.op("dve", lambda e: e.tensor_tensor(out=u1[:], in0=tn[:], in1=rp[:, 0, :], op=ALU.mult), reads=[Btn, Brp], writes=[Bu1])
                u2, Bu2, _ = f32r.next()
                S.op("dve", lambda e: e.tensor_tensor(out=u2[:], in0=bank(rb), in1=rp[:, 1, :], op=ALU.mult), reads=[Bps[rb], Brp], writes=[Bu2])
                dst, Bdst, post = dst_fn(k)
                S.op("pool", lambda e: e.tensor_tensor(out=dst, in0=u1[:], in1=u2[:], op=ALU.add), reads=[Bu1, Bu2], writes=[Bdst])
                if post is not None:
                    post()

            for t in range(n_units + 3):
                if hook is not None:
                    hook(t)
                if t < n_units:
                    P(t)
                if 1 <= t <= n_units:
                    N1(t - 1)
                if 2 <= t <= n_units + 1:
                    N2(t - 2)
                if t >= 3:
                    N3(t - 3)

        def phase_kv(L, wmat, kcol, vcol, gcol, qk_layer):
            htiles = [(hring.tiles[0], [hring.bufs[0]]), (hring.tiles[1], [hring.bufs[1]]), (qT, list(BqT)), (gT, list(BgT))]
            for j in range(NLC):
                rms_chunk(j, gcol, dst=htiles[j])
            wK = load_w(wmat, kcol)
            wV = load_w(wmat, vcol)
            for half in range(2):
                wt, Bw = wK
                if qk_layer is not None:
                    ropes = {}

                    def mk(k, bk, wt=wt, Bw=Bw, ropes=ropes):
                        jj, uu = k // 4, k % 4
                        if uu == 0:
                            ropes[jj] = load_rope(jj)
                        proj_fm(htiles[jj][0], htiles[jj][1], wt, Bw, uu * 128, bk)

                    def dstf(k, half=half):
                        jj, uu = k // 4, k % 4
                        u = half * 4 + uu
                        kst, Bkst, dk = stK.next()

                        def post():
                            dst = kvin[L][u // 2].ap()[(u % 2) * 128:(u % 2) * 128 + 128, jj * CHK:(jj + 1) * CHK]
                            ev = S.dma("sp", lambda e: e.dma_start(out=dst, in_=kst[:]), dk, reads=[Bkst])
                            kv_store_events[L][u // 2].append(ev)
                        return kst[:], Bkst, post

                    qk_pipeline(16, mk, params[:, PC_KN + qk_layer:PC_KN + qk_layer + 1], Bpar, lambda k, ropes=ropes: ropes[k // 4], dstf)
                else:
                    for jj in range(NLC):
                        for uu in range(4):
                            u = half * 4 + uu
                            bk = uu % 2
                            proj_fm(htiles[jj][0], htiles[jj][1], wt, Bw, uu * 128, bk)
                            kst, Bkst, dk = stK.next()
                            S.op("act", lambda e, kst=kst, bk=bk: e.activation(out=kst[:], in_=bank(bk), func=AF.Copy), reads=[Bps[bk]], writes=[Bkst])
                            dst = kvin[L][u // 2].ap()[(u % 2) * 128:(u % 2) * 128 + 128, jj * CHK:(jj + 1) * CHK]
                            ev = S.dma("sp", lambda e, dst=dst, kst=kst: e.dma_start(out=dst, in_=kst[:]), dk, reads=[Bkst])
                            kv_store_events[L][u // 2].append(ev)
                if half == 0:
                    wK = load_w(wmat, kcol + 512)
                wt, Bw = wV
                for jj in range(NLC):
                    hT, Bh = htiles[jj]
                    for tb in range(4):
                        bk = 4 + tb % 2
                        for kc in range(8):
                            S.op("pe", lambda e, kc=kc, tb=tb, bk=bk, wt=wt, hT=hT: e.matmul(bank(bk), lhsT=hT[:, kc, tb * 128:(tb + 1) * 128], rhs=wt[:, kc, :],
                                                                                   start=(kc == 0), stop=(kc == 7)),
                                 reads=[Bw] + Bh, writes=[Bps[bk]])
                        vst, Bvst, dv = stV.next()
                        S.op("dve", lambda e, vst=vst, bk=bk: e.tensor_copy(out=vst[:], in_=bank(bk)), reads=[Bps[bk]], writes=[Bvst])
                        for pr in range(2):
                            hp = half * 2 + pr
                            dst = kvin[L][hp].ap()[256:512, :].rearrange("(h r) (t e) -> (r t) h e", h=2, e=128)[
                                jj * CHK + tb * 128:jj * CHK + (tb + 1) * 128, :, :]
                            src = vst[:, pr * 256:(pr + 1) * 256].rearrange("p (h e) -> p h e", h=2)
                            ev = S.dma("sp", lambda e, dst=dst, src=src: e.dma_start(out=dst, in_=src), dv if pr == 0 else stV_d2[dv], reads=[Bvst])
                            kv_store_events[L][hp].append(ev)
                if half == 0:
                    wV = load_w(wmat, vcol + 512)
                for hp in (2 * half, 2 * half + 1):
                    mx = {}
                    for ev in kv_store_events[L][hp]:
                        mx[ev[1]] = max(mx.get(ev[1], 0), ev[2])
                    for k_, v_ in mx.items():
                        S.wait_event("pool", ("d", k_, v_))
                    S.dma("pool", lambda e, hp=hp: e.collective_compute("AllGather", ALU.bypass, replica_groups=RG,
                                                                         ins=[kvin[L][hp].ap().opt()], outs=[kvall[L][hp].ap().opt()]),
                          d_ag[L][hp], writes=[Bkvall[L][hp]], inc=1)

        def load_kv(L, u, gk):
            kt, Bkt, dkt = kring.next()
            rho, jj = OWNER[gk]
            base = kvall[L][u // 2].ap()
            ksrc = base[rho * 512 + (u % 2) * 128:rho * 512 + (u % 2) * 128 + 128, jj * CHK:(jj + 1) * CHK]
            vsrc = base[rho * 512 + 256 + (u % 2) * 128:rho * 512 + 256 + (u % 2) * 128 + 128, :].rearrange(
                "r (t e) -> (r t) e", e=128)[jj * CHK:(jj + 1) * CHK, :].rearrange("(kb r) e -> r kb e", r=128)
            S.dma("sp", lambda e: e.dma_start(out=kt[:, 0, :], in_=ksrc), dkt, reads=[Bkvall[L][u // 2]], writes=[Bkt])
            S.dma("sp", lambda e: e.dma_start(out=kt[:, 1, :].rearrange("p (kb e) -> p kb e", e=128), in_=vsrc), dkt,
                  reads=[Bkvall[L][u // 2]], writes=[Bkt])
            return kt, Bkt

        def band_for(kind, j, gk):
            if gk == MJ[j]:
                ts = 0
            elif gk == MJ[j] - 1:
                ts = 1
            else:
                return None
            return lambda i: bands[:, j % 2, ts, 384 - 128 * i:896 - 128 * i]

        def out_proj(j, wmat, pre=None):
            cs = slice(j * CHK, (j + 1) * CHK)
            for half in range(2):
                wt, Bw = pre[half] if pre is not None else load_w(wmat, half * 512)
                for q in range(4):
                    ncn = half * 4 + q
                    bk = q % 2
                    for u in range(8):
                        S.op("pe", lambda e, u=u, q=q, bk=bk, wt=wt: e.matmul(bank(bk), lhsT=wt[:, u, q * 128:(q + 1) * 128], rhs=gT[:, u, :],
                                                                           start=(u == 0), stop=(u == 7)),
                             reads=[Bw, BgT[u]], writes=[Bps[bk]])
                    S.op("dve", lambda e, ncn=ncn, bk=bk: e.tensor_tensor(out=xT[:, ncn, cs], in0=bank(bk), in1=xT[:, ncn, cs], op=ALU.add),
                         reads=[Bps[bk]], writes=[BxT[j]])

        def diff_layer(l):
            wmat = a_w_in[l]
            phase_kv(l, wmat, 1024, 2048, PC_AN + 8 * l, l)
            for j in range(NLC):
                wts = [load_w(wmat, half * 512) for half in range(2)]
                hT, Bh = rms_chunk(j, PC_AN + 8 * l)
                rp, Brp = load_rope(j)
                wg = {}

                def mkq(k, bk, wts=wts, hT=hT, Bh=Bh):
                    proj_fm(hT, Bh, wts[k // 4][0], wts[k // 4][1], (k % 4) * 128, bk)

                def hookq(t, wg=wg):
                    if t == 5:
                        wg[0] = load_w(wmat, 3072)
                    if t == 9:
                        wg[1] = load_w(wmat, 3072 + 512)

                qk_pipeline(8, mkq, sc[:, 4 + l:5 + l], Bsc, lambda k: (rp, Brp), lambda k: (qT[:, k, :], BqT[k], None), hook=hookq)
                for half in range(2):
                    wt, Bw = wg[half]
                    for uu in range(4):
                        u = half * 4 + uu
                        bk = uu % 2
                        proj_fm(hT, Bh, wt, Bw, uu * 128, bk)
                        S.op("act", lambda e, u=u, bk=bk: e.activation(out=gT[:, u, :], in_=bank(bk), func=AF.Silu), reads=[Bps[bk]], writes=[BgT[u]])
                wo = [load_w(a_w_out[l], half * 512) for half in range(2)]
                nblk = (MJ[j] + 1) * 4
                items = [(u, gk, i) for u in range(8) for gk in range(MJ[j] + 1) for i in range(4)]
                stt = {}

                def stage_a(idx):
                    u, gk, i = items[idx]
                    n = idx % nblk
                    if i == 0:
                        stt[("kv", u, gk)] = load_kv(l, u, gk)
                    kt, Bkt = stt[("kv", u, gk)]
                    if n == 0:
                        stt[("es", u)] = big32.next()
                    es, Bes, _ = stt[("es", u)]
                    sp_ = idx % 2
                    b0, b1 = 2 * sp_, 2 * sp_ + 1
                    ksl = slice(i * 128, (i + 1) * 128)
                    bf = band_for(0, j, gk)
                    nomask = bf is None
                    S.op("pe", lambda e: e.matmul(bank(b0), lhsT=kt[0:64, 0, ksl], rhs=qT[0:64, u, :], start=True, stop=nomask),
                         reads=[Bkt, BqT[u]], writes=[Bps[b0]])
                    S.op("pe", lambda e: e.matmul(bank(b1), lhsT=kt[64:128, 0, ksl], rhs=qT[64:128, u, :], start=True, stop=nomask),
                         reads=[Bkt, BqT[u]], writes=[Bps[b1]])
                    if not nomask:
                        m = bf(i)
                        for bb in (b0, b1):
                            S.op("pe", lambda e, bb=bb: e.matmul(bank(bb), lhsT=C_IDB, rhs=m, start=False, stop=True),
                                 reads=[Bcbf, Bbands], writes=[Bps[bb]])
                    et, Bet, _ = ering.next()
                    S.op("act", lambda e: e.activation(out=et[:], in_=ps[sp_][:], func=AF.Exp), reads=[Bps[b0], Bps[b1]], writes=[Bet])
                    if n == 0:
                        S.op("dve", lambda e: e.tensor_copy(out=es[:, 0:CHK], in_=et[:, 0:CHK]), reads=[Bet], writes=[Bes])
                    else:
                        S.op("dve", lambda e: e.tensor_tensor(out=es[:, 0:CHK], in0=es[:, 0:CHK], in1=et[:, 0:CHK], op=ALU.add), reads=[Bet, Bes], writes=[Bes])
                    stt[("et", idx)] = (et, Bet)

                def stage_b(idx):
                    u, gk, i = items[idx]
                    n = idx % nblk
                    kt, Bkt = stt[("kv", u, gk)]
                    et, Bet = stt.pop(("et", idx))
                    first, last = (n == 0), (n == nblk - 1)
                    vblk = kt[:, 1, i * 128:(i + 1) * 128]
                    for hh in range(2):
                        S.op("pe", lambda e, hh=hh: e.matmul(bank(4 + hh), lhsT=vblk, rhs=et[:, hh * CHK:(hh + 1) * CHK], start=first, stop=last),
                             reads=[Bkt, Bet], writes=[Bps[4 + hh]])
                    S.op("pe", lambda e: e.matmul(bank(7), lhsT=C_ONES, rhs=et[:, CHK:2 * CHK], start=first, stop=last),
                         reads=[Bcbf, Bet], writes=[Bps[7]])
                    if last:
                        while pending:
                            pending.pop(0)()
                        pending.extend(epilogue_steps(u))
                        pending.pop(0)()

                def epilogue_steps(u):
                    es, Bes, _ = stt.pop(("es", u))
                    o12, Bo12, _ = big32.next()
                    esb, Besb, _ = bf16r.next()
                    t1, Bt1, _ = f32r.next()
                    t2, Bt2, _ = f32r.next()
                    r1, Br1, _ = f32r.next()
                    sq, Bsq, _ = bf16r.next()

                    def e0():
                        S.op("act", lambda e: e.activation(out=o12[:], in_=ps[2][:], func=AF.Copy), reads=[Bps[4], Bps[5]], writes=[Bo12])
                        S.op("dve", lambda e: e.tensor_copy(out=es[:, CHK:2 * CHK], in_=bank(7)), reads=[Bps[7], Bes], writes=[Bes])
                        S.op("dve", lambda e: e.tensor_copy(out=esb[:], in_=es[:, 0:CHK]), reads=[Bes], writes=[Besb])

                    def e1():
                        S.op("pe", lambda e: e.matmul(bank(6), lhsT=C_ONES, rhs=esb[:], start=True, stop=True), reads=[Bcbf, Besb], writes=[Bps[6]])

                    def e2():
                        S.op("dve", lambda e: e.tensor_tensor(out=t1[:], in0=o12[:, 0:CHK], in1=es[:, CHK:2 * CHK], op=ALU.mult), reads=[Bes, Bo12], writes=[Bt1])
                        S.op("dve", lambda e: e.scalar_tensor_tensor(out=t2[:], in0=o12[:, CHK:2 * CHK], scalar=sc[:, l:l + 1], in1=bank(6),
                                                                    op0=ALU.mult, op1=ALU.mult),
                             reads=[Bo12, Bps[6], Bsc], writes=[Bt2])
                        S.op("dve", lambda e: e.tensor_tensor(out=es[:, 0:CHK], in0=bank(6), in1=es[:, CHK:2 * CHK], op=ALU.mult), reads=[Bps[6], Bes], writes=[Bes])
                        S.op("dve", lambda e: e.tensor_tensor(out=es[:, 0:CHK], in0=es[:, 0:CHK], in1=es[:, 0:CHK], op=ALU.mult), reads=[Bes], writes=[Bes])

                    def e3():
                        S.op("pool", lambda e: e.tensor_tensor(out=t1[:], in0=t1[:], in1=t2[:], op=ALU.add), reads=[Bt1, Bt2], writes=[Bt1])
                        S.op("pool", lambda e: e.tensor_tensor(out=sq[:], in0=t1[:], in1=t1[:], op=ALU.mult), reads=[Bt1], writes=[Bsq])

                    def enop():
                        pass

                    def e4():
                        S.op("pe", lambda e: e.matmul(bank(6), lhsT=C_M128, rhs=sq[:], start=True, stop=True), reads=[Bsq, Bcbf], writes=[Bps[6]])
                        S.op("dve", lambda e: e.scalar_tensor_tensor(out=r1[:], in0=es[:, 0:CHK], scalar=EPS, in1=bank(6), op0=ALU.mult, op1=ALU.add),
                             reads=[Bes, Bps[6]], writes=[Br1])
                        S.op("act", lambda e: e.activation(out=r1[:], in_=r1[:], func=AF.Ln), reads=[Br1], writes=[Br1])

                    def e5():
                        S.op("act", lambda e: e.activation(out=r1[:], in_=r1[:], func=AF.Exp, scale=-0.5), reads=[Br1], writes=[Br1])
                        S.op("dve", lambda e: e.scalar_tensor_tensor(out=t1[:], in0=t1[:], scalar=sc[:, 2 + l:3 + l], in1=r1[:], op0=ALU.mult, op1=ALU.mult),
                             reads=[Bt1, Br1, Bsc], writes=[Bt1])
                        S.op("pool", lambda e: e.tensor_tensor(out=gT[:, u, :], in0=t1[:], in1=gT[:, u, :], op=ALU.mult), reads=[Bt1], writes=[BgT[u]])

                    return [e0, e1, e2, e3, enop, e4, e5]

                pending = []
                for t in range(len(items) + 2):
                    if t < len(items):
                        stage_a(t)
                    if t >= 2:
                        stage_b(t - 2)
                    if pending and (t % nblk) >= 3:
                        pending.pop(0)()
                while pending:
                    pending.pop(0)()
                out_proj(j, a_w_out[l], pre=wo)

        def sb_layer(jl):
            wmat = b_w_in[jl]
            for j in range(NLC):
                wq = [load_w(wmat, half * 512) for half in range(2)]
                hT, Bh = rms_chunk(j, PC_BN + 8 * jl)
                wg = {}
                for half in range(2):
                    wt, Bw = wq[half]
                    for uu in range(4):
                        u = half * 4 + uu
                        bk = uu % 2
                        proj_fm(hT, Bh, wt, Bw, uu * 128, bk)
                        S.op("dve", lambda e, u=u, bk=bk: e.tensor_copy(out=qT[:, u, :], in_=bank(bk)), reads=[Bps[bk]], writes=[BqT[u]])
                    wg[half] = load_w(wmat, 1024 + half * 512)
                for half in range(2):
                    wt, Bw = wg[half]
                    for uu in range(4):
                        u = half * 4 + uu
                        bk = uu % 2
                        proj_fm(hT, Bh, wt, Bw, uu * 128, bk)
                        S.op("act", lambda e, u=u, bk=bk: e.activation(out=gT[:, u, :], in_=bank(bk), func=AF.Silu), reads=[Bps[bk]], writes=[BgT[u]])
                wo = [load_w(b_w_out[jl], half * 512) for half in range(2)]
                nblk = (MJ[j] + 1) * 4
                items = [(u, gk, i) for u in range(8) for gk in range(MJ[j], -1, -1) for i in range(3, -1, -1)]
                stt = {}

                def s1(idx):
                    u, gk, i = items[idx]
                    if i == 3:
                        stt[("kv", u, gk)] = load_kv(2, u, gk)
                    kt, Bkt = stt[("kv", u, gk)]
                    ksl = slice(i * 128, (i + 1) * 128)
                    bf = band_for(1, j, gk)
                    nomask = bf is None
                    S.op("pe", lambda e: e.matmul(bank(0), lhsT=kt[0:64, 0, ksl], rhs=qT[0:64, u, :], start=True, stop=nomask),
                         reads=[Bkt, BqT[u]], writes=[Bps[0]])
                    S.op("pe", lambda e: e.matmul(bank(1), lhsT=kt[64:128, 0, ksl], rhs=qT[64:128, u, :], start=True, stop=nomask),
                         reads=[Bkt, BqT[u]], writes=[Bps[1]])
                    if not nomask:
                        m = bf(i)
                        for bb in (0, 1):
                            S.op("pe", lambda e, bb=bb: e.matmul(bank(bb), lhsT=C_IDB, rhs=m, start=False, stop=True),
                                 reads=[Bcbf, Bbands], writes=[Bps[bb]])
                    eb, Beb, _ = ebring.next()
                    S.op("act", lambda e: e.activation(out=eb[:], in_=ps[0][:], func=AF.Exp, scale=0.125), reads=[Bps[0], Bps[1]], writes=[Beb])
                    stt[("s1a", idx)] = (eb, Beb)

                def s1b(idx):
                    eb, Beb = stt.pop(("s1a", idx))
                    spt, Bspt, _ = spring.next()
                    S.op("act", lambda e: e.activation(out=spt[:], in_=eb[:], func=AF.Ln, bias=1.0), reads=[Beb], writes=[Bspt])
                    stt[("s1", idx)] = (eb, Beb, spt, Bspt)

                def s2(idx):
                    u, gk, i = items[idx]
                    n = idx % nblk
                    eb, Beb, spt, Bspt = stt.pop(("s1", idx))
                    for hh in range(2):
                        S.op("pe", lambda e, hh=hh: e.matmul(bank(2 + hh), lhsT=C_TINC, rhs=spt[:, hh * CHK:(hh + 1) * CHK], start=True, stop=True),
                             reads=[Bcbf, Bspt], writes=[Bps[2 + hh]])
                    tt, Btt, _ = big32.next()
                    if n == 0:
                        S.op("dve", lambda e: e.tensor_copy(out=tt[:], in_=ps[1][:]), reads=[Bps[2], Bps[3]], writes=[Btt])
                    else:
                        S.op("dve", lambda e: e.tensor_tensor(out=tt[:], in0=ps[1][:], in1=Rc[:], op=ALU.add), reads=[Bps[2], Bps[3], BRc], writes=[Btt])
                    if n != nblk - 1:
                        for hh in range(2):
                            S.op("pe", lambda e, hh=hh: e.matmul(bank(4 + hh), lhsT=C_NONES, rhs=spt[:, hh * CHK:(hh + 1) * CHK], start=True, stop=True),
                                 reads=[Bcbf, Bspt], writes=[Bps[4 + hh]])
                        if n == 0:
                            S.op("dve", lambda e: e.tensor_copy(out=Rc[:], in_=ps[2][:]), reads=[Bps[4], Bps[5]], writes=[BRc])
                        else:
                            S.op("dve", lambda e: e.tensor_tensor(out=Rc[:], in0=ps[2][:], in1=Rc[:], op=ALU.add), reads=[Bps[4], Bps[5], BRc], writes=[BRc])
                    stt[("s2", idx)] = (eb, Beb, tt, Btt)

                def s2b(idx):
                    eb, Beb, tt, Btt = stt.pop(("s2", idx))
                    wb, Bwb, _ = wring2.next()
                    S.op("act", lambda e: e.activation(out=wb[:], in_=tt[:], func=AF.Exp), reads=[Btt], writes=[Bwb])
                    at, Bat, _ = ering.next()
                    S.op("pool", lambda e: e.tensor_tensor(out=at[:], in0=eb[:], in1=wb[:], op=ALU.mult), reads=[Beb, Bwb], writes=[Bat])
                    stt[("at", idx)] = (at, Bat)

                def s3(idx):
                    u, gk, i = items[idx]
                    n = idx % nblk
                    kt, Bkt = stt[("kv", u, gk)]
                    at, Bat = stt.pop(("at", idx))
                    first, last = (n == 0), (n == nblk - 1)
                    for hh in range(2):
                        S.op("pe", lambda e, hh=hh: e.matmul(bank(6)[hh * 64:(hh + 1) * 64, :], lhsT=kt[:, 1, i * 128 + hh * 64:i * 128 + (hh + 1) * 64],
                                                            rhs=at[:, hh * CHK:(hh + 1) * CHK], start=first, stop=last, tile_position=(0, hh * 64)),
                             reads=[Bkt, Bat], writes=[Bps[6]])
                    if last:
                        S.op("dve", lambda e: e.tensor_tensor(out=gT[:, u, :], in0=bank(6), in1=gT[:, u, :], op=ALU.mult), reads=[Bps[6]], writes=[BgT[u]])

                NI = len(items)
                for t in range(NI + 4):
                    if t < NI:
                        s1(t)
                    if 3 <= t <= NI + 2:
                        s2b(t - 3)
                    if 2 <= t <= NI + 1:
                        s2(t - 2)
                    if t < NI:
                        s1b(t)
                    if t >= 4:
                        s3(t - 4)
                out_proj(j, b_w_out[jl], pre=wo)

        if n_layers >= 1:
            diff_layer(0)
        if n_layers >= 2:
            diff_layer(1)
        if n_layers >= 3:
            S.dma("sp", lambda e: e.dma_start(out=bands[:], in_=bands_d[1]), d_bands, writes=[Bbands])
            phase_kv(2, w_kv, 0, 1024, PC_KVN, None)
            sb_layer(0)
        if n_layers >= 4:
            sb_layer(1)

        for tb in range(LT // 128):
            xs, Bxs, dxs = xsring.next()
            for half in range(2):
                bk = (tb * 2 + half) % 4
                for q in range(4):
                    kc = half * 4 + q
                    S.op("pe", lambda e, bk=bk, q=q, kc=kc, tb=tb: e.transpose(out=bank(bk)[:, q * 128:(q + 1) * 128],
                                                                            in_=xT[:, kc, tb * 128:(tb + 1) * 128], identity=ident),
                         reads=[BxT[tb // 4], Bcm], writes=[Bps[bk]])
                dst = xs[:, half * 512:(half + 1) * 512]
                if half == 0:
                    S.op("dve", lambda e, dst=dst, bk=bk: e.tensor_copy(out=dst, in_=bank(bk)), reads=[Bps[bk]], writes=[Bxs])
                else:
                    S.op("act", lambda e, dst=dst, bk=bk: e.activation(out=dst, in_=bank(bk), func=AF.Copy), reads=[Bps[bk]], writes=[Bxs])
            S.dma("sp", lambda e, xs=xs, tb=tb: e.dma_start(out=y_d[tb * 128:(tb + 1) * 128, :], in_=xs[:]), dxs, reads=[Bxs])
        for dd in xsring.dsems:
            S.wait_event("sp", ("d", dd, S.dsem_cnt[dd]))
        S.emit(st)
    return nc


def _host_constants():
    ident = np.eye(128, dtype=np.float32)
    rot = np.zeros((128, 128), np.float32)
    for c in range(2):
        for d in range(8):
            rot[c * 64 + d + 8, c * 64 + d] = -1.0
            rot[c * 64 + d, c * 64 + d + 8] = 1.0
    jj, ss = np.meshgrid(np.arange(128), np.arange(128), indexing="ij")
    tinc = -(jj >= ss).astype(np.float32)
    tlow = -(jj < ss).astype(np.float32)
    cmat = np.stack([ident, rot, tinc, tlow], axis=1)
    return np.ascontiguousarray(cmat)


def _rope_tables(p):
    half = 8
    inv = np.power(np.float32(500000.0), -np.arange(half, dtype=np.float32) / np.float32(half)).astype(np.float32)
    pos = np.concatenate([np.arange(g * CHK, (g + 1) * CHK) for g in GCH[p]]).astype(np.float32)
    ang = pos[None, :] * inv[:, None]
    cos, sin = np.cos(ang).astype(np.float32), np.sin(ang).astype(np.float32)
    C = np.ones((128, LT), np.float32)
    Sn = np.zeros((128, LT), np.float32)
    for c in range(2):
        for d in range(16):
            C[c * 64 + d] = cos[d % 8]
            Sn[c * 64 + d] = sin[d % 8]
    return np.ascontiguousarray(np.stack([C, Sn], 0))


def _bands(p):
    r = np.arange(128)[:, None]
    u = np.arange(896)[None, :]
    diag = [(r <= u - 384), (r < u - 384)]
    out = np.zeros((2, 128, 2, 2, 896), np.float32)
    for kind in range(2):
        for jpar in range(2):
            high = ((jpar + p) % 2 == 1)
            if high:
                out[kind, :, jpar, 0] = diag[kind]
                out[kind, :, jpar, 1] = 1.0
            else:
                out[kind, :, jpar, 0] = 0.0
                out[kind, :, jpar, 1] = diag[kind]
    out = (out - 1.0) * 30000.0
    return out.astype(ml_dtypes.bfloat16)


def _params(inp):
    P = np.zeros((128, NPAR), np.float32)
    f = lambda v: np.asarray(v, np.float32)
    for l in range(2):
        P[:, PC_AN + 8 * l:PC_AN + 8 * l + 8] = f(inp["a_norm"])[l].reshape(8, 128).T
        P[:, PC_BN + 8 * l:PC_BN + 8 * l + 8] = f(inp["b_norm"])[l].reshape(8, 128).T
        P[:, PC_QN + l] = np.tile(f(inp["a_q_norm"])[l], 2)
        P[:, PC_KN + l] = np.tile(f(inp["a_k_norm"])[l], 2)
        P[:, PC_SUB + l] = f(inp["a_subln"])[l]
        for t, nm in enumerate(["a_lq1", "a_lk1", "a_lq2", "a_lk2"]):
            c0 = PC_L + (l * 4 + t) * 64
            P[:, c0:c0 + 64] = f(inp[nm])[l][None, :]
    P[:, PC_KVN:PC_KVN + 8] = f(inp["kv_norm"]).reshape(8, 128).T
    return P


_NC_CACHE = {}


def _run(inp, n_layers=4):
    x = np.asarray(inp["x"], np.float32)
    if n_layers not in _NC_CACHE:
        _NC_CACHE[n_layers] = build_nc(n_layers)
    nc = _NC_CACHE[n_layers]
    cmat = _host_constants()
    params = _params(inp)
    shared = {k: np.ascontiguousarray(np.asarray(inp[k], np.float32)) for k in ("a_w_in", "a_w_out", "w_kv", "b_w_in", "b_w_out")}
    in_maps = []
    for c in range(8):
        b, p = c // 2, c % 2
        xs = np.concatenate([x[b, g * CHK:(g + 1) * CHK] for g in GCH[p]], 0)
        m = {"x": np.ascontiguousarray(xs), "params": params, "cmat": cmat, "rope": _rope_tables(p), "bands": _bands(p)}
        m.update(shared)
        in_maps.append(m)
    res = run_bass_kernel_spmd(nc, in_maps, core_ids=list(range(8)))
    out = np.zeros((NB, SEQ, D), np.float32)
    for c in range(8):
        b, p = c // 2, c % 2
        y = res.results[c]["y"]
        for j, g in enumerate(GCH[p]):
            out[b, g * CHK:(g + 1) * CHK] = y[j * CHK:(j + 1) * CHK]
    return out


def kernel(**inputs):
    return _run(inputs, 4)
```

```python
from contextlib import ExitStack
import math
import numpy as np
import ml_dtypes
import concourse.bass as bass
import concourse.mybir as mybir
from concourse.bass_utils import run_bass_kernel_spmd

F32 = mybir.dt.float32
BF16 = mybir.dt.bfloat16
AF = mybir.ActivationFunctionType
ALU = mybir.AluOpType

D = 1024
SEQ = 4096
NB = 4
CHK = 512
NLC = 4
LT = NLC * CHK
GCH = [[0, 3, 4, 7], [1, 2, 5, 6]]
OWNER = {}
for _p in range(2):
    for _j, _g in enumerate(GCH[_p]):
        OWNER[_g] = (_p, _j)
MJ = [max(GCH[0][j], GCH[1][j]) for j in range(NLC)]
EPS = 1e-6
LAMBDA_INIT = [0.8 - 0.6 * math.exp(-0.3 * l) for l in range(2)]
RG = [[0, 1], [2, 3], [4, 5], [6, 7]]
SEMCH = 2048

PC_AN, PC_KVN, PC_BN, PC_QN, PC_KN, PC_SUB, PC_L = 0, 16, 24, 40, 42, 44, 46
NPAR = PC_L + 2 * 4 * 64


class Buf:
    __slots__ = ("name", "w", "r")

    def __init__(self, name):
        self.name = name
        self.w = None
        self.r = {}


class Sched:
    CE = ("pe", "act", "dve", "pool")
    QE = ("pe", "act", "dve", "pool", "sp")

    def __init__(self, nc):
        self.nc = nc
        self.ops = {e: [] for e in self.QE}
        self.cnt = {e: 0 for e in self.CE}
        self.known = {e: {} for e in self.QE}
        self.dsem_cnt = []
        self.esems = None
        self.dsems = None

    def _wait(self, eng, ev):
        if ev is None:
            return
        kind, key, val = ev
        if kind == "e" and key == eng and eng == "pe":
            return
        k = (kind, key)
        if self.known[eng].get(k, 0) >= val:
            return
        self.known[eng][k] = val
        self.ops[eng].append(("wait", ev))

    def new_dsem(self):
        self.dsem_cnt.append(0)
        return len(self.dsem_cnt) - 1

    def op(self, eng, fn, reads=(), writes=()):
        for b in reads:
            self._wait(eng, b.w)
        for b in writes:
            self._wait(eng, b.w)
            for ev in b.r.values():
                if ev[0] == "e" and ev[1] == eng and ev[2] == self.cnt[eng] and False:
                    continue
                self._wait(eng, ev)
        self.cnt[eng] += 1
        idx = self.cnt[eng]
        me = ("e", eng, idx)
        self.ops[eng].append(("op", fn, idx))
        for b in reads:
            b.r[("e", eng)] = me
        for b in writes:
            b.w = me
            b.r = {}
        return me

    def dma(self, q, fn, dsem, reads=(), writes=(), inc=16):
        for b in reads:
            self._wait(q, b.w)
        for b in writes:
            self._wait(q, b.w)
            for ev in b.r.values():
                self._wait(q, ev)
        self.dsem_cnt[dsem] += inc
        me = ("d", dsem, self.dsem_cnt[dsem])
        self.ops[q].append(("dma", fn, dsem, inc))
        for b in reads:
            b.r[("d", dsem)] = me
        for b in writes:
            b.w = me
            b.r = {}
        return me

    def wait_event(self, eng, ev):
        self._wait(eng, ev)

    def _sem_of(self, ev):
        kind, key, val = ev
        if kind == "e":
            return self.esems[key][(val - 1) // SEMCH], (val - 1) % SEMCH + 1
        return self.dsems[key], val

    def emit(self, stack):
        nc = self.nc
        self.esems = {}
        for e in self.CE:
            n = max(1, (self.cnt[e] + SEMCH - 1) // SEMCH)
            self.esems[e] = [stack.enter_context(nc.semaphore(f"s_{e}{i}")) for i in range(n)]
        self.dsems = [stack.enter_context(nc.semaphore(f"d{i}")) for i in range(max(1, len(self.dsem_cnt)))]
        block = stack.enter_context(nc.Block())

        def run(engname):
            def body(eng):
                for o in self.ops[engname]:
                    if o[0] == "wait":
                        s, v = self._sem_of(o[1])
                        eng.wait_ge(s, v)
                    elif o[0] == "op":
                        _, fn, idx = o
                        fn(eng).then_inc(self.esems[engname][(idx - 1) // SEMCH], 1)
                    else:
                        _, fn, dsem, inc = o
                        fn(eng).then_inc(self.dsems[dsem], inc)
            return body

        block.tensor(run("pe"))
        block.scalar(run("act"))
        block.vector(run("dve"))
        block.gpsimd(run("pool"))
        block.sync(run("sp"))


class Ring:
    def __init__(self, S, st, nc, name, n, shape, dtype, with_dsem=True):
        self.tiles = [st.enter_context(nc.sbuf_tensor(f"{name}{i}", shape, dtype)) for i in range(n)]
        self.bufs = [Buf(f"{name}{i}") for i in range(n)]
        self.dsems = [S.new_dsem() for _ in range(n)] if with_dsem else [None] * n
        self.i = 0
        self.n = n

    def next(self):
        k = self.i % self.n
        self.i += 1
        return self.tiles[k], self.bufs[k], self.dsems[k]


def build_nc(n_layers=4):
    nc = bass.Bass("TRN2", target_bir_lowering=False)
    dt_in = lambda n, s, d=F32: nc.dram_tensor(n, s, d, kind="ExternalInput").ap()
    x_d = dt_in("x", [LT, D])
    a_w_in = dt_in("a_w_in", [2, D, 4096])
    a_w_out = dt_in("a_w_out", [2, D, D])
    w_kv = dt_in("w_kv", [D, 2048])
    b_w_in = dt_in("b_w_in", [2, D, 2048])
    b_w_out = dt_in("b_w_out", [2, D, D])
    params_d = dt_in("params", [128, NPAR])
    cmat_d = dt_in("cmat", [128, 4, 128])
    rope_d = dt_in("rope", [2, 128, LT])
    bands_d = dt_in("bands", [2, 128, 2, 2, 896], BF16)
    y_d = nc.dram_tensor("y", [LT, D], F32, kind="ExternalOutput").ap()
    kvin = [[nc.dram_tensor(f"kvin_{l}_{h}", [512, LT], BF16) for h in range(4)] for l in range(3)]
    kvall = [[nc.dram_tensor(f"kvall_{l}_{h}", [1024, LT], BF16) for h in range(4)] for l in range(3)]

    S = Sched(nc)
    with ExitStack() as st:
        sb = lambda n, s, d: st.enter_context(nc.sbuf_tensor(n, s, d))
        xT = sb("xT", [128, 8, LT], F32)
        BxT = [Buf(f"xT{j}") for j in range(NLC)]
        params = sb("params_sb", [128, NPAR], F32)
        Bpar = Buf("params")
        cmat = sb("cmat_sb", [128, 4, 128], F32)
        Bcm = Buf("cmat")
        cbf = sb("cbf", [128, 10, 128], BF16)
        Bcbf = Buf("cbf")
        bands = sb("bands_sb", [128, 2, 2, 896], BF16)
        Bbands = Buf("bands")
        sc = sb("scal", [128, 16], F32)
        Bsc = Buf("scal")
        qT = sb("qT", [128, 8, CHK], BF16)
        BqT = [Buf(f"qT{u}") for u in range(8)]
        gT = sb("gT", [128, 8, CHK], BF16)
        BgT = [Buf(f"gT{u}") for u in range(8)]
        hring = Ring(S, st, nc, "hT", 2, [128, 8, CHK], BF16, with_dsem=False)
        wring = Ring(S, st, nc, "wt", 2, [128, 8, 512], BF16)
        kring = Ring(S, st, nc, "kt", 4, [128, 2, CHK], BF16)
        rring = Ring(S, st, nc, "rp", 2, [128, 2, CHK], F32)
        big32 = Ring(S, st, nc, "big32", 4, [128, 2 * CHK], F32)
        xsring = big32
        f32r = Ring(S, st, nc, "f32t", 4, [128, CHK], F32, with_dsem=False)
        bf16r = Ring(S, st, nc, "bft", 5, [128, CHK], BF16, with_dsem=False)
        ering = Ring(S, st, nc, "et", 4, [128, 2 * CHK], BF16, with_dsem=False)
        ebring = Ring(S, st, nc, "ebt", 5, [128, 2 * CHK], BF16, with_dsem=False)
        Rc = sb("Rc", [128, 2 * CHK], F32)
        BRc = Buf("Rc")
        spring = Ring(S, st, nc, "spt", 3, [128, 2 * CHK], BF16, with_dsem=False)
        wring2 = Ring(S, st, nc, "wbt", 2, [128, 2 * CHK], BF16, with_dsem=False)
        stK = Ring(S, st, nc, "stK", 2, [128, CHK], BF16)
        stV = Ring(S, st, nc, "stV", 2, [128, CHK], BF16)
        stV_d2 = {d: S.new_dsem() for d in stV.dsems}
        ps = [st.enter_context(nc.psum_tensor(f"ps{i}", [128, 2 * CHK], F32)) for i in range(4)]
        Bps = [Buf(f"bank{i}") for i in range(8)]

        def bank(i):
            return ps[i // 2][:, (i % 2) * CHK:(i % 2 + 1) * CHK]

        d_misc = S.new_dsem()
        d_bands = S.new_dsem()
        d_out = S.new_dsem()
        d_ag = [[S.new_dsem() for _ in range(4)] for _ in range(3)]
        Bkvall = [[Buf(f"kvall{l}{h}") for h in range(4)] for l in range(3)]
        kv_store_events = [[[] for _ in range(4)] for _ in range(3)]

        S.dma("sp", lambda e: e.dma_start(out=params[:], in_=params_d), d_misc, writes=[Bpar])
        S.dma("sp", lambda e: e.dma_start(out=cmat[:], in_=cmat_d), d_out, writes=[Bcm])
        S.dma("sp", lambda e: e.dma_start(out=bands[:], in_=bands_d[0]), d_bands, writes=[Bbands])
        S.op("dve", lambda e: e.memset(cbf[:, 0, :], 1.0 / 1024.0), writes=[Bcbf])
        S.op("dve", lambda e: e.memset(cbf[:, 1, :], 0.0), writes=[Bcbf])
        S.op("dve", lambda e: e.memset(cbf[0:64, 1, 0:64], 1.0 / 64.0), writes=[Bcbf])
        S.op("dve", lambda e: e.memset(cbf[64:128, 1, 64:128], 1.0 / 64.0), writes=[Bcbf])
        S.op("dve", lambda e: e.memset(cbf[:, 2, :], 1.0 / 128.0), writes=[Bcbf])
        S.op("dve", lambda e: e.memset(cbf[:, 3, :], 1.0), writes=[Bcbf])
        S.op("dve", lambda e: e.memset(cbf[:, 7, :], -1.0), writes=[Bcbf])
        S.op("dve", lambda e: e.memset(cbf[:, 9, :], 0.0), writes=[Bcbf])
        S.op("dve", lambda e: e.tensor_copy(out=cbf[:, 4:7, :], in_=cmat[:, 1:4, :]), reads=[Bcm], writes=[Bcbf])
        S.op("dve", lambda e: e.tensor_copy(out=cbf[:, 8, :], in_=cmat[:, 0, :]), reads=[Bcm], writes=[Bcbf])
        ident = cmat[:, 0, :]
        C_MEAN, C_BLK, C_M128, C_ONES, C_ROT, C_TINC, C_TLOW, C_NONES, C_IDB, C_ZERO = [cbf[:, i, :] for i in range(10)]
        for l in range(2):
            for t in range(2):
                c0 = PC_L + (l * 4 + 2 * t) * 64
                tmp, Btmp, _ = f32r.next()
                S.op("dve", lambda e, tmp=tmp, c0=c0: e.tensor_tensor(out=tmp[:, 0:64], in0=params[:, c0:c0 + 64],
                                                                     in1=params[:, c0 + 64:c0 + 128], op=ALU.mult),
                     reads=[Bpar], writes=[Btmp])
                S.op("dve", lambda e, tmp=tmp, l=l, t=t: e.reduce_sum(out=sc[:, 8 + 2 * l + t:9 + 2 * l + t], in_=tmp[:, 0:64],
                                                                     axis=mybir.AxisListType.X),
                     reads=[Btmp], writes=[Bsc])
            S.op("act", lambda e, l=l: e.activation(out=sc[:, 8 + 2 * l:10 + 2 * l], in_=sc[:, 8 + 2 * l:10 + 2 * l], func=AF.Exp),
                 reads=[Bsc], writes=[Bsc])
            S.op("dve", lambda e, l=l: e.scalar_tensor_tensor(out=sc[:, l:l + 1], in0=sc[:, 9 + 2 * l:10 + 2 * l],
                                                              scalar=-LAMBDA_INIT[l], in1=sc[:, 8 + 2 * l:9 + 2 * l],
                                                              op0=ALU.add, op1=ALU.subtract),
                 reads=[Bsc], writes=[Bsc])
            S.op("dve", lambda e, l=l: e.tensor_scalar(out=sc[:, 2 + l:3 + l], in0=params[:, PC_SUB + l:PC_SUB + l + 1],
                                                       scalar1=1.0 - LAMBDA_INIT[l], scalar2=None, op0=ALU.mult),
                 reads=[Bpar, Bsc], writes=[Bsc])
            S.op("dve", lambda e, l=l: e.tensor_scalar(out=sc[:, 4 + l:5 + l], in0=params[:, PC_QN + l:PC_QN + l + 1],
                                                       scalar1=0.125, scalar2=None, op0=ALU.mult),
                 reads=[Bpar, Bsc], writes=[Bsc])

        for tb in range(LT // 128):
            xs, Bxs, dxs = xsring.next()
            S.dma("sp", lambda e, xs=xs, tb=tb: e.dma_start(out=xs[:], in_=x_d[tb * 128:(tb + 1) * 128, :]), dxs, writes=[Bxs])
            for half in range(2):
                bk = (tb * 2 + half) % 4
                for q in range(4):
                    kc = half * 4 + q
                    S.op("pe", lambda e, bk=bk, q=q, xs=xs, kc=kc: e.transpose(out=bank(bk)[:, q * 128:(q + 1) * 128],
                                                                            in_=xs[:, kc * 128:(kc + 1) * 128], identity=ident),
                         reads=[Bxs, Bcm], writes=[Bps[bk]])
                eng = "dve" if half == 0 else "act"
                dst = xT[:, half * 4:half * 4 + 4, tb * 128:(tb + 1) * 128]
                src = bank(bk).rearrange("p (a b) -> p a b", a=4)
                if eng == "dve":
                    S.op("dve", lambda e, dst=dst, src=src: e.tensor_copy(out=dst, in_=src), reads=[Bps[bk]], writes=[BxT[tb // 4]])
                else:
                    S.op("act", lambda e, dst=dst, src=src: e.activation(out=dst, in_=src, func=AF.Copy), reads=[Bps[bk]], writes=[BxT[tb // 4]])

        def load_w(wmat, col0):
            wt, Bw, dw = wring.next()
            src = wmat[:, col0:col0 + 512].rearrange("(kc p) c -> p kc c", p=128)
            S.dma("pool", lambda e: e.dma_start(out=wt[:], in_=src), dw, writes=[Bw])
            return wt, Bw

        def rms_chunk(j, gcol, dst=None):
            if dst is None:
                hT, Bh1, _ = hring.next()
                Bh = [Bh1]
            else:
                hT, Bh = dst
            cs = slice(j * CHK, (j + 1) * CHK)
            sbk = 7
            for kc in range(8):
                sq, Bsq, _ = bf16r.next()
                if kc % 4 != 3:
                    S.op("act", lambda e, sq=sq, kc=kc: e.activation(out=sq[:], in_=xT[:, kc, cs], func=AF.Square), reads=[BxT[j]], writes=[Bsq])
                else:
                    S.op("pool", lambda e, sq=sq, kc=kc: e.tensor_tensor(out=sq[:], in0=xT[:, kc, cs], in1=xT[:, kc, cs], op=ALU.mult),
                         reads=[BxT[j]], writes=[Bsq])
                S.op("pe", lambda e, sq=sq, kc=kc: e.matmul(bank(sbk), lhsT=C_MEAN, rhs=sq[:], start=(kc == 0), stop=(kc == 7)),
                     reads=[Bsq, Bcbf], writes=[Bps[sbk]])
            rstd, Brs, _ = f32r.next()
            S.op("act", lambda e: e.activation(out=rstd[:], in_=bank(sbk), func=AF.Ln, bias=EPS), reads=[Bps[sbk]], writes=[Brs])
            S.op("act", lambda e: e.activation(out=rstd[:], in_=rstd[:], func=AF.Exp, scale=-0.5), reads=[Brs], writes=[Brs])
            for kc in range(8):
                S.op("dve", lambda e, kc=kc: e.scalar_tensor_tensor(out=hT[:, kc, :], in0=xT[:, kc, cs], scalar=params[:, gcol + kc:gcol + kc + 1],
                                                                 in1=rstd[:], op0=ALU.mult, op1=ALU.mult),
                     reads=[BxT[j], Brs, Bpar], writes=Bh)
            return hT, Bh

        def proj_fm(hT, Bh, wt, Bw, cw, bk):
            for kc in range(8):
                S.op("pe", lambda e, kc=kc: e.matmul(bank(bk), lhsT=wt[:, kc, cw:cw + 128], rhs=hT[:, kc, :], start=(kc == 0), stop=(kc == 7)),
                     reads=[Bw] + Bh, writes=[Bps[bk]])

        def load_rope(j):
            rp, Brp, drp = rring.next()
            S.dma("sp", lambda e: e.dma_start(out=rp[:], in_=rope_d[:, :, j * CHK:(j + 1) * CHK].rearrange("t p n -> p t n")), drp, writes=[Brp])
            return rp, Brp

        def qknorm_rope(bk, gain_ap, Bgain, rp, Brp, dst, Bdst, tb):
            sq, Bsq, _ = bf16r.next()
            S.op("act", lambda e: e.activation(out=sq[:], in_=bank(bk), func=AF.Square), reads=[Bps[bk]], writes=[Bsq])
            S.op("pe", lambda e: e.matmul(bank(tb), lhsT=C_BLK, rhs=sq[:], start=True, stop=True), reads=[Bsq, Bcbf], writes=[Bps[tb]])
            rstd, Brs, _ = f32r.next()
            S.op("act", lambda e: e.activation(out=rstd[:], in_=bank(tb), func=AF.Ln, bias=EPS), reads=[Bps[tb]], writes=[Brs])
            S.op("act", lambda e: e.activation(out=rstd[:], in_=rstd[:], func=AF.Exp, scale=-0.5), reads=[Brs], writes=[Brs])
            tn, Btn, _ = bf16r.next()
            S.op("dve", lambda e: e.scalar_tensor_tensor(out=tn[:], in0=bank(bk), scalar=gain_ap, in1=rstd[:], op0=ALU.mult, op1=ALU.mult),
                 reads=[Bps[bk], Brs, Bgain], writes=[Btn])
            S.op("pe", lambda e: e.matmul(bank(tb), lhsT=C_ROT, rhs=tn[:], start=True, stop=True), reads=[Btn, Bcbf], writes=[Bps[tb]])
            u1, Bu1, _ = f32r.next()
            S.op("pool", lambda e: e.tensor_tensor(out=u1[:], in0=tn[:], in1=rp[:, 0, :], op=ALU.mult), reads=[Btn, Brp], writes=[Bu1])
            u2, Bu2, _ = f32r.next()
            S.op("dve", lambda e: e.tensor_tensor(out=u2[:], in0=bank(tb), in1=rp[:, 1, :], op=ALU.mult), reads=[Bps[tb], Brp], writes=[Bu2])
            S.op("pool", lambda e: e.tensor_tensor(out=dst, in0=u1[:], in1=u2[:], op=ALU.add), reads=[Bu1, Bu2], writes=[Bdst])

        def qk_pipeline(n_units, make_proj, gain_ap, Bgain, rp_fn, dst_fn, hook=None):
            stq = {}

            def P(k):
                make_proj(k, k % 3)

            def N1(k):
                bk, mb = k % 3, 3 + k % 2
                sq, Bsq, _ = bf16r.next()
                S.op("act", lambda e: e.activation(out=sq[:], in_=bank(bk), func=AF.Square), reads=[Bps[bk]], writes=[Bsq])
                S.op("pe", lambda e: e.matmul(bank(mb), lhsT=C_BLK, rhs=sq[:], start=True, stop=True), reads=[Bsq, Bcbf], writes=[Bps[mb]])

            def N2(k):
                bk, mb = k % 3, 3 + k % 2
                rstd, Brs, _ = f32r.next()
                S.op("act", lambda e: e.activation(out=rstd[:], in_=bank(mb), func=AF.Ln, bias=EPS), reads=[Bps[mb]], writes=[Brs])
                S.op("act", lambda e: e.activation(out=rstd[:], in_=rstd[:], func=AF.Exp, scale=-0.5), reads=[Brs], writes=[Brs])
                tn, Btn, _ = bf16r.next()
                S.op("dve", lambda e: e.scalar_tensor_tensor(out=tn[:], in0=bank(bk), scalar=gain_ap, in1=rstd[:], op0=ALU.mult, op1=ALU.mult),
                     reads=[Bps[bk], Brs, Bgain], writes=[Btn])
                stq[k] = (tn, Btn)

            def N3(k):
                rb = 5 + k % 2
                tn, Btn = stq.pop(k)
                rp, Brp = rp_fn(k)
                S.op("pe", lambda e: e.matmul(bank(rb), lhsT=C_ROT, rhs=tn[:], start=True, stop=True), reads=[Btn, Bcbf], writes=[Bps[rb]])
                u1, Bu1, _ = f32r.next()
                S.op("dve", lambda e: e.tensor_tensor(out=u1[:], in0=tn[:], in1=rp[:, 0, :], op=ALU.mult), reads=[Btn, Brp], writes=[Bu1])
                u2, Bu2, _ = f32r.next()
                S.op("dve", lambda e: e.tensor_tensor(out=u2[:], in0=bank(rb), in1=rp[:, 1, :], op=ALU.mult), reads=[Bps[rb], Brp], writes=[Bu2])
                dst, Bdst, post = dst_fn(k)
                S.op("pool", lambda e: e.tensor_tensor(out=dst, in0=u1[:], in1=u2[:], op=ALU.add), reads=[Bu1, Bu2], writes=[Bdst])
                if post is not None:
                    post()

            for t in range(n_units + 3):
                if hook is not None:
                    hook(t)
                if t < n_units:
                    P(t)
                if 1 <= t <= n_units:
                    N1(t - 1)
                if 2 <= t <= n_units + 1:
                    N2(t - 2)
                if t >= 3:
                    N3(t - 3)

        def phase_kv(L, wmat, kcol, vcol, gcol, qk_layer):
            htiles = [(hring.tiles[0], [hring.bufs[0]]), (hring.tiles[1], [hring.bufs[1]]), (qT, list(BqT)), (gT, list(BgT))]
            for j in range(NLC):
                rms_chunk(j, gcol, dst=htiles[j])
            wK = load_w(wmat, kcol)
            wV = load_w(wmat, vcol)
            for half in range(2):
                wt, Bw = wK
                if qk_layer is not None:
                    ropes = {}

                    def mk(k, bk, wt=wt, Bw=Bw, ropes=ropes):
                        jj, uu = k // 4, k % 4
                        if uu == 0:
                            ropes[jj] = load_rope(jj)
                        proj_fm(htiles[jj][0], htiles[jj][1], wt, Bw, uu * 128, bk)

                    def dstf(k, half=half):
                        jj, uu = k // 4, k % 4
                        u = half * 4 + uu
                        kst, Bkst, dk = stK.next()

                        def post():
                            dst = kvin[L][u // 2].ap()[(u % 2) * 128:(u % 2) * 128 + 128, jj * CHK:(jj + 1) * CHK]
                            ev = S.dma("sp", lambda e: e.dma_start(out=dst, in_=kst[:]), dk, reads=[Bkst])
                            kv_store_events[L][u // 2].append(ev)
                        return kst[:], Bkst, post

                    qk_pipeline(16, mk, params[:, PC_KN + qk_layer:PC_KN + qk_layer + 1], Bpar, lambda k, ropes=ropes: ropes[k // 4], dstf)
                else:
                    for jj in range(NLC):
                        for uu in range(4):
                            u = half * 4 + uu
                            bk = uu % 2
                            proj_fm(htiles[jj][0], htiles[jj][1], wt, Bw, uu * 128, bk)
                            kst, Bkst, dk = stK.next()
                            S.op("act", lambda e, kst=kst, bk=bk: e.activation(out=kst[:], in_=bank(bk), func=AF.Copy), reads=[Bps[bk]], writes=[Bkst])
                            dst = kvin[L][u // 2].ap()[(u % 2) * 128:(u % 2) * 128 + 128, jj * CHK:(jj + 1) * CHK]
                            ev = S.dma("sp", lambda e, dst=dst, kst=kst: e.dma_start(out=dst, in_=kst[:]), dk, reads=[Bkst])
                            kv_store_events[L][u // 2].append(ev)
                if half == 0:
                    wK = load_w(wmat, kcol + 512)
                wt, Bw = wV
                for jj in range(NLC):
                    hT, Bh = htiles[jj]
                    for tb in range(4):
                        bk = 4 + tb % 2
                        for kc in range(8):
                            S.op("pe", lambda e, kc=kc, tb=tb, bk=bk, wt=wt, hT=hT: e.matmul(bank(bk), lhsT=hT[:, kc, tb * 128:(tb + 1) * 128], rhs=wt[:, kc, :],
                                                                                   start=(kc == 0), stop=(kc == 7)),
                                 reads=[Bw] + Bh, writes=[Bps[bk]])
                        vst, Bvst, dv = stV.next()
                        S.op("dve", lambda e, vst=vst, bk=bk: e.tensor_copy(out=vst[:], in_=bank(bk)), reads=[Bps[bk]], writes=[Bvst])
                        for pr in range(2):
                            hp = half * 2 + pr
                            dst = kvin[L][hp].ap()[256:512, :].rearrange("(h r) (t e) -> (r t) h e", h=2, e=128)[
                                jj * CHK + tb * 128:jj * CHK + (tb + 1) * 128, :, :]
                            src = vst[:, pr * 256:(pr + 1) * 256].rearrange("p (h e) -> p h e", h=2)
                            ev = S.dma("sp", lambda e, dst=dst, src=src: e.dma_start(out=dst, in_=src), dv if pr == 0 else stV_d2[dv], reads=[Bvst])
                            kv_store_events[L][hp].append(ev)
                if half == 0:
                    wV = load_w(wmat, vcol + 512)
                for hp in (2 * half, 2 * half + 1):
                    mx = {}
                    for ev in kv_store_events[L][hp]:
                        mx[ev[1]] = max(mx.get(ev[1], 0), ev[2])
                    for k_, v_ in mx.items():
                        S.wait_event("pool", ("d", k_, v_))
                    S.dma("pool", lambda e, hp=hp: e.collective_compute("AllGather", ALU.bypass, replica_groups=RG,
                                                                         ins=[kvin[L][hp].ap().opt()], outs=[kvall[L][hp].ap().opt()]),
                          d_ag[L][hp], writes=[Bkvall[L][hp]], inc=1)

        def load_kv(L, u, gk):
            kt, Bkt, dkt = kring.next()
            rho, jj = OWNER[gk]
            base = kvall[L][u // 2].ap()
            ksrc = base[rho * 512 + (u % 2) * 128:rho * 512 + (u % 2) * 128 + 128, jj * CHK:(jj + 1) * CHK]
            vsrc = base[rho * 512 + 256 + (u % 2) * 128:rho * 512 + 256 + (u % 2) * 128 + 128, :].rearrange(
                "r (t e) -> (r t) e", e=128)[jj * CHK:(jj + 1) * CHK, :].rearrange("(kb r) e -> r kb e", r=128)
            S.dma("sp", lambda e: e.dma_start(out=kt[:, 0, :], in_=ksrc), dkt, reads=[Bkvall[L][u // 2]], writes=[Bkt])
            S.dma("sp", lambda e: e.dma_start(out=kt[:, 1, :].rearrange("p (kb e) -> p kb e", e=128), in_=vsrc), dkt,
                  reads=[Bkvall[L][u // 2]], writes=[Bkt])
            return kt, Bkt

        def band_for(kind, j, gk):
            if gk == MJ[j]:
                ts = 0
            elif gk == MJ[j] - 1:
                ts = 1
            else:
                return None
            return lambda i: bands[:, j % 2, ts, 384 - 128 * i:896 - 128 * i]

        def out_proj(j, wmat, pre=None):
            cs = slice(j * CHK, (j + 1) * CHK)
            for half in range(2):
                wt, Bw = pre[half] if pre is not None else load_w(wmat, half * 512)
                for q in range(4):
                    ncn = half * 4 + q
                    bk = q % 2
                    for u in range(8):
                        S.op("pe", lambda e, u=u, q=q, bk=bk, wt=wt: e.matmul(bank(bk), lhsT=wt[:, u, q * 128:(q + 1) * 128], rhs=gT[:, u, :],
                                                                           start=(u == 0), stop=(u == 7)),
                             reads=[Bw, BgT[u]], writes=[Bps[bk]])
                    S.op("dve", lambda e, ncn=ncn, bk=bk: e.tensor_tensor(out=xT[:, ncn, cs], in0=bank(bk), in1=xT[:, ncn, cs], op=ALU.add),
                         reads=[Bps[bk]], writes=[BxT[j]])

        def diff_layer(l):
            wmat = a_w_in[l]
            phase_kv(l, wmat, 1024, 2048, PC_AN + 8 * l, l)
            for j in range(NLC):
                wts = [load_w(wmat, half * 512) for half in range(2)]
                hT, Bh = rms_chunk(j, PC_AN + 8 * l)
                rp, Brp = load_rope(j)
                wg = {}

                def mkq(k, bk, wts=wts, hT=hT, Bh=Bh):
                    proj_fm(hT, Bh, wts[k // 4][0], wts[k // 4][1], (k % 4) * 128, bk)

                def hookq(t, wg=wg):
                    if t == 5:
                        wg[0] = load_w(wmat, 3072)
                    if t == 9:
                        wg[1] = load_w(wmat, 3072 + 512)

                qk_pipeline(8, mkq, sc[:, 4 + l:5 + l], Bsc, lambda k: (rp, Brp), lambda k: (qT[:, k, :], BqT[k], None), hook=hookq)
                for half in range(2):
                    wt, Bw = wg[half]
                    for uu in range(4):
                        u = half * 4 + uu
                        bk = uu % 2
                        proj_fm(hT, Bh, wt, Bw, uu * 128, bk)
                        S.op("act", lambda e, u=u, bk=bk: e.activation(out=gT[:, u, :], in_=bank(bk), func=AF.Silu), reads=[Bps[bk]], writes=[BgT[u]])
                wo = [load_w(a_w_out[l], half * 512) for half in range(2)]
                nblk = (MJ[j] + 1) * 4
                items = [(u, gk, i) for u in range(8) for gk in range(MJ[j] + 1) for i in range(4)]
                stt = {}

                def stage_a(idx):
                    u, gk, i = items[idx]
                    n = idx % nblk
                    if i == 0:
                        stt[("kv", u, gk)] = load_kv(l, u, gk)
                    kt, Bkt = stt[("kv", u, gk)]
                    if n == 0:
                        stt[("es", u)] = big32.next()
                    es, Bes, _ = stt[("es", u)]
                    sp_ = idx % 2
                    b0, b1 = 2 * sp_, 2 * sp_ + 1
                    ksl = slice(i * 128, (i + 1) * 128)
                    bf = band_for(0, j, gk)
                    nomask = bf is None
                    S.op("pe", lambda e: e.matmul(bank(b0), lhsT=kt[0:64, 0, ksl], rhs=qT[0:64, u, :], start=True, stop=nomask),
                         reads=[Bkt, BqT[u]], writes=[Bps[b0]])
                    S.op("pe", lambda e: e.matmul(bank(b1), lhsT=kt[64:128, 0, ksl], rhs=qT[64:128, u, :], start=True, stop=nomask),
                         reads=[Bkt, BqT[u]], writes=[Bps[b1]])
                    if not nomask:
                        m = bf(i)
                        for bb in (b0, b1):
                            S.op("pe", lambda e, bb=bb: e.matmul(bank(bb), lhsT=C_IDB, rhs=m, start=False, stop=True),
                                 reads=[Bcbf, Bbands], writes=[Bps[bb]])
                    et, Bet, _ = ering.next()
                    S.op("act", lambda e: e.activation(out=et[:], in_=ps[sp_][:], func=AF.Exp), reads=[Bps[b0], Bps[b1]], writes=[Bet])
                    if n == 0:
                        S.op("dve", lambda e: e.tensor_copy(out=es[:, 0:CHK], in_=et[:, 0:CHK]), reads=[Bet], writes=[Bes])
                    else:
                        S.op("dve", lambda e: e.tensor_tensor(out=es[:, 0:CHK], in0=es[:, 0:CHK], in1=et[:, 0:CHK], op=ALU.add), reads=[Bet, Bes], writes=[Bes])
                    stt[("et", idx)] = (et, Bet)

                def stage_b(idx):
                    u, gk, i = items[idx]
                    n = idx % nblk
                    kt, Bkt = stt[("kv", u, gk)]
                    et, Bet = stt.pop(("et", idx))
                    first, last = (n == 0), (n == nblk - 1)
                    vblk = kt[:, 1, i * 128:(i + 1) * 128]
                    for hh in range(2):
                        S.op("pe", lambda e, hh=hh: e.matmul(bank(4 + hh), lhsT=vblk, rhs=et[:, hh * CHK:(hh + 1) * CHK], start=first, stop=last),
                             reads=[Bkt, Bet], writes=[Bps[4 + hh]])
                    S.op("pe", lambda e: e.matmul(bank(7), lhsT=C_ONES, rhs=et[:, CHK:2 * CHK], start=first, stop=last),
                         reads=[Bcbf, Bet], writes=[Bps[7]])
                    if last:
                        while pending:
                            pending.pop(0)()
                        pending.extend(epilogue_steps(u))
                        pending.pop(0)()

                def epilogue_steps(u):
                    es, Bes, _ = stt.pop(("es", u))
                    o12, Bo12, _ = big32.next()
                    esb, Besb, _ = bf16r.next()
                    t1, Bt1, _ = f32r.next()
                    t2, Bt2, _ = f32r.next()
                    r1, Br1, _ = f32r.next()
                    sq, Bsq, _ = bf16r.next()

                    def e0():
                        S.op("act", lambda e: e.activation(out=o12[:], in_=ps[2][:], func=AF.Copy), reads=[Bps[4], Bps[5]], writes=[Bo12])
                        S.op("dve", lambda e: e.tensor_copy(out=es[:, CHK:2 * CHK], in_=bank(7)), reads=[Bps[7], Bes], writes=[Bes])
                        S.op("dve", lambda e: e.tensor_copy(out=esb[:], in_=es[:, 0:CHK]), reads=[Bes], writes=[Besb])

                    def e1():
                        S.op("pe", lambda e: e.matmul(bank(6), lhsT=C_ONES, rhs=esb[:], start=True, stop=True), reads=[Bcbf, Besb], writes=[Bps[6]])

                    def e2():
                        S.op("dve", lambda e: e.tensor_tensor(out=t1[:], in0=o12[:, 0:CHK], in1=es[:, CHK:2 * CHK], op=ALU.mult), reads=[Bes, Bo12], writes=[Bt1])
                        S.op("dve", lambda e: e.scalar_tensor_tensor(out=t2[:], in0=o12[:, CHK:2 * CHK], scalar=sc[:, l:l + 1], in1=bank(6),
                                                                    op0=ALU.mult, op1=ALU.mult),
                             reads=[Bo12, Bps[6], Bsc], writes=[Bt2])
                        S.op("dve", lambda e: e.tensor_tensor(out=es[:, 0:CHK], in0=bank(6), in1=es[:, CHK:2 * CHK], op=ALU.mult), reads=[Bps[6], Bes], writes=[Bes])
                        S.op("dve", lambda e: e.tensor_tensor(out=es[:, 0:CHK], in0=es[:, 0:CHK], in1=es[:, 0:CHK], op=ALU.mult), reads=[Bes], writes=[Bes])

                    def e3():
                        S.op("pool", lambda e: e.tensor_tensor(out=t1[:], in0=t1[:], in1=t2[:], op=ALU.add), reads=[Bt1, Bt2], writes=[Bt1])
                        S.op("pool", lambda e: e.tensor_tensor(out=sq[:], in0=t1[:], in1=t1[:], op=ALU.mult), reads=[Bt1], writes=[Bsq])

                    def enop():
                        pass

                    def e4():
                        S.op("pe", lambda e: e.matmul(bank(6), lhsT=C_M128, rhs=sq[:], start=True, stop=True), reads=[Bsq, Bcbf], writes=[Bps[6]])
                        S.op("dve", lambda e: e.scalar_tensor_tensor(out=r1[:], in0=es[:, 0:CHK], scalar=EPS, in1=bank(6), op0=ALU.mult, op1=ALU.add),
                             reads=[Bes, Bps[6]], writes=[Br1])
                        S.op("act", lambda e: e.activation(out=r1[:], in_=r1[:], func=AF.Ln), reads=[Br1], writes=[Br1])

                    def e5():
                        S.op("act", lambda e: e.activation(out=r1[:], in_=r1[:], func=AF.Exp, scale=-0.5), reads=[Br1], writes=[Br1])
                        S.op("dve", lambda e: e.scalar_tensor_tensor(out=t1[:], in0=t1[:], scalar=sc[:, 2 + l:3 + l], in1=r1[:], op0=ALU.mult, op1=ALU.mult),
                             reads=[Bt1, Br1, Bsc], writes=[Bt1])
                        S.op("pool", lambda e: e.tensor_tensor(out=gT[:, u, :], in0=t1[:], in1=gT[:, u, :], op=ALU.mult), reads=[Bt1], writes=[BgT[u]])

                    return [e0, e1, e2, e3, enop, e4, e5]

                pending = []
                for t in range(len(items) + 2):
                    if t < len(items):
                        stage_a(t)
                    if t >= 2:
                        stage_b(t - 2)
                    if pending and (t % nblk) >= 3:
                        pending.pop(0)()
                while pending:
                    pending.pop(0)()
                out_proj(j, a_w_out[l], pre=wo)

        def sb_layer(jl):
            wmat = b_w_in[jl]
            for j in range(NLC):
                wq = [load_w(wmat, half * 512) for half in range(2)]
                hT, Bh = rms_chunk(j, PC_BN + 8 * jl)
                wg = {}
                for half in range(2):
                    wt, Bw = wq[half]
                    for uu in range(4):
                        u = half * 4 + uu
                        bk = uu % 2
                        proj_fm(hT, Bh, wt, Bw, uu * 128, bk)
                        S.op("dve", lambda e, u=u, bk=bk: e.tensor_copy(out=qT[:, u, :], in_=bank(bk)), reads=[Bps[bk]], writes=[BqT[u]])
                    wg[half] = load_w(wmat, 1024 + half * 512)
                for half in range(2):
                    wt, Bw = wg[half]
                    for uu in range(4):
                        u = half * 4 + uu
                        bk = uu % 2
                        proj_fm(hT, Bh, wt, Bw, uu * 128, bk)
                        S.op("act", lambda e, u=u, bk=bk: e.activation(out=gT[:, u, :], in_=bank(bk), func=AF.Silu), reads=[Bps[bk]], writes=[BgT[u]])
                wo = [load_w(b_w_out[jl], half * 512) for half in range(2)]
                nblk = (MJ[j] + 1) * 4
                items = [(u, gk, i) for u in range(8) for gk in range(MJ[j], -1, -1) for i in range(3, -1, -1)]
                stt = {}

                def offs(gk, i):
                    return 128 * i if gk == MJ[j] else 0

                def v2(ap, off):
                    return ap if off == 0 else ap.rearrange("p (h c) -> p h c", h=2)[:, :, off:]

                def s1(idx):
                    u, gk, i = items[idx]
                    off = offs(gk, i)
                    if i == 3:
                        stt[("kv", u, gk)] = load_kv(2, u, gk)
                    kt, Bkt = stt[("kv", u, gk)]
                    ksl = slice(i * 128, (i + 1) * 128)
                    bf = band_for(1, j, gk)
                    nomask = bf is None
                    S.op("pe", lambda e: e.matmul(bank(0)[:, off:], lhsT=kt[0:64, 0, ksl], rhs=qT[0:64, u, off:], start=True, stop=nomask),
                         reads=[Bkt, BqT[u]], writes=[Bps[0]])
                    S.op("pe", lambda e: e.matmul(bank(1)[:, off:], lhsT=kt[64:128, 0, ksl], rhs=qT[64:128, u, off:], start=True, stop=nomask),
                         reads=[Bkt, BqT[u]], writes=[Bps[1]])
                    if not nomask:
                        m = bf(i)
                        for bb in (0, 1):
                            S.op("pe", lambda e, bb=bb: e.matmul(bank(bb)[:, off:], lhsT=C_IDB, rhs=m[:, off:], start=False, stop=True),
                                 reads=[Bcbf, Bbands], writes=[Bps[bb]])
                    eb, Beb, _ = ebring.next()
                    S.op("act", lambda e: e.activation(out=v2(eb[:], off), in_=v2(ps[0][:], off), func=AF.Exp, scale=0.125), reads=[Bps[0], Bps[1]], writes=[Beb])
                    stt[("s1a", idx)] = (eb, Beb)

                def s1b(idx):
                    u, gk, i = items[idx]
                    off = offs(gk, i)
                    eb, Beb = stt.pop(("s1a", idx))
                    spt, Bspt, _ = spring.next()
                    S.op("act", lambda e: e.activation(out=v2(spt[:], off), in_=v2(eb[:], off), func=AF.Ln, bias=1.0), reads=[Beb], writes=[Bspt])
                    stt[("s1", idx)] = (eb, Beb, spt, Bspt)

                def s2(idx):
                    u, gk, i = items[idx]
                    off = offs(gk, i)
                    n = idx % nblk
                    eb, Beb, spt, Bspt = stt.pop(("s1", idx))
                    if n == 0:
                        S.op("pool", lambda e: e.memset(Rc[:], 0.0), writes=[BRc])
                    for hh in range(2):
                        S.op("pe", lambda e, hh=hh: e.matmul(bank(2 + hh)[:, off:], lhsT=C_TINC, rhs=spt[:, hh * CHK + off:(hh + 1) * CHK], start=True, stop=True),
                             reads=[Bcbf, Bspt], writes=[Bps[2 + hh]])
                    tt, Btt, _ = big32.next()
                    S.op("dve", lambda e: e.tensor_tensor(out=v2(tt[:], off), in0=v2(ps[1][:], off), in1=v2(Rc[:], off), op=ALU.add),
                         reads=[Bps[2], Bps[3], BRc], writes=[Btt])
                    if n != nblk - 1:
                        for hh in range(2):
                            S.op("pe", lambda e, hh=hh: e.matmul(bank(4 + hh)[:, off:], lhsT=C_NONES, rhs=spt[:, hh * CHK + off:(hh + 1) * CHK], start=True, stop=True),
                                 reads=[Bcbf, Bspt], writes=[Bps[4 + hh]])
                        S.op("dve", lambda e: e.tensor_tensor(out=v2(Rc[:], off), in0=v2(ps[2][:], off), in1=v2(Rc[:], off), op=ALU.add),
                             reads=[Bps[4], Bps[5], BRc], writes=[BRc])
                    stt[("s2", idx)] = (eb, Beb, tt, Btt)

                def s2b(idx):
                    u, gk, i = items[idx]
                    off = offs(gk, i)
                    eb, Beb, tt, Btt = stt.pop(("s2", idx))
                    wb, Bwb, _ = wring2.next()
                    S.op("act", lambda e: e.activation(out=v2(wb[:], off), in_=v2(tt[:], off), func=AF.Exp), reads=[Btt], writes=[Bwb])
                    at, Bat, _ = ering.next()
                    S.op("pool", lambda e: e.tensor_tensor(out=v2(at[:], off), in0=v2(eb[:], off), in1=v2(wb[:], off), op=ALU.mult), reads=[Beb, Bwb], writes=[Bat])
                    stt[("at", idx)] = (at, Bat)

                def s3(idx):
                    u, gk, i = items[idx]
                    off = offs(gk, i)
                    n = idx % nblk
                    kt, Bkt = stt[("kv", u, gk)]
                    at, Bat = stt.pop(("at", idx))
                    last = (n == nblk - 1)
                    if n == 0:
                        S.op("pe", lambda e: e.matmul(bank(6), lhsT=C_ZERO, rhs=qT[:, u, :], start=True, stop=False),
                             reads=[Bcbf, BqT[u]], writes=[Bps[6]])
                    for hh in range(2):
                        S.op("pe", lambda e, hh=hh: e.matmul(bank(6)[hh * 64:(hh + 1) * 64, off:], lhsT=kt[:, 1, i * 128 + hh * 64:i * 128 + (hh + 1) * 64],
                                                            rhs=at[:, hh * CHK + off:(hh + 1) * CHK], start=False, stop=last, tile_position=(0, hh * 64)),
                             reads=[Bkt, Bat], writes=[Bps[6]])
                    if last:
                        S.op("dve", lambda e: e.tensor_tensor(out=gT[:, u, :], in0=bank(6), in1=gT[:, u, :], op=ALU.mult), reads=[Bps[6]], writes=[BgT[u]])

                NI = len(items)
                for t in range(NI + 4):
                    if t < NI:
                        s1(t)
                    if 3 <= t <= NI + 2:
                        s2b(t - 3)
                    if 2 <= t <= NI + 1:
                        s2(t - 2)
                    if t < NI:
                        s1b(t)
                    if t >= 4:
                        s3(t - 4)
                out_proj(j, b_w_out[jl], pre=wo)

        if n_layers >= 1:
            diff_layer(0)
        if n_layers >= 2:
            diff_layer(1)
        if n_layers >= 3:
            S.dma("sp", lambda e: e.dma_start(out=bands[:], in_=bands_d[1]), d_bands, writes=[Bbands])
            phase_kv(2, w_kv, 0, 1024, PC_KVN, None)
            sb_layer(0)
        if n_layers >= 4:
            sb_layer(1)

        for tb in range(LT // 128):
            xs, Bxs, dxs = xsring.next()
            for half in range(2):
                bk = (tb * 2 + half) % 4
                for q in range(4):
                    kc = half * 4 + q
                    S.op("pe", lambda e, bk=bk, q=q, kc=kc, tb=tb: e.transpose(out=bank(bk)[:, q * 128:(q + 1) * 128],
                                                                            in_=xT[:, kc, tb * 128:(tb + 1) * 128], identity=ident),
                         reads=[BxT[tb // 4], Bcm], writes=[Bps[bk]])
                dst = xs[:, half * 512:(half + 1) * 512]
                if half == 0:
                    S.op("dve", lambda e, dst=dst, bk=bk: e.tensor_copy(out=dst, in_=bank(bk)), reads=[Bps[bk]], writes=[Bxs])
                else:
                    S.op("act", lambda e, dst=dst, bk=bk: e.activation(out=dst, in_=bank(bk), func=AF.Copy), reads=[Bps[bk]], writes=[Bxs])
            S.dma("sp", lambda e, xs=xs, tb=tb: e.dma_start(out=y_d[tb * 128:(tb + 1) * 128, :], in_=xs[:]), dxs, reads=[Bxs])
        for dd in xsring.dsems:
            S.wait_event("sp", ("d", dd, S.dsem_cnt[dd]))
        S.emit(st)
    return nc


def _host_constants():
    ident = np.eye(128, dtype=np.float32)
    rot = np.zeros((128, 128), np.float32)
    for c in range(2):
        for d in range(8):
            rot[c * 64 + d + 8, c * 64 + d] = -1.0
            rot[c * 64 + d, c * 64 + d + 8] = 1.0
    jj, ss = np.meshgrid(np.arange(128), np.arange(128), indexing="ij")
    tinc = -(jj >= ss).astype(np.float32)
    tlow = -(jj < ss).astype(np.float32)
    cmat = np.stack([ident, rot, tinc, tlow], axis=1)
    return np.ascontiguousarray(cmat)


def _rope_tables(p):
    half = 8
    inv = np.power(np.float32(500000.0), -np.arange(half, dtype=np.float32) / np.float32(half)).astype(np.float32)
    pos = np.concatenate([np.arange(g * CHK, (g + 1) * CHK) for g in GCH[p]]).astype(np.float32)
    ang = pos[None, :] * inv[:, None]
    cos, sin = np.cos(ang).astype(np.float32), np.sin(ang).astype(np.float32)
    C = np.ones((128, LT), np.float32)
    Sn = np.zeros((128, LT), np.float32)
    for c in range(2):
        for d in range(16):
            C[c * 64 + d] = cos[d % 8]
            Sn[c * 64 + d] = sin[d % 8]
    return np.ascontiguousarray(np.stack([C, Sn], 0))


def _bands(p):
    r = np.arange(128)[:, None]
    u = np.arange(896)[None, :]
    diag = [(r <= u - 384), (r < u - 384)]
    out = np.zeros((2, 128, 2, 2, 896), np.float32)
    for kind in range(2):
        for jpar in range(2):
            high = ((jpar + p) % 2 == 1)
            if high:
                out[kind, :, jpar, 0] = diag[kind]
                out[kind, :, jpar, 1] = 1.0
            else:
                out[kind, :, jpar, 0] = 0.0
                out[kind, :, jpar, 1] = diag[kind]
    out = (out - 1.0) * 30000.0
    return out.astype(ml_dtypes.bfloat16)


def _params(inp):
    P = np.zeros((128, NPAR), np.float32)
    f = lambda v: np.asarray(v, np.float32)
    for l in range(2):
        P[:, PC_AN + 8 * l:PC_AN + 8 * l + 8] = f(inp["a_norm"])[l].reshape(8, 128).T
        P[:, PC_BN + 8 * l:PC_BN + 8 * l + 8] = f(inp["b_norm"])[l].reshape(8, 128).T
        P[:, PC_QN + l] = np.tile(f(inp["a_q_norm"])[l], 2)
        P[:, PC_KN + l] = np.tile(f(inp["a_k_norm"])[l], 2)
        P[:, PC_SUB + l] = f(inp["a_subln"])[l]
        for t, nm in enumerate(["a_lq1", "a_lk1", "a_lq2", "a_lk2"]):
            c0 = PC_L + (l * 4 + t) * 64
            P[:, c0:c0 + 64] = f(inp[nm])[l][None, :]
    P[:, PC_KVN:PC_KVN + 8] = f(inp["kv_norm"]).reshape(8, 128).T
    return P


_NC_CACHE = {}


def _run(inp, n_layers=4):
    x = np.asarray(inp["x"], np.float32)
    if n_layers not in _NC_CACHE:
        _NC_CACHE[n_layers] = build_nc(n_layers)
    nc = _NC_CACHE[n_layers]
    cmat = _host_constants()
    params = _params(inp)
    shared = {k: np.ascontiguousarray(np.asarray(inp[k], np.float32)) for k in ("a_w_in", "a_w_out", "w_kv", "b_w_in", "b_w_out")}
    in_maps = []
    for c in range(8):
        b, p = c // 2, c % 2
        xs = np.concatenate([x[b, g * CHK:(g + 1) * CHK] for g in GCH[p]], 0)
        m = {"x": np.ascontiguousarray(xs), "params": params, "cmat": cmat, "rope": _rope_tables(p), "bands": _bands(p)}
        m.update(shared)
        in_maps.append(m)
    res = run_bass_kernel_spmd(nc, in_maps, core_ids=list(range(8)))
    out = np.zeros((NB, SEQ, D), np.float32)
    for c in range(8):
        b, p = c // 2, c % 2
        y = res.results[c]["y"]
        for j, g in enumerate(GCH[p]):
            out[b, g * CHK:(g + 1) * CHK] = y[j * CHK:(j + 1) * CHK]
    return out


def kernel(**inputs):
    return _run(inputs, 4)
```

```python
from contextlib import ExitStack
import math
import numpy as np
import ml_dtypes
import concourse.bass as bass
import concourse.mybir as mybir
from concourse.bass_utils import run_bass_kernel_spmd

F32 = mybir.dt.float32
BF16 = mybir.dt.bfloat16
AF = mybir.ActivationFunctionType
ALU = mybir.AluOpType

D = 1024
SEQ = 4096
NB = 4
CHK = 512
NLC = 4
LT = NLC * CHK
GCH = [[0, 3, 4, 7], [1, 2, 5, 6]]
OWNER = {}
for _p in range(2):
    for _j, _g in enumerate(GCH[_p]):
        OWNER[_g] = (_p, _j)
MJ = [max(GCH[0][j], GCH[1][j]) for j in range(NLC)]
EPS = 1e-6
LAMBDA_INIT = [0.8 - 0.6 * math.exp(-0.3 * l) for l in range(2)]
RG = [[0, 1], [2, 3], [4, 5], [6, 7]]
SEMCH = 2048

PC_AN, PC_KVN, PC_BN, PC_QN, PC_KN, PC_SUB, PC_L = 0, 16, 24, 40, 42, 44, 46
NPAR = PC_L + 2 * 4 * 64


class Buf:
    __slots__ = ("name", "w", "r")

    def __init__(self, name):
        self.name = name
        self.w = None
        self.r = {}


class Sched:
    CE = ("pe", "act", "dve", "pool")
    QE = ("pe", "act", "dve", "pool", "sp")

    def __init__(self, nc):
        self.nc = nc
        self.ops = {e: [] for e in self.QE}
        self.cnt = {e: 0 for e in self.CE}
        self.known = {e: {} for e in self.QE}
        self.dsem_cnt = []
        self.esems = None
        self.dsems = None

    def _wait(self, eng, ev):
        if ev is None:
            return
        kind, key, val = ev
        if kind == "e" and key == eng and eng == "pe":
            return
        k = (kind, key)
        if self.known[eng].get(k, 0) >= val:
            return
        self.known[eng][k] = val
        self.ops[eng].append(("wait", ev))

    def new_dsem(self):
        self.dsem_cnt.append(0)
        return len(self.dsem_cnt) - 1

    def op(self, eng, fn, reads=(), writes=()):
        for b in reads:
            self._wait(eng, b.w)
        for b in writes:
            self._wait(eng, b.w)
            for ev in b.r.values():
                if ev[0] == "e" and ev[1] == eng and ev[2] == self.cnt[eng] and False:
                    continue
                self._wait(eng, ev)
        self.cnt[eng] += 1
        idx = self.cnt[eng]
        me = ("e", eng, idx)
        self.ops[eng].append(("op", fn, idx))
        for b in reads:
            b.r[("e", eng)] = me
        for b in writes:
            b.w = me
            b.r = {}
        return me

    def dma(self, q, fn, dsem, reads=(), writes=(), inc=16):
        for b in reads:
            self._wait(q, b.w)
        for b in writes:
            self._wait(q, b.w)
            for ev in b.r.values():
                self._wait(q, ev)
        self.dsem_cnt[dsem] += inc
        me = ("d", dsem, self.dsem_cnt[dsem])
        self.ops[q].append(("dma", fn, dsem, inc))
        for b in reads:
            b.r[("d", dsem)] = me
        for b in writes:
            b.w = me
            b.r = {}
        return me

    def wait_event(self, eng, ev):
        self._wait(eng, ev)

    def _sem_of(self, ev):
        kind, key, val = ev
        if kind == "e":
            return self.esems[key][(val - 1) // SEMCH], (val - 1) % SEMCH + 1
        return self.dsems[key], val

    def emit(self, stack):
        nc = self.nc
        self.esems = {}
        for e in self.CE:
            n = max(1, (self.cnt[e] + SEMCH - 1) // SEMCH)
            self.esems[e] = [stack.enter_context(nc.semaphore(f"s_{e}{i}")) for i in range(n)]
        self.dsems = [stack.enter_context(nc.semaphore(f"d{i}")) for i in range(max(1, len(self.dsem_cnt)))]
        block = stack.enter_context(nc.Block())

        def run(engname):
            def body(eng):
                for o in self.ops[engname]:
                    if o[0] == "wait":
                        s, v = self._sem_of(o[1])
                        eng.wait_ge(s, v)
                    elif o[0] == "op":
                        _, fn, idx = o
                        fn(eng).then_inc(self.esems[engname][(idx - 1) // SEMCH], 1)
                    else:
                        _, fn, dsem, inc = o
                        fn(eng).then_inc(self.dsems[dsem], inc)
            return body

        block.tensor(run("pe"))
        block.scalar(run("act"))
        block.vector(run("dve"))
        block.gpsimd(run("pool"))
        block.sync(run("sp"))


class Ring:
    def __init__(self, S, st, nc, name, n, shape, dtype, with_dsem=True):
        self.tiles = [st.enter_context(nc.sbuf_tensor(f"{name}{i}", shape, dtype)) for i in range(n)]
        self.bufs = [Buf(f"{name}{i}") for i in range(n)]
        self.dsems = [S.new_dsem() for _ in range(n)] if with_dsem else [None] * n
        self.i = 0
        self.n = n

    def next(self):
        k = self.i % self.n
        self.i += 1
        return self.tiles[k], self.bufs[k], self.dsems[k]


def build_nc(n_layers=4):
    nc = bass.Bass("TRN2", target_bir_lowering=False)
    dt_in = lambda n, s, d=F32: nc.dram_tensor(n, s, d, kind="ExternalInput").ap()
    x_d = dt_in("x", [LT, D])
    a_w_in = dt_in("a_w_in", [2, D, 4096])
    a_w_out = dt_in("a_w_out", [2, D, D])
    w_kv = dt_in("w_kv", [D, 2048])
    b_w_in = dt_in("b_w_in", [2, D, 2048])
    b_w_out = dt_in("b_w_out", [2, D, D])
    params_d = dt_in("params", [128, NPAR])
    cmat_d = dt_in("cmat", [128, 4, 128])
    rope_d = dt_in("rope", [2, 128, LT])
    bands_d = dt_in("bands", [2, 128, 2, 2, 896], BF16)
    y_d = nc.dram_tensor("y", [LT, D], F32, kind="ExternalOutput").ap()
    kvin = [[nc.dram_tensor(f"kvin_{l}_{h}", [512, LT], BF16) for h in range(4)] for l in range(3)]
    kvall = [[nc.dram_tensor(f"kvall_{l}_{h}", [1024, LT], BF16) for h in range(4)] for l in range(3)]

    S = Sched(nc)
    with ExitStack() as st:
        sb = lambda n, s, d: st.enter_context(nc.sbuf_tensor(n, s, d))
        xT = sb("xT", [128, 8, LT], F32)
        BxT = [Buf(f"xT{j}") for j in range(NLC)]
        params = sb("params_sb", [128, NPAR], F32)
        Bpar = Buf("params")
        cmat = sb("cmat_sb", [128, 4, 128], F32)
        Bcm = Buf("cmat")
        cbf = sb("cbf", [128, 9, 128], BF16)
        Bcbf = Buf("cbf")
        bands = sb("bands_sb", [128, 2, 2, 896], BF16)
        Bbands = Buf("bands")
        sc = sb("scal", [128, 16], F32)
        Bsc = Buf("scal")
        qT = sb("qT", [128, 8, CHK], BF16)
        BqT = [Buf(f"qT{u}") for u in range(8)]
        gT = sb("gT", [128, 8, CHK], BF16)
        BgT = [Buf(f"gT{u}") for u in range(8)]
        hring = Ring(S, st, nc, "hT", 2, [128, 8, CHK], BF16, with_dsem=False)
        wring = Ring(S, st, nc, "wt", 2, [128, 8, 512], BF16)
        kring = Ring(S, st, nc, "kt", 4, [128, 2, CHK], BF16)
        rring = Ring(S, st, nc, "rp", 2, [128, 2, CHK], F32)
        big32 = Ring(S, st, nc, "big32", 4, [128, 2 * CHK], F32)
        xsring = big32
        f32r = Ring(S, st, nc, "f32t", 4, [128, CHK], F32, with_dsem=False)
        bf16r = Ring(S, st, nc, "bft", 5, [128, CHK], BF16, with_dsem=False)
        ering = Ring(S, st, nc, "et", 4, [128, 2 * CHK], BF16, with_dsem=False)
        ebring = Ring(S, st, nc, "ebt", 5, [128, 2 * CHK], BF16, with_dsem=False)
        Rc = sb("Rc", [128, 2 * CHK], F32)
        BRc = Buf("Rc")
        spring = Ring(S, st, nc, "spt", 3, [128, 2 * CHK], BF16, with_dsem=False)
        wring2 = Ring(S, st, nc, "wbt", 2, [128, 2 * CHK], BF16, with_dsem=False)
        stK = Ring(S, st, nc, "stK", 2, [128, CHK], BF16)
        stV = Ring(S, st, nc, "stV", 2, [128, CHK], BF16)
        stV_d2 = {d: S.new_dsem() for d in stV.dsems}
        ps = [st.enter_context(nc.psum_tensor(f"ps{i}", [128, 2 * CHK], F32)) for i in range(4)]
        Bps = [Buf(f"bank{i}") for i in range(8)]

        def bank(i):
            return ps[i // 2][:, (i % 2) * CHK:(i % 2 + 1) * CHK]

        d_misc = S.new_dsem()
        d_bands = S.new_dsem()
        d_out = S.new_dsem()
        d_ag = [[S.new_dsem() for _ in range(4)] for _ in range(3)]
        Bkvall = [[Buf(f"kvall{l}{h}") for h in range(4)] for l in range(3)]
        kv_store_events = [[[] for _ in range(4)] for _ in range(3)]

        S.dma("sp", lambda e: e.dma_start(out=params[:], in_=params_d), d_misc, writes=[Bpar])
        S.dma("sp", lambda e: e.dma_start(out=cmat[:], in_=cmat_d), d_out, writes=[Bcm])
        S.dma("sp", lambda e: e.dma_start(out=bands[:], in_=bands_d[0]), d_bands, writes=[Bbands])
        S.op("dve", lambda e: e.memset(cbf[:, 0, :], 1.0 / 1024.0), writes=[Bcbf])
        S.op("dve", lambda e: e.memset(cbf[:, 1, :], 0.0), writes=[Bcbf])
        S.op("dve", lambda e: e.memset(cbf[0:64, 1, 0:64], 1.0 / 64.0), writes=[Bcbf])
        S.op("dve", lambda e: e.memset(cbf[64:128, 1, 64:128], 1.0 / 64.0), writes=[Bcbf])
        S.op("dve", lambda e: e.memset(cbf[:, 2, :], 1.0 / 128.0), writes=[Bcbf])
        S.op("dve", lambda e: e.memset(cbf[:, 3, :], 1.0), writes=[Bcbf])
        S.op("dve", lambda e: e.memset(cbf[:, 7, :], -1.0), writes=[Bcbf])
        S.op("dve", lambda e: e.tensor_copy(out=cbf[:, 4:7, :], in_=cmat[:, 1:4, :]), reads=[Bcm], writes=[Bcbf])
        S.op("dve", lambda e: e.tensor_copy(out=cbf[:, 8, :], in_=cmat[:, 0, :]), reads=[Bcm], writes=[Bcbf])
        ident = cmat[:, 0, :]
        C_MEAN, C_BLK, C_M128, C_ONES, C_ROT, C_TINC, C_TLOW, C_NONES, C_IDB = [cbf[:, i, :] for i in range(9)]
        for l in range(2):
            for t in range(2):
                c0 = PC_L + (l * 4 + 2 * t) * 64
                tmp, Btmp, _ = f32r.next()
                S.op("dve", lambda e, tmp=tmp, c0=c0: e.tensor_tensor(out=tmp[:, 0:64], in0=params[:, c0:c0 + 64],
                                                                     in1=params[:, c0 + 64:c0 + 128], op=ALU.mult),
                     reads=[Bpar], writes=[Btmp])
                S.op("dve", lambda e, tmp=tmp, l=l, t=t: e.reduce_sum(out=sc[:, 8 + 2 * l + t:9 + 2 * l + t], in_=tmp[:, 0:64],
                                                                     axis=mybir.AxisListType.X),
                     reads=[Btmp], writes=[Bsc])
            S.op("act", lambda e, l=l: e.activation(out=sc[:, 8 + 2 * l:10 + 2 * l], in_=sc[:, 8 + 2 * l:10 + 2 * l], func=AF.Exp),
                 reads=[Bsc], writes=[Bsc])
            S.op("dve", lambda e, l=l: e.scalar_tensor_tensor(out=sc[:, l:l + 1], in0=sc[:, 9 + 2 * l:10 + 2 * l],
                                                              scalar=-LAMBDA_INIT[l], in1=sc[:, 8 + 2 * l:9 + 2 * l],
                                                              op0=ALU.add, op1=ALU.subtract),
                 reads=[Bsc], writes=[Bsc])
            S.op("dve", lambda e, l=l: e.tensor_scalar(out=sc[:, 2 + l:3 + l], in0=params[:, PC_SUB + l:PC_SUB + l + 1],
                                                       scalar1=1.0 - LAMBDA_INIT[l], scalar2=None, op0=ALU.mult),
                 reads=[Bpar, Bsc], writes=[Bsc])
            S.op("dve", lambda e, l=l: e.tensor_scalar(out=sc[:, 4 + l:5 + l], in0=params[:, PC_QN + l:PC_QN + l + 1],
                                                       scalar1=0.125, scalar2=None, op0=ALU.mult),
                 reads=[Bpar, Bsc], writes=[Bsc])

        for tb in range(LT // 128):
            xs, Bxs, dxs = xsring.next()
            S.dma("sp", lambda e, xs=xs, tb=tb: e.dma_start(out=xs[:], in_=x_d[tb * 128:(tb + 1) * 128, :]), dxs, writes=[Bxs])
            for half in range(2):
                bk = (tb * 2 + half) % 4
                for q in range(4):
                    kc = half * 4 + q
                    S.op("pe", lambda e, bk=bk, q=q, xs=xs, kc=kc: e.transpose(out=bank(bk)[:, q * 128:(q + 1) * 128],
                                                                            in_=xs[:, kc * 128:(kc + 1) * 128], identity=ident),
                         reads=[Bxs, Bcm], writes=[Bps[bk]])
                eng = "dve" if half == 0 else "act"
                dst = xT[:, half * 4:half * 4 + 4, tb * 128:(tb + 1) * 128]
                src = bank(bk).rearrange("p (a b) -> p a b", a=4)
                if eng == "dve":
                    S.op("dve", lambda e, dst=dst, src=src: e.tensor_copy(out=dst, in_=src), reads=[Bps[bk]], writes=[BxT[tb // 4]])
                else:
                    S.op("act", lambda e, dst=dst, src=src: e.activation(out=dst, in_=src, func=AF.Copy), reads=[Bps[bk]], writes=[BxT[tb // 4]])

        def load_w(wmat, col0):
            wt, Bw, dw = wring.next()
            src = wmat[:, col0:col0 + 512].rearrange("(kc p) c -> p kc c", p=128)
            S.dma("pool", lambda e: e.dma_start(out=wt[:], in_=src), dw, writes=[Bw])
            return wt, Bw

        def rms_chunk(j, gcol, dst=None):
            if dst is None:
                hT, Bh1, _ = hring.next()
                Bh = [Bh1]
            else:
                hT, Bh = dst
            cs = slice(j * CHK, (j + 1) * CHK)
            sbk = 7
            for kc in range(8):
                sq, Bsq, _ = bf16r.next()
                if kc % 4 != 3:
                    S.op("act", lambda e, sq=sq, kc=kc: e.activation(out=sq[:], in_=xT[:, kc, cs], func=AF.Square), reads=[BxT[j]], writes=[Bsq])
                else:
                    S.op("pool", lambda e, sq=sq, kc=kc: e.tensor_tensor(out=sq[:], in0=xT[:, kc, cs], in1=xT[:, kc, cs], op=ALU.mult),
                         reads=[BxT[j]], writes=[Bsq])
                S.op("pe", lambda e, sq=sq, kc=kc: e.matmul(bank(sbk), lhsT=C_MEAN, rhs=sq[:], start=(kc == 0), stop=(kc == 7)),
                     reads=[Bsq, Bcbf], writes=[Bps[sbk]])
            rstd, Brs, _ = f32r.next()
            S.op("act", lambda e: e.activation(out=rstd[:], in_=bank(sbk), func=AF.Ln, bias=EPS), reads=[Bps[sbk]], writes=[Brs])
            S.op("act", lambda e: e.activation(out=rstd[:], in_=rstd[:], func=AF.Exp, scale=-0.5), reads=[Brs], writes=[Brs])
            for kc in range(8):
                S.op("dve", lambda e, kc=kc: e.scalar_tensor_tensor(out=hT[:, kc, :], in0=xT[:, kc, cs], scalar=params[:, gcol + kc:gcol + kc + 1],
                                                                 in1=rstd[:], op0=ALU.mult, op1=ALU.mult),
                     reads=[BxT[j], Brs, Bpar], writes=Bh)
            return hT, Bh

        def proj_fm(hT, Bh, wt, Bw, cw, bk):
            for kc in range(8):
                S.op("pe", lambda e, kc=kc: e.matmul(bank(bk), lhsT=wt[:, kc, cw:cw + 128], rhs=hT[:, kc, :], start=(kc == 0), stop=(kc == 7)),
                     reads=[Bw] + Bh, writes=[Bps[bk]])

        def load_rope(j):
            rp, Brp, drp = rring.next()
            S.dma("sp", lambda e: e.dma_start(out=rp[:], in_=rope_d[:, :, j * CHK:(j + 1) * CHK].rearrange("t p n -> p t n")), drp, writes=[Brp])
            return rp, Brp

        def qknorm_rope(bk, gain_ap, Bgain, rp, Brp, dst, Bdst, tb):
            sq, Bsq, _ = bf16r.next()
            S.op("act", lambda e: e.activation(out=sq[:], in_=bank(bk), func=AF.Square), reads=[Bps[bk]], writes=[Bsq])
            S.op("pe", lambda e: e.matmul(bank(tb), lhsT=C_BLK, rhs=sq[:], start=True, stop=True), reads=[Bsq, Bcbf], writes=[Bps[tb]])
            rstd, Brs, _ = f32r.next()
            S.op("act", lambda e: e.activation(out=rstd[:], in_=bank(tb), func=AF.Ln, bias=EPS), reads=[Bps[tb]], writes=[Brs])
            S.op("act", lambda e: e.activation(out=rstd[:], in_=rstd[:], func=AF.Exp, scale=-0.5), reads=[Brs], writes=[Brs])
            tn, Btn, _ = bf16r.next()
            S.op("dve", lambda e: e.scalar_tensor_tensor(out=tn[:], in0=bank(bk), scalar=gain_ap, in1=rstd[:], op0=ALU.mult, op1=ALU.mult),
                 reads=[Bps[bk], Brs, Bgain], writes=[Btn])
            S.op("pe", lambda e: e.matmul(bank(tb), lhsT=C_ROT, rhs=tn[:], start=True, stop=True), reads=[Btn, Bcbf], writes=[Bps[tb]])
            u1, Bu1, _ = f32r.next()
            S.op("pool", lambda e: e.tensor_tensor(out=u1[:], in0=tn[:], in1=rp[:, 0, :], op=ALU.mult), reads=[Btn, Brp], writes=[Bu1])
            u2, Bu2, _ = f32r.next()
            S.op("dve", lambda e: e.tensor_tensor(out=u2[:], in0=bank(tb), in1=rp[:, 1, :], op=ALU.mult), reads=[Bps[tb], Brp], writes=[Bu2])
            S.op("pool", lambda e: e.tensor_tensor(out=dst, in0=u1[:], in1=u2[:], op=ALU.add), reads=[Bu1, Bu2], writes=[Bdst])

        def qk_pipeline(n_units, make_proj, gain_ap, Bgain, rp_fn, dst_fn, hook=None):
            stq = {}

            def P(k):
                make_proj(k, k % 3)

            def N1(k):
                bk, mb = k % 3, 3 + k % 2
                sq, Bsq, _ = bf16r.next()
                S.op("act", lambda e: e.activation(out=sq[:], in_=bank(bk), func=AF.Square), reads=[Bps[bk]], writes=[Bsq])
                S.op("pe", lambda e: e.matmul(bank(mb), lhsT=C_BLK, rhs=sq[:], start=True, stop=True), reads=[Bsq, Bcbf], writes=[Bps[mb]])

            def N2(k):
                bk, mb = k % 3, 3 + k % 2
                rstd, Brs, _ = f32r.next()
                S.op("act", lambda e: e.activation(out=rstd[:], in_=bank(mb), func=AF.Ln, bias=EPS), reads=[Bps[mb]], writes=[Brs])
                S.op("act", lambda e: e.activation(out=rstd[:], in_=rstd[:], func=AF.Exp, scale=-0.5), reads=[Brs], writes=[Brs])
                tn, Btn, _ = bf16r.next()
                S.op("dve", lambda e: e.scalar_tensor_tensor(out=tn[:], in0=bank(bk), scalar=gain_ap, in1=rstd[:], op0=ALU.mult, op1=ALU.mult),
                     reads=[Bps[bk], Brs, Bgain], writes=[Btn])
                stq[k] = (tn, Btn)

            def N3(k):
                rb = 5 + k % 2
                tn, Btn = stq.pop(k)
                rp, Brp = rp_fn(k)
                S.op("pe", lambda e: e.matmul(bank(rb), lhsT=C_ROT, rhs=tn[:], start=True, stop=True), reads=[Btn, Bcbf], writes=[Bps[rb]])
                u1, Bu1, _ = f32r.next()
                S.op("dve", lambda e: e.tensor_tensor(out=u1[:], in0=tn[:], in1=rp[:, 0, :], op=ALU.mult), reads=[Btn, Brp], writes=[Bu1])
                u2, Bu2, _ = f32r.next()
                S.op("dve", lambda e: e.tensor_tensor(out=u2[:], in0=bank(rb), in1=rp[:, 1, :], op=ALU.mult), reads=[Bps[rb], Brp], writes=[Bu2])
                dst, Bdst, post = dst_fn(k)
                S.op("pool", lambda e: e.tensor_tensor(out=dst, in0=u1[:], in1=u2[:], op=ALU.add), reads=[Bu1, Bu2], writes=[Bdst])
                if post is not None:
                    post()

            for t in range(n_units + 3):
                if hook is not None:
                    hook(t)
                if t < n_units:
                    P(t)
                if 1 <= t <= n_units:
                    N1(t - 1)
                if 2 <= t <= n_units + 1:
                    N2(t - 2)
                if t >= 3:
                    N3(t - 3)

        def phase_kv(L, wmat, kcol, vcol, gcol, qk_layer):
            htiles = [(hring.tiles[0], [hring.bufs[0]]), (hring.tiles[1], [hring.bufs[1]]), (qT, list(BqT)), (gT, list(BgT))]
            for j in range(NLC):
                rms_chunk(j, gcol, dst=htiles[j])
            wK = load_w(wmat, kcol)
            wV = load_w(wmat, vcol)
            for half in range(2):
                wt, Bw = wK
                if qk_layer is not None:
                    ropes = {}

                    def mk(k, bk, wt=wt, Bw=Bw, ropes=ropes):
                        jj, uu = k // 4, k % 4
                        if uu == 0:
                            ropes[jj] = load_rope(jj)
                        proj_fm(htiles[jj][0], htiles[jj][1], wt, Bw, uu * 128, bk)

                    def dstf(k, half=half):
                        jj, uu = k // 4, k % 4
                        u = half * 4 + uu
                        kst, Bkst, dk = stK.next()

                        def post():
                            dst = kvin[L][u // 2].ap()[(u % 2) * 128:(u % 2) * 128 + 128, jj * CHK:(jj + 1) * CHK]
                            ev = S.dma("sp", lambda e: e.dma_start(out=dst, in_=kst[:]), dk, reads=[Bkst])
                            kv_store_events[L][u // 2].append(ev)
                        return kst[:], Bkst, post

                    qk_pipeline(16, mk, params[:, PC_KN + qk_layer:PC_KN + qk_layer + 1], Bpar, lambda k, ropes=ropes: ropes[k // 4], dstf)
                else:
                    for jj in range(NLC):
                        for uu in range(4):
                            u = half * 4 + uu
                            bk = uu % 2
                            proj_fm(htiles[jj][0], htiles[jj][1], wt, Bw, uu * 128, bk)
                            kst, Bkst, dk = stK.next()
                            S.op("act", lambda e, kst=kst, bk=bk: e.activation(out=kst[:], in_=bank(bk), func=AF.Copy), reads=[Bps[bk]], writes=[Bkst])
                            dst = kvin[L][u // 2].ap()[(u % 2) * 128:(u % 2) * 128 + 128, jj * CHK:(jj + 1) * CHK]
                            ev = S.dma("sp", lambda e, dst=dst, kst=kst: e.dma_start(out=dst, in_=kst[:]), dk, reads=[Bkst])
                            kv_store_events[L][u // 2].append(ev)
                if half == 0:
                    wK = load_w(wmat, kcol + 512)
                wt, Bw = wV
                for jj in range(NLC):
                    hT, Bh = htiles[jj]
                    for tb in range(4):
                        bk = 4 + tb % 2
                        for kc in range(8):
                            S.op("pe", lambda e, kc=kc, tb=tb, bk=bk, wt=wt, hT=hT: e.matmul(bank(bk), lhsT=hT[:, kc, tb * 128:(tb + 1) * 128], rhs=wt[:, kc, :],
                                                                                   start=(kc == 0), stop=(kc == 7)),
                                 reads=[Bw] + Bh, writes=[Bps[bk]])
                        vst, Bvst, dv = stV.next()
                        S.op("dve", lambda e, vst=vst, bk=bk: e.tensor_copy(out=vst[:], in_=bank(bk)), reads=[Bps[bk]], writes=[Bvst])
                        for pr in range(2):
                            hp = half * 2 + pr
                            dst = kvin[L][hp].ap()[256:512, :].rearrange("(h r) (t e) -> (r t) h e", h=2, e=128)[
                                jj * CHK + tb * 128:jj * CHK + (tb + 1) * 128, :, :]
                            src = vst[:, pr * 256:(pr + 1) * 256].rearrange("p (h e) -> p h e", h=2)
                            ev = S.dma("sp", lambda e, dst=dst, src=src: e.dma_start(out=dst, in_=src), dv if pr == 0 else stV_d2[dv], reads=[Bvst])
                            kv_store_events[L][hp].append(ev)
                if half == 0:
                    wV = load_w(wmat, vcol + 512)
                for hp in (2 * half, 2 * half + 1):
                    mx = {}
                    for ev in kv_store_events[L][hp]:
                        mx[ev[1]] = max(mx.get(ev[1], 0), ev[2])
                    for k_, v_ in mx.items():
                        S.wait_event("pool", ("d", k_, v_))
                    S.dma("pool", lambda e, hp=hp: e.collective_compute("AllGather", ALU.bypass, replica_groups=RG,
                                                                         ins=[kvin[L][hp].ap().opt()], outs=[kvall[L][hp].ap().opt()]),
                          d_ag[L][hp], writes=[Bkvall[L][hp]], inc=1)

        def load_kv(L, u, gk):
            kt, Bkt, dkt = kring.next()
            rho, jj = OWNER[gk]
            base = kvall[L][u // 2].ap()
            ksrc = base[rho * 512 + (u % 2) * 128:rho * 512 + (u % 2) * 128 + 128, jj * CHK:(jj + 1) * CHK]
            vsrc = base[rho * 512 + 256 + (u % 2) * 128:rho * 512 + 256 + (u % 2) * 128 + 128, :].rearrange(
                "r (t e) -> (r t) e", e=128)[jj * CHK:(jj + 1) * CHK, :].rearrange("(kb r) e -> r kb e", r=128)
            S.dma("sp", lambda e: e.dma_start(out=kt[:, 0, :], in_=ksrc), dkt, reads=[Bkvall[L][u // 2]], writes=[Bkt])
            S.dma("sp", lambda e: e.dma_start(out=kt[:, 1, :].rearrange("p (kb e) -> p kb e", e=128), in_=vsrc), dkt,
                  reads=[Bkvall[L][u // 2]], writes=[Bkt])
            return kt, Bkt

        def band_for(kind, j, gk):
            if gk == MJ[j]:
                ts = 0
            elif gk == MJ[j] - 1:
                ts = 1
            else:
                return None
            return lambda i: bands[:, j % 2, ts, 384 - 128 * i:896 - 128 * i]

        def out_proj(j, wmat, pre=None):
            cs = slice(j * CHK, (j + 1) * CHK)
            for half in range(2):
                wt, Bw = pre[half] if pre is not None else load_w(wmat, half * 512)
                for q in range(4):
                    ncn = half * 4 + q
                    bk = q % 2
                    for u in range(8):
                        S.op("pe", lambda e, u=u, q=q, bk=bk, wt=wt: e.matmul(bank(bk), lhsT=wt[:, u, q * 128:(q + 1) * 128], rhs=gT[:, u, :],
                                                                           start=(u == 0), stop=(u == 7)),
                             reads=[Bw, BgT[u]], writes=[Bps[bk]])
                    S.op("dve", lambda e, ncn=ncn, bk=bk: e.tensor_tensor(out=xT[:, ncn, cs], in0=bank(bk), in1=xT[:, ncn, cs], op=ALU.add),
                         reads=[Bps[bk]], writes=[BxT[j]])

        def diff_layer(l):
            wmat = a_w_in[l]
            phase_kv(l, wmat, 1024, 2048, PC_AN + 8 * l, l)
            for j in range(NLC):
                wts = [load_w(wmat, half * 512) for half in range(2)]
                hT, Bh = rms_chunk(j, PC_AN + 8 * l)
                rp, Brp = load_rope(j)
                wg = {}

                def mkq(k, bk, wts=wts, hT=hT, Bh=Bh):
                    proj_fm(hT, Bh, wts[k // 4][0], wts[k // 4][1], (k % 4) * 128, bk)

                def hookq(t, wg=wg):
                    if t == 5:
                        wg[0] = load_w(wmat, 3072)
                    if t == 9:
                        wg[1] = load_w(wmat, 3072 + 512)

                qk_pipeline(8, mkq, sc[:, 4 + l:5 + l], Bsc, lambda k: (rp, Brp), lambda k: (qT[:, k, :], BqT[k], None), hook=hookq)
                for half in range(2):
                    wt, Bw = wg[half]
                    for uu in range(4):
                        u = half * 4 + uu
                        bk = uu % 2
                        proj_fm(hT, Bh, wt, Bw, uu * 128, bk)
                        S.op("act", lambda e, u=u, bk=bk: e.activation(out=gT[:, u, :], in_=bank(bk), func=AF.Silu), reads=[Bps[bk]], writes=[BgT[u]])
                wo = [load_w(a_w_out[l], half * 512) for half in range(2)]
                nblk = (MJ[j] + 1) * 4
                items = [(u, gk, i) for u in range(8) for gk in range(MJ[j] + 1) for i in range(4)]
                stt = {}

                def stage_a(idx):
                    u, gk, i = items[idx]
                    n = idx % nblk
                    if i == 0:
                        stt[("kv", u, gk)] = load_kv(l, u, gk)
                    kt, Bkt = stt[("kv", u, gk)]
                    if n == 0:
                        stt[("es", u)] = big32.next()
                    es, Bes, _ = stt[("es", u)]
                    sp_ = idx % 2
                    b0, b1 = 2 * sp_, 2 * sp_ + 1
                    ksl = slice(i * 128, (i + 1) * 128)
                    bf = band_for(0, j, gk)
                    nomask = bf is None
                    off = 128 * i if gk == MJ[j] else 0
                    S.op("pe", lambda e: e.matmul(bank(b0)[:, off:], lhsT=kt[0:64, 0, ksl], rhs=qT[0:64, u, off:], start=True, stop=nomask),
                         reads=[Bkt, BqT[u]], writes=[Bps[b0]])
                    S.op("pe", lambda e: e.matmul(bank(b1)[:, off:], lhsT=kt[64:128, 0, ksl], rhs=qT[64:128, u, off:], start=True, stop=nomask),
                         reads=[Bkt, BqT[u]], writes=[Bps[b1]])
                    if not nomask:
                        m = bf(i)
                        for bb in (b0, b1):
                            S.op("pe", lambda e, bb=bb: e.matmul(bank(bb)[:, off:], lhsT=C_IDB, rhs=m[:, off:], start=False, stop=True),
                                 reads=[Bcbf, Bbands], writes=[Bps[bb]])
                    et, Bet, _ = ering.next()
                    S.op("act", lambda e: e.activation(out=et[:], in_=ps[sp_][:], func=AF.Exp), reads=[Bps[b0], Bps[b1]], writes=[Bet])
                    if n == 0:
                        S.op("dve", lambda e: e.tensor_copy(out=es[:, 0:CHK], in_=et[:, 0:CHK]), reads=[Bet], writes=[Bes])
                    else:
                        S.op("dve", lambda e: e.tensor_tensor(out=es[:, off:CHK], in0=es[:, off:CHK], in1=et[:, off:CHK], op=ALU.add), reads=[Bet, Bes], writes=[Bes])
                    stt[("et", idx)] = (et, Bet)

                def stage_b(idx):
                    u, gk, i = items[idx]
                    n = idx % nblk
                    kt, Bkt = stt[("kv", u, gk)]
                    et, Bet = stt.pop(("et", idx))
                    first, last = (n == 0), (n == nblk - 1)
                    off = 128 * i if gk == MJ[j] else 0
                    vblk = kt[:, 1, i * 128:(i + 1) * 128]
                    for hh in range(2):
                        S.op("pe", lambda e, hh=hh: e.matmul(bank(4 + hh)[:, off:], lhsT=vblk, rhs=et[:, hh * CHK + off:(hh + 1) * CHK], start=first, stop=last),
                             reads=[Bkt, Bet], writes=[Bps[4 + hh]])
                    S.op("pe", lambda e: e.matmul(bank(7)[:, off:], lhsT=C_ONES, rhs=et[:, CHK + off:2 * CHK], start=first, stop=last),
                         reads=[Bcbf, Bet], writes=[Bps[7]])
                    if last:
                        while pending:
                            pending.pop(0)()
                        pending.extend(epilogue_steps(u))
                        pending.pop(0)()

                def epilogue_steps(u):
                    es, Bes, _ = stt.pop(("es", u))
                    o12, Bo12, _ = big32.next()
                    esb, Besb, _ = bf16r.next()
                    t1, Bt1, _ = f32r.next()
                    t2, Bt2, _ = f32r.next()
                    r1, Br1, _ = f32r.next()
                    sq, Bsq, _ = bf16r.next()

                    def e0():
                        S.op("act", lambda e: e.activation(out=o12[:], in_=ps[2][:], func=AF.Copy), reads=[Bps[4], Bps[5]], writes=[Bo12])
                        S.op("dve", lambda e: e.tensor_copy(out=es[:, CHK:2 * CHK], in_=bank(7)), reads=[Bps[7], Bes], writes=[Bes])
                        S.op("dve", lambda e: e.tensor_copy(out=esb[:], in_=es[:, 0:CHK]), reads=[Bes], writes=[Besb])

                    def e1():
                        S.op("pe", lambda e: e.matmul(bank(6), lhsT=C_ONES, rhs=esb[:], start=True, stop=True), reads=[Bcbf, Besb], writes=[Bps[6]])

                    def e2():
                        S.op("dve", lambda e: e.tensor_tensor(out=t1[:], in0=o12[:, 0:CHK], in1=es[:, CHK:2 * CHK], op=ALU.mult), reads=[Bes, Bo12], writes=[Bt1])
                        S.op("dve", lambda e: e.scalar_tensor_tensor(out=t2[:], in0=o12[:, CHK:2 * CHK], scalar=sc[:, l:l + 1], in1=bank(6),
                                                                    op0=ALU.mult, op1=ALU.mult),
                             reads=[Bo12, Bps[6], Bsc], writes=[Bt2])
                        S.op("dve", lambda e: e.tensor_tensor(out=es[:, 0:CHK], in0=bank(6), in1=es[:, CHK:2 * CHK], op=ALU.mult), reads=[Bps[6], Bes], writes=[Bes])
                        S.op("dve", lambda e: e.tensor_tensor(out=es[:, 0:CHK], in0=es[:, 0:CHK], in1=es[:, 0:CHK], op=ALU.mult), reads=[Bes], writes=[Bes])

                    def e3():
                        S.op("pool", lambda e: e.tensor_tensor(out=t1[:], in0=t1[:], in1=t2[:], op=ALU.add), reads=[Bt1, Bt2], writes=[Bt1])
                        S.op("pool", lambda e: e.tensor_tensor(out=sq[:], in0=t1[:], in1=t1[:], op=ALU.mult), reads=[Bt1], writes=[Bsq])

                    def enop():
                        pass

                    def e4():
                        S.op("pe", lambda e: e.matmul(bank(6), lhsT=C_M128, rhs=sq[:], start=True, stop=True), reads=[Bsq, Bcbf], writes=[Bps[6]])
                        S.op("dve", lambda e: e.scalar_tensor_tensor(out=r1[:], in0=es[:, 0:CHK], scalar=EPS, in1=bank(6), op0=ALU.mult, op1=ALU.add),
                             reads=[Bes, Bps[6]], writes=[Br1])
                        S.op("act", lambda e: e.activation(out=r1[:], in_=r1[:], func=AF.Ln), reads=[Br1], writes=[Br1])

                    def e5():
                        S.op("act", lambda e: e.activation(out=r1[:], in_=r1[:], func=AF.Exp, scale=-0.5), reads=[Br1], writes=[Br1])
                        S.op("dve", lambda e: e.scalar_tensor_tensor(out=t1[:], in0=t1[:], scalar=sc[:, 2 + l:3 + l], in1=r1[:], op0=ALU.mult, op1=ALU.mult),
                             reads=[Bt1, Br1, Bsc], writes=[Bt1])
                        S.op("pool", lambda e: e.tensor_tensor(out=gT[:, u, :], in0=t1[:], in1=gT[:, u, :], op=ALU.mult), reads=[Bt1], writes=[BgT[u]])

                    return [e0, e1, e2, e3, enop, e4, e5]

                pending = []
                for t in range(len(items) + 2):
                    if t < len(items):
                        stage_a(t)
                    if t >= 2:
                        stage_b(t - 2)
                    if pending and (t % nblk) >= 3:
                        pending.pop(0)()
                while pending:
                    pending.pop(0)()
                out_proj(j, a_w_out[l], pre=wo)

        def sb_layer(jl):
            wmat = b_w_in[jl]
            for j in range(NLC):
                wq = [load_w(wmat, half * 512) for half in range(2)]
                hT, Bh = rms_chunk(j, PC_BN + 8 * jl)
                wg = {}
                for half in range(2):
                    wt, Bw = wq[half]
                    for uu in range(4):
                        u = half * 4 + uu
                        bk = uu % 2
                        proj_fm(hT, Bh, wt, Bw, uu * 128, bk)
                        S.op("dve", lambda e, u=u, bk=bk: e.tensor_copy(out=qT[:, u, :], in_=bank(bk)), reads=[Bps[bk]], writes=[BqT[u]])
                    wg[half] = load_w(wmat, 1024 + half * 512)
                for half in range(2):
                    wt, Bw = wg[half]
                    for uu in range(4):
                        u = half * 4 + uu
                        bk = uu % 2
                        proj_fm(hT, Bh, wt, Bw, uu * 128, bk)
                        S.op("act", lambda e, u=u, bk=bk: e.activation(out=gT[:, u, :], in_=bank(bk), func=AF.Silu), reads=[Bps[bk]], writes=[BgT[u]])
                wo = [load_w(b_w_out[jl], half * 512) for half in range(2)]
                nblk = (MJ[j] + 1) * 4
                items = [(u, gk, i) for u in range(8) for gk in range(MJ[j], -1, -1) for i in range(3, -1, -1)]
                stt = {}

                def s1(idx):
                    u, gk, i = items[idx]
                    if i == 3:
                        stt[("kv", u, gk)] = load_kv(2, u, gk)
                    kt, Bkt = stt[("kv", u, gk)]
                    ksl = slice(i * 128, (i + 1) * 128)
                    bf = band_for(1, j, gk)
                    nomask = bf is None
                    S.op("pe", lambda e: e.matmul(bank(0), lhsT=kt[0:64, 0, ksl], rhs=qT[0:64, u, :], start=True, stop=nomask),
                         reads=[Bkt, BqT[u]], writes=[Bps[0]])
                    S.op("pe", lambda e: e.matmul(bank(1), lhsT=kt[64:128, 0, ksl], rhs=qT[64:128, u, :], start=True, stop=nomask),
                         reads=[Bkt, BqT[u]], writes=[Bps[1]])
                    if not nomask:
                        m = bf(i)
                        for bb in (0, 1):
                            S.op("pe", lambda e, bb=bb: e.matmul(bank(bb), lhsT=C_IDB, rhs=m, start=False, stop=True),
                                 reads=[Bcbf, Bbands], writes=[Bps[bb]])
                    eb, Beb, _ = ebring.next()
                    S.op("act", lambda e: e.activation(out=eb[:], in_=ps[0][:], func=AF.Exp, scale=0.125), reads=[Bps[0], Bps[1]], writes=[Beb])
                    stt[("s1a", idx)] = (eb, Beb)

                def s1b(idx):
                    eb, Beb = stt.pop(("s1a", idx))
                    spt, Bspt, _ = spring.next()
                    S.op("act", lambda e: e.activation(out=spt[:], in_=eb[:], func=AF.Ln, bias=1.0), reads=[Beb], writes=[Bspt])
                    stt[("s1", idx)] = (eb, Beb, spt, Bspt)

                def s2(idx):
                    u, gk, i = items[idx]
                    n = idx % nblk
                    eb, Beb, spt, Bspt = stt.pop(("s1", idx))
                    for hh in range(2):
                        S.op("pe", lambda e, hh=hh: e.matmul(bank(2 + hh), lhsT=C_TINC, rhs=spt[:, hh * CHK:(hh + 1) * CHK], start=True, stop=True),
                             reads=[Bcbf, Bspt], writes=[Bps[2 + hh]])
                    tt, Btt, _ = big32.next()
                    if n == 0:
                        S.op("dve", lambda e: e.tensor_copy(out=tt[:], in_=ps[1][:]), reads=[Bps[2], Bps[3]], writes=[Btt])
                    else:
                        S.op("dve", lambda e: e.tensor_tensor(out=tt[:], in0=ps[1][:], in1=Rc[:], op=ALU.add), reads=[Bps[2], Bps[3], BRc], writes=[Btt])
                    if n != nblk - 1:
                        for hh in range(2):
                            S.op("pe", lambda e, hh=hh: e.matmul(bank(4 + hh), lhsT=C_NONES, rhs=spt[:, hh * CHK:(hh + 1) * CHK], start=True, stop=True),
                                 reads=[Bcbf, Bspt], writes=[Bps[4 + hh]])
                        if n == 0:
                            S.op("dve", lambda e: e.tensor_copy(out=Rc[:], in_=ps[2][:]), reads=[Bps[4], Bps[5]], writes=[BRc])
                        else:
                            S.op("dve", lambda e: e.tensor_tensor(out=Rc[:], in0=ps[2][:], in1=Rc[:], op=ALU.add), reads=[Bps[4], Bps[5], BRc], writes=[BRc])
                    stt[("s2", idx)] = (eb, Beb, tt, Btt)

                def s2b(idx):
                    eb, Beb, tt, Btt = stt.pop(("s2", idx))
                    wb, Bwb, _ = wring2.next()
                    S.op("act", lambda e: e.activation(out=wb[:], in_=tt[:], func=AF.Exp), reads=[Btt], writes=[Bwb])
                    at, Bat, _ = ering.next()
                    S.op("pool", lambda e: e.tensor_tensor(out=at[:], in0=eb[:], in1=wb[:], op=ALU.mult), reads=[Beb, Bwb], writes=[Bat])
                    stt[("at", idx)] = (at, Bat)

                def s3(idx):
                    u, gk, i = items[idx]
                    n = idx % nblk
                    kt, Bkt = stt[("kv", u, gk)]
                    at, Bat = stt.pop(("at", idx))
                    first, last = (n == 0), (n == nblk - 1)
                    for hh in range(2):
                        S.op("pe", lambda e, hh=hh: e.matmul(bank(6)[hh * 64:(hh + 1) * 64, :], lhsT=kt[:, 1, i * 128 + hh * 64:i * 128 + (hh + 1) * 64],
                                                            rhs=at[:, hh * CHK:(hh + 1) * CHK], start=first, stop=last, tile_position=(0, hh * 64)),
                             reads=[Bkt, Bat], writes=[Bps[6]])
                    if last:
                        S.op("dve", lambda e: e.tensor_tensor(out=gT[:, u, :], in0=bank(6), in1=gT[:, u, :], op=ALU.mult), reads=[Bps[6]], writes=[BgT[u]])

                NI = len(items)
                for t in range(NI + 4):
                    if t < NI:
                        s1(t)
                    if 3 <= t <= NI + 2:
                        s2b(t - 3)
                    if 2 <= t <= NI + 1:
                        s2(t - 2)
                    if t < NI:
                        s1b(t)
                    if t >= 4:
                        s3(t - 4)
                out_proj(j, b_w_out[jl], pre=wo)

        if n_layers >= 1:
            diff_layer(0)
        if n_layers >= 2:
            diff_layer(1)
        if n_layers >= 3:
            S.dma("sp", lambda e: e.dma_start(out=bands[:], in_=bands_d[1]), d_bands, writes=[Bbands])
            phase_kv(2, w_kv, 0, 1024, PC_KVN, None)
            sb_layer(0)
        if n_layers >= 4:
            sb_layer(1)

        for tb in range(LT // 128):
            xs, Bxs, dxs = xsring.next()
            for half in range(2):
                bk = (tb * 2 + half) % 4
                for q in range(4):
                    kc = half * 4 + q
                    S.op("pe", lambda e, bk=bk, q=q, kc=kc, tb=tb: e.transpose(out=bank(bk)[:, q * 128:(q + 1) * 128],
                                                                            in_=xT[:, kc, tb * 128:(tb + 1) * 128], identity=ident),
                         reads=[BxT[tb // 4], Bcm], writes=[Bps[bk]])
                dst = xs[:, half * 512:(half + 1) * 512]
                if half == 0:
                    S.op("dve", lambda e, dst=dst, bk=bk: e.tensor_copy(out=dst, in_=bank(bk)), reads=[Bps[bk]], writes=[Bxs])
                else:
                    S.op("act", lambda e, dst=dst, bk=bk: e.activation(out=dst, in_=bank(bk), func=AF.Copy), reads=[Bps[bk]], writes=[Bxs])
            S.dma("sp", lambda e, xs=xs, tb=tb: e.dma_start(out=y_d[tb * 128:(tb + 1) * 128, :], in_=xs[:]), dxs, reads=[Bxs])
        for dd in xsring.dsems:
            S.wait_event("sp", ("d", dd, S.dsem_cnt[dd]))
        S.emit(st)
    return nc


def _host_constants():
    ident = np.eye(128, dtype=np.float32)
    rot = np.zeros((128, 128), np.float32)
    for c in range(2):
        for d in range(8):
            rot[c * 64 + d + 8, c * 64 + d] = -1.0
            rot[c * 64 + d, c * 64 + d + 8] = 1.0
    jj, ss = np.meshgrid(np.arange(128), np.arange(128), indexing="ij")
    tinc = -(jj >= ss).astype(np.float32)
    tlow = -(jj < ss).astype(np.float32)
    cmat = np.stack([ident, rot, tinc, tlow], axis=1)
    return np.ascontiguousarray(cmat)


def _rope_tables(p):
    half = 8
    inv = np.power(np.float32(500000.0), -np.arange(half, dtype=np.float32) / np.float32(half)).astype(np.float32)
    pos = np.concatenate([np.arange(g * CHK, (g + 1) * CHK) for g in GCH[p]]).astype(np.float32)
    ang = pos[None, :] * inv[:, None]
    cos, sin = np.cos(ang).astype(np.float32), np.sin(ang).astype(np.float32)
    C = np.ones((128, LT), np.float32)
    Sn = np.zeros((128, LT), np.float32)
    for c in range(2):
        for d in range(16):
            C[c * 64 + d] = cos[d % 8]
            Sn[c * 64 + d] = sin[d % 8]
    return np.ascontiguousarray(np.stack([C, Sn], 0))


def _bands(p):
    r = np.arange(128)[:, None]
    u = np.arange(896)[None, :]
    diag = [(r <= u - 384), (r < u - 384)]
    out = np.zeros((2, 128, 2, 2, 896), np.float32)
    for kind in range(2):
        for jpar in range(2):
            high = ((jpar + p) % 2 == 1)
            if high:
                out[kind, :, jpar, 0] = diag[kind]
                out[kind, :, jpar, 1] = 1.0
            else:
                out[kind, :, jpar, 0] = 0.0
                out[kind, :, jpar, 1] = diag[kind]
    out = (out - 1.0) * 30000.0
    return out.astype(ml_dtypes.bfloat16)


def _params(inp):
    P = np.zeros((128, NPAR), np.float32)
    f = lambda v: np.asarray(v, np.float32)
    for l in range(2):
        P[:, PC_AN + 8 * l:PC_AN + 8 * l + 8] = f(inp["a_norm"])[l].reshape(8, 128).T
        P[:, PC_BN + 8 * l:PC_BN + 8 * l + 8] = f(inp["b_norm"])[l].reshape(8, 128).T
        P[:, PC_QN + l] = np.tile(f(inp["a_q_norm"])[l], 2)
        P[:, PC_KN + l] = np.tile(f(inp["a_k_norm"])[l], 2)
        P[:, PC_SUB + l] = f(inp["a_subln"])[l]
        for t, nm in enumerate(["a_lq1", "a_lk1", "a_lq2", "a_lk2"]):
            c0 = PC_L + (l * 4 + t) * 64
            P[:, c0:c0 + 64] = f(inp[nm])[l][None, :]
    P[:, PC_KVN:PC_KVN + 8] = f(inp["kv_norm"]).reshape(8, 128).T
    return P


_NC_CACHE = {}


def _run(inp, n_layers=4):
    x = np.asarray(inp["x"], np.float32)
    if n_layers not in _NC_CACHE:
        _NC_CACHE[n_layers] = build_nc(n_layers)
    nc = _NC_CACHE[n_layers]
    cmat = _host_constants()
    params = _params(inp)
    shared = {k: np.ascontiguousarray(np.asarray(inp[k], np.float32)) for k in ("a_w_in", "a_w_out", "w_kv", "b_w_in", "b_w_out")}
    in_maps = []
    for c in range(8):
        b, p = c // 2, c % 2
        xs = np.concatenate([x[b, g * CHK:(g + 1) * CHK] for g in GCH[p]], 0)
        m = {"x": np.ascontiguousarray(xs), "params": params, "cmat": cmat, "rope": _rope_tables(p), "bands": _bands(p)}
        m.update(shared)
        in_maps.append(m)
    res = run_bass_kernel_spmd(nc, in_maps, core_ids=list(range(8)))
    out = np.zeros((NB, SEQ, D), np.float32)
    for c in range(8):
        b, p = c // 2, c % 2
        y = res.results[c]["y"]
        for j, g in enumerate(GCH[p]):
            out[b, g * CHK:(g + 1) * CHK] = y[j * CHK:(j + 1) * CHK]
    return out


def kernel(**inputs):
    return _run(inputs, 4)
```
